# Optimizing a Trainium2 kernel written in Bass

```python
import jax, jax.numpy as jnp
from jax import lax
import numpy as np

D_MODEL = 1024
BATCH = 8
SEQ = 2048
DEPTH = 1

CHUNK = 64
N_META = 16
Q_BLOCK = 128
ROPE_THETA = 10000.0
RMS_EPS = 1e-6
D_MIX = D_MODEL
SB_HEAD_DIM = 64
SB_HEADS = (D_MIX // 2) // SB_HEAD_DIM
SB_WIDTH = SB_HEADS * SB_HEAD_DIM
DIFF_HEAD_DIM = 64
DIFF_V_DIM = 2 * DIFF_HEAD_DIM
DIFF_HEADS = (D_MIX // 2) // DIFF_V_DIM
DIFF_QK_WIDTH = DIFF_HEADS * 2 * DIFF_HEAD_DIM
DIFF_WIDTH = DIFF_HEADS * DIFF_V_DIM
SPLIT_SIZES = (SB_WIDTH, SB_WIDTH, SB_WIDTH, SB_WIDTH,
               DIFF_QK_WIDTH, DIFF_QK_WIDTH, DIFF_WIDTH, DIFF_WIDTH)
IN_COLS = int(sum(SPLIT_SIZES))

kernel_name = "hymba_stickbreak_diffattn_chunk_causal"


def rmsnorm(x, g):
    xf = x.astype(jnp.float32)
    y = xf * lax.rsqrt(jnp.mean(xf * xf, axis=-1, keepdims=True) + RMS_EPS)
    return (y * g.astype(jnp.float32)).astype(x.dtype)


def rope_tables(length, dim):
    inv = 1.0 / (ROPE_THETA ** (jnp.arange(0, dim, 2, dtype=jnp.float32) / dim))
    ang = jnp.arange(length, dtype=jnp.float32)[:, None] * inv[None, :]
    return jnp.cos(ang), jnp.sin(ang)


def apply_rope(x, cos, sin):
    xf = x.astype(jnp.float32)
    x1, x2 = jnp.split(xf, 2, axis=-1)
    c = cos[None, :, None, :]
    s = sin[None, :, None, :]
    return jnp.concatenate([x1 * c - x2 * s, x2 * c + x1 * s], axis=-1).astype(x.dtype)


def chunk_ids(length):
    p = jnp.arange(length)
    return jnp.where(p < N_META, 0, (p - N_META) // CHUNK + 1)


def chunk_end(pos):
    if pos < N_META:
        return N_META
    return N_META + CHUNK * ((pos - N_META) // CHUNK + 1)


def stick_breaking_attention(q, k, v):
    Lp = q.shape[1]
    d = q.shape[-1]
    scale = d ** -0.5
    outs = []
    for t0 in range(0, Lp, Q_BLOCK):
        t1 = t0 + Q_BLOCK
        z = jnp.einsum('bqhd,bkhd->bhqk', q[:, t0:t1].astype(jnp.float32),
                       k[:, :t1].astype(jnp.float32)) * scale
        strict = jnp.arange(t1)[None, :] < jnp.arange(t0, t1)[:, None]
        log_beta = jax.nn.log_sigmoid(z)
        log_keep = jnp.where(strict, jax.nn.log_sigmoid(-z), 0.0)
        stick = jnp.pad(lax.cumsum(log_keep[..., 1:], axis=3, reverse=True),
                        ((0, 0), (0, 0), (0, 0), (0, 1)))
        w = jnp.where(strict, jnp.exp(log_beta + stick), 0.0)
        o = jnp.einsum('bhqk,bkhd->bqhd', w, v[:, :t1].astype(jnp.float32))
        outs.append(o.astype(v.dtype))
    return jnp.concatenate(outs, axis=1)


def differential_attention(q, k, v, lam, chunk):
    Lp = q.shape[1]
    d = q.shape[-1]
    scale = d ** -0.5
    outs = []
    for t0 in range(0, Lp, Q_BLOCK):
        t1 = t0 + Q_BLOCK
        kl = min(chunk_end(t1 - 1), Lp)
        s = jnp.einsum('bqhcd,bkhcd->bhcqk', q[:, t0:t1].astype(jnp.float32),
                       k[:, :kl].astype(jnp.float32)) * scale
        mask = chunk[t0:t1][:, None] >= chunk[:kl][None, :]
        s = jnp.where(mask, s, -1e30)
        p = jax.nn.softmax(s, axis=-1)
        a = p[:, :, 0] - lam * p[:, :, 1]
        o = jnp.einsum('bhqk,bkhe->bqhe', a, v[:, :kl].astype(jnp.float32))
        outs.append(o.astype(v.dtype))
    return jnp.concatenate(outs, axis=1)


def hybrid_layer(h, norm_g, w_in, w_out, lq1, lk1, lq2, lk2, subln_g,
                 layer_idx, cos, sin, chunk, Lp):
    B, L, _ = h.shape
    u = rmsnorm(h, norm_g)
    proj = jnp.einsum('bld,dc->blc', u, w_in.astype(u.dtype))
    proj = jnp.pad(proj, ((0, 0), (0, Lp - L), (0, 0)))
    idx = [int(i) for i in np.cumsum(SPLIT_SIZES)[:-1]]
    sb_q, sb_k, sb_v, sb_g, df_q, df_k, df_v, df_g = jnp.split(proj, idx, axis=-1)

    shp_sb = (B, Lp, SB_HEADS, SB_HEAD_DIM)
    sb_o = stick_breaking_attention(sb_q.reshape(shp_sb), sb_k.reshape(shp_sb),
                                    sb_v.reshape(shp_sb)).reshape(B, Lp, SB_WIDTH)

    shp_qk = (B, Lp, 2 * DIFF_HEADS, DIFF_HEAD_DIM)
    dq = apply_rope(df_q.reshape(shp_qk), cos, sin).reshape(B, Lp, DIFF_HEADS, 2, DIFF_HEAD_DIM)
    dk = apply_rope(df_k.reshape(shp_qk), cos, sin).reshape(B, Lp, DIFF_HEADS, 2, DIFF_HEAD_DIM)
    dv = df_v.reshape(B, Lp, DIFF_HEADS, DIFF_V_DIM)
    lambda_init = 0.8 - 0.6 * float(np.exp(-0.3 * layer_idx))
    lam = (jnp.exp(jnp.sum(lq1.astype(jnp.float32) * lk1.astype(jnp.float32)))
           - jnp.exp(jnp.sum(lq2.astype(jnp.float32) * lk2.astype(jnp.float32)))
           + lambda_init)
    df_o = differential_attention(dq, dk, dv, lam, chunk)
    df_o = (rmsnorm(df_o, subln_g) * (1.0 - lambda_init)).reshape(B, Lp, DIFF_WIDTH)

    mix = jnp.concatenate([sb_o * jax.nn.silu(sb_g), df_o * jax.nn.silu(df_g)], axis=-1)[:, :L]
    return h + jnp.einsum('blc,cd->bld', mix, w_out.astype(mix.dtype))


def setup_inputs(seed: int = 0) -> dict:
    key = jax.random.key(seed)
    ks = jax.random.split(key, 11)
    f32 = jnp.float32
    x = jax.random.normal(ks[0], (BATCH, SEQ, D_MODEL), f32)
    meta_tokens = jax.random.normal(ks[1], (N_META, D_MODEL), f32)
    norm_gain = 1.0 + 0.02 * jax.random.normal(ks[2], (DEPTH, D_MODEL), f32)
    w_in = jax.random.normal(ks[3], (DEPTH, D_MODEL, IN_COLS), f32) * D_MODEL ** -0.5
    w_out = jax.random.normal(ks[4], (DEPTH, D_MIX, D_MODEL), f32) * D_MIX ** -0.5
    lambda_q1 = 0.1 * jax.random.normal(ks[5], (DEPTH, DIFF_HEAD_DIM), f32)
    lambda_k1 = 0.1 * jax.random.normal(ks[6], (DEPTH, DIFF_HEAD_DIM), f32)
    lambda_q2 = 0.1 * jax.random.normal(ks[7], (DEPTH, DIFF_HEAD_DIM), f32)
    lambda_k2 = 0.1 * jax.random.normal(ks[8], (DEPTH, DIFF_HEAD_DIM), f32)
    subln_gain = 1.0 + 0.02 * jax.random.normal(ks[9], (DEPTH, DIFF_V_DIM), f32)
    final_norm_gain = 1.0 + 0.02 * jax.random.normal(ks[10], (D_MODEL,), f32)
    return {"x": x, "meta_tokens": meta_tokens, "norm_gain": norm_gain,
            "w_in": w_in, "w_out": w_out,
            "lambda_q1": lambda_q1, "lambda_k1": lambda_k1,
            "lambda_q2": lambda_q2, "lambda_k2": lambda_k2,
            "subln_gain": subln_gain, "final_norm_gain": final_norm_gain}


def reference(x, meta_tokens, norm_gain, w_in, w_out, lambda_q1, lambda_k1,
              lambda_q2, lambda_k2, subln_gain, final_norm_gain):
    B = x.shape[0]
    meta = jnp.broadcast_to(meta_tokens[None].astype(x.dtype), (B, N_META, D_MODEL))
    h = jnp.concatenate([meta, x], axis=1)
    L = h.shape[1]
    Lp = -(-L // Q_BLOCK) * Q_BLOCK
    cos, sin = rope_tables(Lp, DIFF_HEAD_DIM)
    chunk = chunk_ids(Lp)
    for i in range(DEPTH):
        h = hybrid_layer(h, norm_gain[i], w_in[i], w_out[i],
                         lambda_q1[i], lambda_k1[i], lambda_q2[i], lambda_k2[i],
                         subln_gain[i], i, cos, sin, chunk, Lp)
    y = rmsnorm(h, final_norm_gain)
    return y[:, N_META:]
```

```python
import contextlib
import numpy as np
import ml_dtypes
import concourse.bass as bass
import concourse.mybir as mybir
from concourse.bass_utils import run_bass_kernel_spmd

F32 = mybir.dt.float32
BF16 = mybir.dt.bfloat16
AF = mybir.ActivationFunctionType
ALU = mybir.AluOpType
AX = mybir.AxisListType

SEQ = 2048
NMETA = 16
NTOK = SEQ + NMETA
D = 1024
NB = 17
EPS = 1e-6
LAMBDA_INIT = 0.8 - 0.6 * float(np.exp(-0.3 * 0))

DEBUG = False
STOP_AFTER = 99
DF_META = True
DFP_LEVEL = 9
DF_ODB = True
DF_DSK = 4
DF_ZP = 4
DF_LEVEL = 9
DMAT_MOD = 2
EPI_ACT_FROM_TILE = 16


def blk_rows(b):
    return 128 if b < 16 else 16


class Prog:
    ENG = ("pe", "act", "dve", "pool", "sp")

    def __init__(self, nc, stack):
        self.nc = nc
        self.stack = stack
        self.streams = {e: [] for e in self.ENG}
        self.sems = {}
        self.semval = {}
        for e in self.ENG:
            self.sems[e] = stack.enter_context(nc.semaphore("s_" + e))
            self.semval[e] = 0
        self.known = {e: {} for e in self.ENG}
        self.snap = {}
        self.res = {}
        self.ninstr = 0
        self.enabled = True
        self.nbar = 0

    def dma_sem(self, key):
        if key not in self.sems:
            self.sems[key] = self.stack.enter_context(self.nc.semaphore("d_" + key))
            self.semval[key] = 0
        return key

    def _wait(self, eng, ev):
        key, val = ev
        if self.known[eng].get(key, 0) >= val:
            return
        self.known[eng][key] = val
        inherited = self.snap.get((key, val))
        if inherited:
            kn = self.known[eng]
            for k2, v2 in inherited.items():
                if k2 != eng and kn.get(k2, 0) < v2:
                    kn[k2] = v2
        sem = self.sems[key]
        self.streams[eng].append(lambda e, sem=sem, val=val: e.wait_ge(sem, val))

    def _deps(self, eng, reads, writes):
        evs = []
        for r in reads:
            st = self.res.get(r)
            if st and st["w"]:
                evs.append((st["w"], "raw"))
        for w in writes:
            st = self.res.get(w)
            if st:
                if st["w"]:
                    evs.append((st["w"], "waw"))
                for k, v in st["r"].items():
                    evs.append(((k, v), "war"))
        for ev, kind in evs:
            key = ev[0]
            if key == eng:
                if eng in ("pe", "sp") or kind == "war":
                    continue
            self._wait(eng, ev)

    def _commit(self, ev, reads, writes):
        for r in reads:
            st = self.res.setdefault(r, {"w": None, "r": {}})
            k, v = ev
            if st["r"].get(k, 0) < v:
                st["r"][k] = v
        for w in writes:
            self.res[w] = {"w": ev, "r": {}}

    def op(self, eng, fn, reads=(), writes=()):
        if not self.enabled:
            return
        self._deps(eng, reads, writes)
        self.semval[eng] += 1
        ev = (eng, self.semval[eng])
        self.snap[ev] = dict(self.known[eng])
        sem = self.sems[eng]
        self.streams[eng].append(lambda e, fn=fn, sem=sem: fn(e).then_inc(sem, 1))
        self._commit(ev, reads, writes)
        self.ninstr += 1

    def dma(self, eng, slot, out, in_, reads=(), writes=(), transpose=False):
        if not self.enabled:
            return
        key = self.dma_sem(slot)
        self._deps(eng, reads, writes)
        self.semval[key] += 16
        ev = (key, self.semval[key])
        sem = self.sems[key]
        if transpose:
            self.streams[eng].append(
                lambda e, out=out, in_=in_, sem=sem: e.dma_start_transpose(out=out, in_=in_).then_inc(sem, 16))
        else:
            self.streams[eng].append(
                lambda e, out=out, in_=in_, sem=sem: e.dma_start(out=out, in_=in_).then_inc(sem, 16))
        self._commit(ev, reads, writes)
        self.ninstr += 1

    def barrier(self):
        if not self.enabled:
            return
        self.nbar += 1
        if self.nbar > STOP_AFTER:
            self.enabled = False
        for e in self.ENG:
            for k, v in self.semval.items():
                if k != e and v > 0:
                    self._wait(e, (k, v))
        self.res = {}


def build_nc(dbg=False):
    nc = bass.Bass("TRN2", target_bir_lowering=False)

    def din(name, shape, dt=F32):
        return nc.dram_tensor(name, list(shape), dt, kind="ExternalInput").ap()

    xr = din("xr", [SEQ, D])
    metar = din("metar", [NMETA, D])
    wext = din("wext", [D, 4096])
    wout = din("wout", [D, D])
    gbc_d = din("gbc", [128, D])
    gfin_d = din("gfin", [128, D])
    gsub_d = din("gsub", [128, 128])
    lamv_d = din("lamv", [128, 256])
    ropec_d = din("ropec", [128, NTOK])
    ropes_d = din("ropes", [128, NTOK])
    cbf_d = din("cbf", [128, 640], BF16)
    cf_d = din("cf", [128, 2192])
    out_d = nc.dram_tensor("out", [SEQ, D], F32, kind="ExternalOutput").ap()
    if dbg:
        dbg_d = nc.dram_tensor("dbg", [128, 8 * NTOK], F32, kind="ExternalOutput").ap()

    with contextlib.ExitStack() as st:
        def sb(name, shape, dt):
            return st.enter_context(nc.sbuf_tensor(name, list(shape), dt))

        uT = sb("uT", [128, 8, NTOK], BF16)
        A1 = sb("A1", [128, 4, NTOK], BF16)
        A2 = sb("A2", [128, 4, NTOK], BF16)
        A3 = sb("A3", [128, NB, 520], BF16)
        A4 = sb("A4", [128, 16, 512], BF16)
        A5flat = sb("A5", [128, NB * 512], BF16)
        A5 = A5flat[:, :].rearrange("p (b n) -> p b n", n=512)
        A5f = A5flat.bitcast(F32)
        mixT = sb("mixT", [128, 8, SEQ], BF16)
        scr = sb("scr", [128, 16384], BF16)
        scrf = scr.bitcast(F32)
        tmpf = sb("tmpf", [128, 2048], F32)
        gbc = sb("gbcs", [128, D], F32)
        cbf = sb("cbfs", [128, 640], BF16)
        cf = sb("cfs", [128, 2192], F32)
        gsub = sb("gsubs", [128, 128], F32)
        lamv = sb("lamvs", [128, 256], F32)
        small = sb("small", [128, 256], F32)
        ps = [st.enter_context(nc.psum_tensor(f"ps{i}", [128, 512], F32)) for i in range(8)]
        psb = [p.bitcast(BF16) for p in ps]

        P = Prog(nc, st)

        ident_bf = cbf[:, 0:128]
        negM2T = cbf[:, 128:256]
        Dmat = cbf[:, 256:384]
        dlast = cbf[0:1, 384:512]
        permT_bf = cbf[:, 512:640]
        permT = cf[:, 0:128]
        M1Z = cf[:, 128:2192]

        P.dma("sp", "c_gbc", gbc[:, :], gbc_d, writes=["gbc"])
        P.dma("sp", "c_cbf", cbf[:, :], cbf_d, writes=["cbf"])
        P.dma("sp", "c_cf", cf[:, :], cf_d, writes=["cf"])
        P.dma("sp", "c_gsub", gsub[:, :], gsub_d, writes=["gsub"])
        P.dma("sp", "c_lamv", lamv[:, :], lamv_d, writes=["lamv"])

        P.op("pool", lambda e: e.memset(small[:, 65:66], EPS), writes=["epsc"])
        epsc = small[:, 65:66]
        P.op("pool", lambda e: e.memset(small[:, 66:68], -0.5), writes=["neghalf"])
        neghalf = small[:, 66:67]
        neghalf2 = small[:, 66:68]
        P.op("dve", lambda e: e.tensor_tensor(out=tmpf[:, 0:64], in0=lamv[:, 0:64], in1=lamv[:, 64:128], op=ALU.mult),
             reads=["lamv"], writes=["lt0"])
        P.op("dve", lambda e: e.reduce_sum(out=small[:, 60:61], in_=tmpf[:, 0:64], axis=AX.X),
             reads=["lt0"], writes=["s1"])
        P.op("dve", lambda e: e.tensor_tensor(out=tmpf[:, 64:128], in0=lamv[:, 128:192], in1=lamv[:, 192:256], op=ALU.mult),
             reads=["lamv"], writes=["lt1"])
        P.op("dve", lambda e: e.reduce_sum(out=small[:, 61:62], in_=tmpf[:, 64:128], axis=AX.X),
             reads=["lt1"], writes=["s2"])
        P.op("act", lambda e: e.activation(out=small[:, 62:64], in_=small[:, 60:62], func=AF.Exp),
             reads=["s1", "s2"], writes=["e12"])
        P.op("dve", lambda e: e.tensor_tensor(out=small[:, 64:65], in0=small[:, 63:64], in1=small[:, 62:63], op=ALU.subtract),
             reads=["e12"], writes=["nl0"])
        P.op("dve", lambda e: e.tensor_scalar(out=small[:, 64:65], in0=small[:, 64:65], scalar1=-LAMBDA_INIT, scalar2=None, op0=ALU.add),
             reads=["nl0"], writes=["neglam"])
        neglam = small[:, 64:65]
        P.op("dve", lambda e: e.tensor_scalar(out=gsub[:, :], in0=gsub[:, :], scalar1=1.0 - LAMBDA_INIT, scalar2=None, op0=ALU.mult),
             reads=["gsub"], writes=["gsub"])

        wbuf = [scr[:, 8192 + s * 4096: 8192 + (s + 1) * 4096].rearrange("p (c n) -> p c n", n=512) for s in range(2)] + \
               [scr[:, s * 4096:(s + 1) * 4096].rearrange("p (c n) -> p c n", n=512) for s in range(2)]

        def load_w(slot, g, after=()):
            src = wext[:, g * 512:(g + 1) * 512].rearrange("(c p) n -> p c n", p=128)
            P.dma("pool", f"w{slot}", wbuf[slot], src, reads=list(after), writes=[f"w{slot}"])

        load_w(0, 0)
        load_w(1, 1, after=["w0"])

        xs = [A5f[:, s * 1024:(s + 1) * 1024] for s in range(2)]
        ub = [A5flat[:, 4096 + s * 1024: 4096 + (s + 1) * 1024] for s in range(2)]
        sqj = A5flat[:, 6144:7168]
        def p0_front(b):
            rows = blk_rows(b)
            s = b % 2
            src = xr[b * 128:(b + 1) * 128, :] if b < 16 else metar
            P.dma("sp", f"xs{s}", xs[s][:rows, :], src, writes=[f"xs{s}"])
            P.op("act", lambda e: e.activation(
                out=sqj[:rows, :], in_=xs[s][:rows, :], func=AF.Square, accum_out=small[:rows, b:b + 1]),
                reads=[f"xs{s}"], writes=["sqj", f"ss{b}"])
            P.op("dve", lambda e: e.tensor_scalar(
                out=small[:rows, 17 + b:18 + b], in0=small[:rows, b:b + 1], scalar1=1.0 / D, scalar2=EPS,
                op0=ALU.mult, op1=ALU.add),
                reads=[f"ss{b}"], writes=[f"sr{b}"])
            P.op("pool", lambda e: e.tensor_tensor(
                out=small[:rows, 34 + b:35 + b], in0=small[:rows, 17 + b:18 + b], in1=neghalf[:rows, :], op=ALU.pow),
                reads=[f"sr{b}", "neghalf"], writes=[f"rstd{b}"])
            P.op("dve", lambda e: e.scalar_tensor_tensor(
                out=ub[s][:rows, :], in0=xs[s][:rows, :], scalar=small[:rows, 34 + b:35 + b], in1=gbc[:rows, :],
                op0=ALU.mult, op1=ALU.mult),
                reads=[f"xs{s}", f"rstd{b}", "gbc"], writes=[f"ub{s}"])

        def p0_tr(b):
            rows = blk_rows(b)
            s = b % 2
            pi = 4 + b % 2
            for c in range(8):
                P.op("pe", lambda e, c=c: e.transpose(
                    out=psb[pi][:, c * 128:c * 128 + rows], in_=ub[s][:rows, c * 128:(c + 1) * 128],
                    identity=ident_bf[:rows, :rows]),
                    reads=[f"ub{s}", "cbf"], writes=[f"ps{pi}"])

        def p0_back(b):
            rows = blk_rows(b)
            pi = 4 + b % 2
            srcv = psb[pi][:, :].rearrange("p (c t) -> p c t", t=128)[:, :, :rows]
            dstv = uT[:, :, b * 128:b * 128 + rows]
            if b % 2 == 0:
                P.op("act", lambda e: e.activation(out=dstv, in_=srcv, func=AF.Copy),
                     reads=[f"ps{pi}"], writes=[f"uT{b}"])
            else:
                P.op("dve", lambda e: e.tensor_copy(out=dstv, in_=srcv),
                     reads=[f"ps{pi}"], writes=[f"uT{b}"])


        TR = [(0, 512), (512, 512), (1024, 512), (1536, 512), (2048, 16)]
        pcount = [0]
        pspool = [[0, 1, 2, 3, 6, 7]]

        def nextps(pool):
            i = pool[pcount[0] % len(pool)]
            pcount[0] += 1
            return i

        def feat_unit(slot, evac, cc, t0, n):
            pi = nextps(pspool[0])
            ub_ = [f"uT{b}" for b in range(t0 // 128, (t0 + n + 127) // 128)]
            for dc in range(8):
                P.op("pe", lambda e, dc=dc: e.matmul(
                    out=ps[pi][:, :n], lhsT=wbuf[slot][:, dc, cc * 128:(cc + 1) * 128],
                    rhs=uT[:, dc, t0:t0 + n], start=(dc == 0), stop=(dc == 7)),
                    reads=[f"w{slot}"] + ub_, writes=[f"ps{pi}"])
            evac(cc, t0, n, pi)

        def tok_unit(slot, evac, b):
            rows = blk_rows(b)
            pi = nextps(pspool[0])
            for dc in range(8):
                P.op("pe", lambda e, dc=dc: e.matmul(
                    out=ps[pi][:rows, :512], lhsT=uT[:, dc, b * 128:b * 128 + rows],
                    rhs=wbuf[slot][:, dc, :], start=(dc == 0), stop=(dc == 7)),
                    reads=[f"w{slot}", f"uT{b}"], writes=[f"ps{pi}"])
            evac(b, rows, pi)

        def proj_feat(slot, evac, tr=None):
            for cc in range(4):
                for (t0, n) in (tr or TR):
                    feat_unit(slot, evac, cc, t0, n)

        def proj_tok(slot, evac, nblk):
            for b in range(nblk):
                tok_unit(slot, evac, b)

        flip = [0]

        def evac_copy_to(dst3):
            def f(cc, t0, n, pi):
                flip[0] ^= 1
                if flip[0]:
                    P.op("act", lambda e: e.activation(out=dst3[:, cc, t0:t0 + n], in_=ps[pi][:, :n], func=AF.Copy),
                         reads=[f"ps{pi}"], writes=[])
                else:
                    P.op("dve", lambda e: e.tensor_copy(out=dst3[:, cc, t0:t0 + n], in_=ps[pi][:, :n]),
                         reads=[f"ps{pi}"], writes=[])
            return f

        QP = [A1[:, h, 0:SEQ] for h in range(4)] + [mixT[:, 4 + h, :] for h in range(4)]
        for h in range(8):
            zr = slice(64, 128) if h % 2 == 0 else slice(0, 64)
            P.op("pool", lambda e, h=h, zr=zr: e.memset(QP[h][zr, :], 0.0), writes=[f"qpz{h}"])

        def evac_q(cc, t0, n, pi):
            if t0 >= SEQ:
                return
            P.op("act", lambda e: e.activation(out=QP[2 * cc][0:64, t0:t0 + n], in_=ps[pi][0:64, :n], func=AF.Copy),
                 reads=[f"ps{pi}"], writes=[])
            P.op("dve", lambda e: e.tensor_copy(out=QP[2 * cc + 1][64:128, t0:t0 + n], in_=ps[pi][64:128, :n]),
                 reads=[f"ps{pi}"], writes=[])
        def evac_v(b, rows, pi):
            P.op("act", lambda e: e.activation(out=A3[:rows, b, 0:512], in_=ps[pi][:rows, :512], func=AF.Copy),
                 reads=[f"ps{pi}"], writes=[f"v{b}"])

        def evac_g(b, rows, pi):
            P.op("act", lambda e: e.activation(out=A4[:rows, b, :], in_=ps[pi][:rows, :512], func=AF.Silu),
                 reads=[f"ps{pi}"], writes=[])

        load_w(2, 2, after=["w1"])
        load_w(3, 3, after=["w1"])
        evac_k = evac_copy_to(A2)
        fifo = []
        p0_front(0)
        p0_tr(0)
        for b in range(1, NB + 1):
            if b < NB:
                p0_front(b)
            if b >= 1:
                p0_back(b - 1)
                bb = b - 1
                if bb % 4 == 3 or bb == 16:
                    r = bb // 4
                    t0, n = TR[r]
                    for cc in range(4):
                        if r < 4:
                            fifo.append(lambda cc=cc, t0=t0, n=n: feat_unit(0, evac_q, cc, t0, n))
                        fifo.append(lambda cc=cc, t0=t0, n=n: feat_unit(1, evac_k, cc, t0, n))
                    for b2 in range(4 * r, min(4 * r + 4, NB)):
                        fifo.append(lambda b2=b2: tok_unit(2, evac_v, b2))
                        if b2 < 16:
                            fifo.append(lambda b2=b2: tok_unit(3, evac_g, b2))
            for _ in range(4):
                if fifo:
                    fifo.pop(0)()
            if b < NB:
                p0_tr(b)
        while fifo:
            fifo.pop(0)()
        P.barrier()
        P.op("pool", lambda e: e.memset(A5[:, 16, :], 0.0), writes=["dvmeta"])
        for b in range(NB):
            rows = blk_rows(b)
            pi = nextps([0, 1, 2, 3])
            last = (b == NB - 1)
            P.op("pe", lambda e, b=b, rows=rows, pi=pi, last=last: e.matmul(
                out=ps[pi][:rows, :512], lhsT=Dmat[:rows, :rows], rhs=A3[:rows, b, 0:512], start=True, stop=last),
                reads=[f"v{b}", "cbf"], writes=[f"ps{pi}"])
            if not last:
                P.op("pe", lambda e, b=b, pi=pi: e.matmul(
                    out=ps[pi][:128, :512], lhsT=dlast, rhs=A3[0:1, b + 1, 0:512], start=False, stop=True),
                    reads=[f"v{b + 1}"], writes=[f"ps{pi}"])
            P.op("dve", lambda e, b=b, rows=rows, pi=pi: e.tensor_copy(out=A5[:rows, b, :], in_=ps[pi][:rows, :512]),
                 reads=[f"ps{pi}"] + (["dvmeta"] if b == 16 else []), writes=[f"dv{b}"] + (["dvmeta"] if b == 16 else []))

        keepL = [scrf[:, i * 2064:(i + 1) * 2064] for i in range(2)]
        Pb = [scr[:, 8256 + i * 2176: 8256 + (i + 1) * 2176] for i in range(3)]
        tmpb = tmpf.bitcast(BF16)
        if DMAT_MOD:
            PTDs = [tmpb[:, 512 + i * 1152: 512 + (i + 1) * 1152].rearrange("p (b t) -> p b t", t=128) for i in range(2)]
            PTb = [scr[:, 14784 + i * 512: 14784 + (i + 1) * 512] for i in range(3)] + \
                  [tmpb[:, 2816 + i * 512: 2816 + (i + 1) * 512] for i in range(2)]
        else:
            PTb = [scr[:, 14784 + i * 512: 14784 + (i + 1) * 512] for i in range(3)] + \
                  [tmpb[:, 512 + i * 512: 512 + (i + 1) * 512] for i in range(7)]
        NPT = len(PTb)

        def is_dma_chunk(k, c):
            return bool(DMAT_MOD) and c >= 2
        for i in range(3):
            P.op("pool", lambda e, i=i: e.memset(Pb[i][:, :], 0.0), writes=[f"Pb{i}"])
        mixtok = [tmpf.bitcast(BF16)[:, 0:512]]
        cnt = {"z": 0, "k": 0, "p": 0, "t": 0, "pt": 0, "ev": 0}

        def rot(name, n):
            i = cnt[name] % n
            cnt[name] += 1
            return i

        def sb_chunks(j):
            blocks = list(range(j, 16))
            chunks = []
            while blocks:
                cb = blocks[:4]
                blocks = blocks[4:]
                chunks.append([(kb, 128) for kb in cb])
            if len(chunks[-1]) < 4:
                chunks[-1].append((16, 16))
            else:
                chunks.append([(16, 16)])
            return chunks

        items = [(j, h) for j in range(16) for h in range(8)]
        NI = len(items)
        ptslot = {}

        def st_z(k, c):
            j, h = items[k]
            ch = sb_chunks(j)[c]
            cc, po = h // 2, (h % 2) * 64
            n = sum(r for _, r in ch)
            t0 = ch[0][0] * 128
            off = t0 - 128 * j
            zi = rot("z", 2)
            ks = k % 2
            P.op("pe", lambda e: e.matmul(
                out=ps[zi][:, :n], lhsT=QP[h][:, j * 128:(j + 1) * 128],
                rhs=A2[:, cc, t0:t0 + n], start=True, stop=True),
                reads=[], writes=[f"ps{zi}"])
            P.op("act", lambda e: e.activation(
                out=keepL[ks][:, off:off + n], in_=ps[zi][:, :n], func=AF.Sigmoid, scale=-0.125),
                reads=[f"ps{zi}"], writes=[f"keep{ks}"])

        def st_scan(k):
            j, h = items[k]
            ntot = NTOK - 128 * j
            ks, pslot = k % 2, k % 3
            P.op("dve", lambda e: e.tensor_tensor_scan(
                out=Pb[pslot][:, :ntot], data0=keepL[ks][:, :ntot], data1=M1Z[:, :ntot], initial=1.0,
                op0=ALU.mult, op1=ALU.max),
                reads=[f"keep{ks}", "cf"], writes=[f"Pb{pslot}"])

        def st_t(k, c):
            j, h = items[k]
            ch = sb_chunks(j)[c]
            pslot = k % 3
            if is_dma_chunk(k, c):
                if c == 2:
                    nfar = 17 - j - 8
                    P.dma("sp", "dmat", PTDs[k % 2][:, 0:nfar, :], Pb[pslot][:, 1024:1024 + nfar * 128],
                          reads=[f"Pb{pslot}"], writes=[f"PTD{k % 2}", "dmat_serial"], transpose=True)
                return
            ti = 2 + rot("t", 2)
            off = ch[0][0] * 128 - 128 * j
            col = off
            for bi, (kb, r) in enumerate(ch):
                P.op("pe", lambda e, bi=bi, col=col: e.transpose(
                    out=psb[ti][:, bi * 128:(bi + 1) * 128], in_=Pb[pslot][:, col:col + 128],
                    identity=ident_bf),
                    reads=[f"Pb{pslot}", "cbf"], writes=[f"ps{ti}"])
                col += r
            pti = rot("pt", NPT)
            ptslot[(k, c)] = pti
            nb_ = len(ch)
            if ((not DMAT_MOD) and rot("ev", 4) == 3) or (DMAT_MOD and j >= 11 and c == 1):
                P.op("dve", lambda e: e.tensor_copy(out=PTb[pti][:, :nb_ * 128], in_=psb[ti][:, :nb_ * 128]),
                     reads=[f"ps{ti}"], writes=[f"PT{pti}"])
            else:
                P.op("act", lambda e: e.activation(
                    out=PTb[pti][:, :nb_ * 128], in_=psb[ti][:, :nb_ * 128], func=AF.Copy),
                    reads=[f"ps{ti}"], writes=[f"PT{pti}"])

        def st_av(k, c):
            j, h = items[k]
            ch = sb_chunks(j)[c]
            oi = 6 + (j % 2)
            hc = slice(h * 64, (h + 1) * 64)
            if is_dma_chunk(k, c):
                for bi, (kb, r) in enumerate(ch):
                    P.op("pe", lambda e, bi=bi, kb=kb: e.matmul(
                        out=ps[oi][:, hc], lhsT=PTDs[k % 2][:, kb - j - 8, :],
                        rhs=A5[:, kb, hc], start=(c == 0 and bi == 0), stop=False),
                        reads=[f"PTD{k % 2}", f"dv{kb}"], writes=[f"ps{oi}"])
                return
            pti = ptslot[(k, c)]
            for bi, (kb, r) in enumerate(ch):
                P.op("pe", lambda e, bi=bi, kb=kb: e.matmul(
                    out=ps[oi][:, hc], lhsT=PTb[pti][:, bi * 128:(bi + 1) * 128],
                    rhs=A5[:, kb, hc], start=(c == 0 and bi == 0), stop=False),
                    reads=[f"PT{pti}", f"dv{kb}"], writes=[f"ps{oi}"])

        def st_tail(k):
            j, h = items[k]
            oi = 6 + (j % 2)
            hc = slice(h * 64, (h + 1) * 64)
            P.op("pe", lambda e: e.matmul(
                out=ps[oi][:, hc], lhsT=negM2T, rhs=A5[:, j, hc], start=False, stop=False),
                reads=["cbf", f"dv{j}"], writes=[f"ps{oi}"])
            P.op("pe", lambda e: e.matmul(
                out=ps[oi][:, hc], lhsT=ident_bf, rhs=A3[:, j, hc], start=False, stop=True),
                reads=["cbf"], writes=[f"ps{oi}"])
            if h == 7:
                P.op("dve", lambda e: e.tensor_tensor(
                    out=mixtok[0][:, :], in0=ps[oi][:, :512], in1=A4[:, j, :], op=ALU.mult),
                    reads=[f"ps{oi}"], writes=["mixtok"])
                ti = 4 + (j % 2)
                for c in range(4):
                    P.op("pe", lambda e, c=c: e.transpose(
                        out=psb[ti][:, c * 128:(c + 1) * 128], in_=mixtok[0][:, c * 128:(c + 1) * 128], identity=ident_bf),
                        reads=["mixtok", "cbf"], writes=[f"ps{ti}"])
                P.op("act", lambda e: e.activation(
                    out=mixT[:, 0:4, j * 128:(j + 1) * 128],
                    in_=psb[ti][:, 0:512].rearrange("p (c t) -> p c t", t=128), func=AF.Copy),
                    reads=[f"ps{ti}"], writes=[])

        def nch(k):
            return len(sb_chunks(items[k][0])) if 0 <= k < NI else 0

        SK = 2
        for k in range(NI + SK + 1):
            if 0 <= k - 1 < NI:
                st_scan(k - 1)
            na, nt_, nv_ = nch(k), nch(k - SK), nch(k - SK - 1)
            for c in range(na):
                st_z(k, c)
            for c in range(max(nt_, nv_)):
                if c < nt_:
                    st_t(k - SK, c)
                if c < nv_:
                    st_av(k - SK - 1, c)
            if 0 <= k - SK - 1 < NI:
                st_tail(k - SK - 1)
            if k == NI:
                for slot, g in ((2, 6), (3, 7)):
                    src = wext[:, g * 512:(g + 1) * 512].rearrange("(c p) n -> p c n", p=128)
                    P.dma("pool", f"w{slot}", wbuf[slot], src, writes=[f"w{slot}", "keep0", "keep1", "dmat_serial"])
        P.barrier()
        load_w(0, 4)
        load_w(1, 5)

        pspool[0] = [0, 1, 2, 3, 4, 5, 6, 7]
        ropec = A5f[:, 0:NTOK]
        ropes = A5f[:, NTOK:2 * NTOK]
        P.dma("sp", "c_rc", ropec, ropec_d, writes=["ropec"])
        P.dma("sp", "c_rs", ropes, ropes_d, writes=["ropes"])
        vaug4 = A3[:, :, :].rearrange("p b (h e) -> p b h e", e=130)
        P.op("pool", lambda e: e.memset(vaug4[:, :, :, 128:129], 1.0), writes=["vones"])
        P.op("pool", lambda e: e.memset(vaug4[:, :, :, 129:130], 0.0), writes=["vzero"])

        QPD = [A1[:, h, 0:SEQ] for h in range(4)] + [mixT[:, 4 + h, :] for h in range(4)]
        for h in range(8):
            zr = slice(64, 128) if h % 2 == 0 else slice(0, 64)
            P.op("pool", lambda e, h=h, zr=zr: e.memset(QPD[h][zr, :], 0.0), writes=[f"qpdz{h}"])

        qhb = [scr[:, i * 512:(i + 1) * 512] for i in range(4)]
        qlb = [scr[:, 2048 + i * 512: 2048 + (i + 1) * 512] for i in range(4)]

        def proj_rope(dst3, slot, padded=False):
            units = [(cc, t0, n) for cc in range(4) for (t0, n) in (TR[:4] if padded else TR)]
            pend = None

            def finish(u):
                cc, t0, n, pa, qi = u
                pb = nextps([0, 1, 2, 3, 4, 5, 6, 7])
                P.op("pe", lambda e: e.matmul(out=ps[pb][:, :n], lhsT=permT_bf, rhs=qhb[qi][:, :n], start=True, stop=False),
                     reads=[f"qs{qi}", "cbf"], writes=[f"ps{pb}"])
                P.op("pe", lambda e: e.matmul(out=ps[pb][:, :n], lhsT=permT_bf, rhs=qlb[qi][:, :n], start=False, stop=True),
                     reads=[f"ql{qi}", "cbf"], writes=[f"ps{pb}"])
                ta = rot("k", 2)
                t1 = tmpf[:, ta * 1024: ta * 1024 + n]
                t2 = tmpf[:, ta * 1024 + 512: ta * 1024 + 512 + n]
                P.op("dve", lambda e: e.tensor_tensor(out=t1, in0=ps[pa][:, :n], in1=ropec[:, t0:t0 + n], op=ALU.mult),
                     reads=[f"ps{pa}", "ropec"], writes=[f"t1{ta}"])
                P.op("dve", lambda e: e.tensor_tensor(out=t2, in0=ps[pb][:, :n], in1=ropes[:, t0:t0 + n], op=ALU.mult),
                     reads=[f"ps{pb}", "ropes"], writes=[f"t2{ta}"])
                if padded:
                    P.op("pool", lambda e: e.tensor_tensor(
                        out=QPD[2 * cc][0:64, t0:t0 + n], in0=t1[0:64, :], in1=t2[0:64, :], op=ALU.add),
                        reads=[f"t1{ta}", f"t2{ta}", f"qpdz{2 * cc}"], writes=[f"qpa{ta}"])
                    P.op("pool", lambda e: e.tensor_tensor(
                        out=QPD[2 * cc + 1][64:128, t0:t0 + n], in0=t1[64:128, :], in1=t2[64:128, :], op=ALU.add),
                        reads=[f"t1{ta}", f"t2{ta}", f"qpdz{2 * cc + 1}"], writes=[f"qpb{ta}"])
                else:
                    P.op("pool", lambda e: e.tensor_tensor(
                        out=dst3[:, cc, t0:t0 + n], in0=t1, in1=t2, op=ALU.add),
                        reads=[f"t1{ta}", f"t2{ta}"], writes=[f"qpa{ta}"])

            for (cc, t0, n) in units:
                pa = nextps([0, 1, 2, 3, 4, 5, 6, 7])
                for dc in range(8):
                    P.op("pe", lambda e, dc=dc, pa=pa, cc=cc, t0=t0, n=n: e.matmul(
                        out=ps[pa][:, :n], lhsT=wbuf[slot][:, dc, cc * 128:(cc + 1) * 128],
                        rhs=uT[:, dc, t0:t0 + n], start=(dc == 0), stop=(dc == 7)),
                        reads=[f"w{slot}"], writes=[f"ps{pa}"])
                qi = rot("p", 4)
                P.op("act", lambda e, pa=pa, qi=qi, n=n: e.activation(out=qhb[qi][:, :n], in_=ps[pa][:, :n], func=AF.Copy),
                     reads=[f"ps{pa}"], writes=[f"qs{qi}"])
                P.op("dve", lambda e, pa=pa, qi=qi, n=n: e.tensor_tensor(
                    out=qlb[qi][:, :n], in0=ps[pa][:, :n], in1=qhb[qi][:, :n], op=ALU.subtract),
                    reads=[f"ps{pa}", f"qs{qi}"], writes=[f"ql{qi}"])
                if pend is not None:
                    finish(pend)
                pend = (cc, t0, n, pa, qi)
            finish(pend)

        def evac_vd(b, rows, pi):
            P.op("act", lambda e: e.activation(
                out=vaug4[:rows, b, :, 0:128], in_=ps[pi][:rows, :512].rearrange("p (h e) -> p h e", e=128), func=AF.Copy),
                reads=[f"ps{pi}"], writes=[])
        if DFP_LEVEL >= 1:
            proj_tok(2, evac_vd, NB)
        if DFP_LEVEL >= 2:
            proj_tok(3, evac_g, 16)
        if DFP_LEVEL >= 3:
            proj_rope(None, 0, padded=True)
        if DFP_LEVEL >= 4:
            proj_rope(A2, 1)
        P.barrier()

        wo = scr[:, 8192:16384].rearrange("p (c n) -> p c n", n=1024)
        P.dma("pool", "w0", wo, wout.rearrange("(c p) n -> p c n", p=128), writes=["wo"])
        gfin = A5f[:, 0:1024]
        P.dma("sp", "c_rc", gfin, gfin_d, writes=["gfin"])
        ETb = [scr[:, i * 512:(i + 1) * 512] for i in range(8)]
        of32 = [tmpf[:, i * 128:(i + 1) * 128] for i in range(2)]
        yf32 = [tmpf[:, 256 + i * 128: 256 + (i + 1) * 128] for i in range(2)]
        sqj2 = tmpf[:, 0:128]
        mixtok2s = [scr[:, 4096:4608], scr[:, 4608:5120]]
        sc = {"i": 0}
        dsteps = []
        for j in range(16):
            for hp in range(2):
                st_l = [[kb] for kb in range(j, NB)]
                for si, kbs in enumerate(st_l):
                    dsteps.append({"j": j, "hp": hp, "kbs": kbs, "last": si == len(st_l) - 1,
                                   "par": ((2 * j + hp) % 2) if DF_ODB else 1})

        def d_zexp(stp):
            j, hp, kbs = stp["j"], stp["hp"], stp["kbs"]
            rows = blk_rows(kbs[0])
            zb = [rot("z", DF_ZP) for _ in kbs]
            eb = [rot("k", 8) for _ in kbs]
            stp["eb"] = eb
            for bi, kb in enumerate(kbs):
                for p2 in range(2):
                    hc0 = 4 * hp + 2 * p2
                    cc = hc0 // 2
                    if hc0 < 4:
                        rhs2 = A1[:, hc0:hc0 + 2, j * 128:(j + 1) * 128]
                    else:
                        rhs2 = mixT[:, hc0:hc0 + 2, j * 128:(j + 1) * 128]
                    P.op("pe", lambda e, zi=zb[bi], p2=p2, cc=cc, kb=kb, rhs2=rhs2: e.matmul(
                        out=ps[zi][:rows, p2 * 256:(p2 + 1) * 256], lhsT=A2[:, cc, kb * 128:kb * 128 + rows],
                        rhs=rhs2, start=True, stop=True),
                        reads=[f"qpd{j}"], writes=[f"ps{zb[bi]}"])
                zi, ei = zb[bi], eb[bi]
                P.op("act", lambda e, zi=zi, ei=ei: e.activation(
                    out=ETb[ei][:rows, :], in_=ps[zi][:rows, :], func=AF.Exp, scale=0.125),
                    reads=[f"ps{zi}"], writes=[f"ET{ei}"])
                if kb == j:
                    rect = ETb[ei][0:64, :].rearrange("p (i t) -> p i t", t=128)[:, :, 64:128]
                    P.op("pool", lambda e, rect=rect: e.memset(rect, 0.0),
                         reads=[], writes=[f"ET{ei}"])

        def d_av(stp):
            j, hp, kbs, eb = stp["j"], stp["hp"], stp["kbs"], stp["eb"]
            rows = blk_rows(kbs[0])
            for bi, kb in enumerate(kbs):
                for i in range(4):
                    head = (4 * hp + i) // 2
                    ei = eb[bi]
                    col = i * 128
                    ob, oc = 4 + 2 * stp["par"] + i // 2, (i % 2) * 256
                    P.op("pe", lambda e, i=i, ei=ei, col=col, kb=kb, head=head, ob=ob, oc=oc: e.matmul(
                        out=ps[ob][:, oc:oc + 129], lhsT=ETb[ei][:rows, col:col + 128],
                        rhs=A3[:rows, kb, head * 130:head * 130 + 129],
                        start=(kb == j and i % 2 == 0), stop=(kb == NB - 1), skip_group_check=True),
                        reads=[f"ET{ei}", "vones"], writes=[f"ps{ob}"])
            if stp["last"]:
                d_epilogue(j, hp, stp["par"])

        def d_epilogue(j, hp, par):
            k = sc["i"] % 4
            sc["i"] += 1
            q = k % 2
            cb = 80 + k * 16
            banks = [4 + 2 * par, 5 + 2 * par]
            tb = 512 + q * 768
            t1b = [tmpf[:, tb + hl * 128: tb + (hl + 1) * 128] for hl in range(2)]
            ofb = [tmpf[:, tb + 256 + hl * 128: tb + 256 + (hl + 1) * 128] for hl in range(2)]
            yfb = [tmpf[:, tb + 512 + hl * 128: tb + 512 + (hl + 1) * 128] for hl in range(2)]
            for hl in range(2):
                pb_ = banks[hl]
                P.op("dve", lambda e, pb_=pb_, hl=hl: e.reciprocal(
                    out=small[:, cb + 2 * hl:cb + 2 * hl + 2], in_=ps[pb_][:, 128:512:256]),
                    reads=[f"ps{pb_}"], writes=[f"rz{k}_{hl}"])
            P.op("dve", lambda e: e.tensor_scalar(
                out=small[:, cb + 4:cb + 6], in0=small[:, cb + 1:cb + 4:2], scalar1=neglam, scalar2=None, op0=ALU.mult),
                reads=[f"rz{k}_0", f"rz{k}_1", "neglam"], writes=[f"c1{k}"])
            act_t1 = j >= EPI_ACT_FROM_TILE
            for hl in range(2):
                pb_ = banks[hl]
                if act_t1:
                    P.op("act", lambda e, pb_=pb_, hl=hl: e.activation(
                        out=t1b[hl], in_=ps[pb_][:, 256:384], func=AF.Copy, scale=small[:, cb + 4 + hl:cb + 5 + hl]),
                        reads=[f"ps{pb_}", f"c1{k}"], writes=[f"t1e{q}_{hl}"])
                else:
                    P.op("dve", lambda e, pb_=pb_, hl=hl: e.tensor_scalar(
                        out=t1b[hl], in0=ps[pb_][:, 256:384], scalar1=small[:, cb + 4 + hl:cb + 5 + hl], scalar2=None, op0=ALU.mult),
                        reads=[f"ps{pb_}", f"c1{k}"], writes=[f"t1e{q}_{hl}"])
            for hl in range(2):
                pb_ = banks[hl]
                P.op("dve", lambda e, pb_=pb_, hl=hl: e.scalar_tensor_tensor(
                    out=ofb[hl], in0=ps[pb_][:, 0:128], scalar=small[:, cb + 2 * hl:cb + 2 * hl + 1], in1=t1b[hl],
                    op0=ALU.mult, op1=ALU.add),
                    reads=[f"ps{pb_}", f"rz{k}_{hl}", f"t1e{q}_{hl}"], writes=[f"of{q}_{hl}"])
            for hl in range(2):
                P.op("dve", lambda e, hl=hl: e.scalar_tensor_tensor(
                    out=sqj2, in0=ofb[hl], scalar=1.0, in1=ofb[hl], op0=ALU.mult, op1=ALU.mult,
                    accum_out=small[:, cb + 6 + hl:cb + 7 + hl]),
                    reads=[f"of{q}_{hl}"], writes=["sqj2", f"ss2{k}_{hl}"])
            P.op("dve", lambda e: e.tensor_scalar(
                out=small[:, cb + 8:cb + 10], in0=small[:, cb + 6:cb + 8], scalar1=1.0 / 128, scalar2=EPS,
                op0=ALU.mult, op1=ALU.add),
                reads=[f"ss2{k}_0", f"ss2{k}_1"], writes=[f"sq2{k}"])
            P.op("pool", lambda e: e.tensor_tensor(
                out=small[:, cb + 10:cb + 12], in0=small[:, cb + 8:cb + 10], in1=neghalf2, op=ALU.pow),
                reads=[f"sq2{k}", "neghalf"], writes=[f"rs2{k}"])
            def part_b():
                for hl in range(2):
                    head = 2 * hp + hl
                    P.op("dve", lambda e, hl=hl: e.scalar_tensor_tensor(
                        out=yfb[hl], in0=ofb[hl], scalar=small[:, cb + 10 + hl:cb + 11 + hl], in1=gsub[:, :],
                        op0=ALU.mult, op1=ALU.mult),
                        reads=[f"of{q}_{hl}", f"rs2{k}", "gsub"], writes=[f"yf{q}_{hl}"])
                    P.op("pool", lambda e, hl=hl, head=head: e.tensor_tensor(
                        out=mixtok2s[j % 2][:, head * 128:(head + 1) * 128], in0=yfb[hl], in1=A4[:, j, head * 128:(head + 1) * 128], op=ALU.mult),
                        reads=[f"yf{q}_{hl}"], writes=[f"mt2_{j % 2}_{head}"])
                if hp == 1:
                    pending_mix.append([j, 3])
            while pend_b:
                pend_b.pop(0)()
            pend_b.append(part_b)

        pend_b = []

        def d_mix(j):
            ti = rot("z", DF_ZP)
            mt = mixtok2s[j % 2]
            for c in range(4):
                P.op("pe", lambda e, ti=ti, c=c: e.transpose(
                    out=psb[ti][:, c * 128:(c + 1) * 128], in_=mt[:, c * 128:(c + 1) * 128], identity=ident_bf),
                    reads=[f"mt2_{j % 2}_{c}", "cbf"], writes=[f"ps{ti}"])
            P.op("dve", lambda e, ti=ti, j=j: e.tensor_copy(
                out=mixT[:, 4:8, j * 128:(j + 1) * 128],
                in_=psb[ti][:, 0:512].rearrange("p (c t) -> p c t", t=128)),
                reads=[f"ps{ti}"], writes=[f"qpd{j}"])

        pending_mix = []
        DSK = DF_DSK
        for idx in range(len(dsteps) + DSK):
            if idx < len(dsteps):
                d_zexp(dsteps[idx])
            for pm in list(pending_mix):
                pm[1] -= 1
                if pm[1] <= 0:
                    pending_mix.remove(pm)
                    d_mix(pm[0])
            if idx >= DSK:
                d_av(dsteps[idx - DSK])
        while pend_b:
            pend_b.pop(0)()
        for pm in pending_mix:
            d_mix(pm[0])
        P.barrier()

        xs2 = [scrf[:, s * 1024:(s + 1) * 1024] for s in range(4)]
        yo = [tmpf[:, 0:1024], tmpf[:, 1024:2048]]
        sqj3 = A4[:, 0:2, :].rearrange("p a n -> p (a n)")

        def p5_front(j):
            s = j % 4
            P.dma("sp", f"xs{s}", xs2[s], xr[j * 128:(j + 1) * 128, :], writes=[f"xs2{s}"])
            pa, pb = 2 * (j % 4), 2 * (j % 4) + 1
            for (pi, half) in ((pa, 0), (pb, 1)):
                for c in range(8):
                    P.op("pe", lambda e, pi=pi, half=half, c=c: e.matmul(
                        out=ps[pi][:, :512], lhsT=mixT[:, c, j * 128:(j + 1) * 128],
                        rhs=wo[:, c, half * 512:(half + 1) * 512], start=(c == 0), stop=(c == 7)),
                        reads=["wo"], writes=[f"ps{pi}"])
            for (pi, half) in ((pa, 0), (pb, 1)):
                P.op("dve", lambda e, pi=pi, half=half: e.tensor_tensor(
                    out=xs2[s][:, half * 512:(half + 1) * 512], in0=ps[pi][:, :512],
                    in1=xs2[s][:, half * 512:(half + 1) * 512], op=ALU.add),
                    reads=[f"ps{pi}", f"xs2{s}"], writes=[f"xs2{s}"])
            cb = 200 + (j % 4) * 4
            P.op("act", lambda e: e.activation(
                out=sqj3, in_=xs2[s], func=AF.Square, accum_out=small[:, cb:cb + 1]),
                reads=[f"xs2{s}"], writes=["sqj3", f"ss3{j % 4}"])

        def p5_back(j):
            s = j % 4
            so = j % 2
            cb = 200 + (j % 4) * 4
            P.op("dve", lambda e: e.tensor_scalar(
                out=small[:, cb + 1:cb + 2], in0=small[:, cb:cb + 1], scalar1=1.0 / D, scalar2=EPS,
                op0=ALU.mult, op1=ALU.add),
                reads=[f"ss3{j % 4}"], writes=[f"sq3{j % 4}"])
            P.op("pool", lambda e: e.tensor_tensor(
                out=small[:, cb + 2:cb + 3], in0=small[:, cb + 1:cb + 2], in1=neghalf, op=ALU.pow),
                reads=[f"sq3{j % 4}", "neghalf"], writes=[f"rs3{j % 4}"])
            P.op("dve", lambda e: e.scalar_tensor_tensor(
                out=yo[so], in0=xs2[s], scalar=small[:, cb + 2:cb + 3], in1=gfin, op0=ALU.mult, op1=ALU.mult),
                reads=[f"xs2{s}", f"rs3{j % 4}", "gfin"], writes=[f"yo{so}"])
            P.dma("pool", f"o{so}", out_d[j * 128:(j + 1) * 128, :], yo[so], reads=[f"yo{so}"], writes=[f"out{j}"])

        for j in range(17):
            if j < 16:
                p5_front(j)
            if j >= 1:
                p5_back(j - 1)
        P.barrier()

        with nc.Block() as block:
            @block.tensor
            def _(e):
                for f in P.streams["pe"]:
                    f(e)

            @block.scalar
            def _(e):
                for f in P.streams["act"]:
                    f(e)

            @block.vector
            def _(e):
                for f in P.streams["dve"]:
                    f(e)

            @block.gpsimd
            def _(e):
                for f in P.streams["pool"]:
                    f(e)

            @block.sync
            def _(e):
                for f in P.streams["sp"]:
                    f(e)
    return nc


def _consts():
    bf = ml_dtypes.bfloat16
    k = np.arange(128)[:, None]
    m = np.arange(128)[None, :]
    ident = (k == m).astype(np.float32)
    negM2T = -(k < m).astype(np.float32)
    Dm = (k == m + 1).astype(np.float32) - (k == m).astype(np.float32)
    dlast = np.zeros((128, 128), np.float32)
    dlast[0, 127] = 1.0
    swp = (np.arange(128) // 64) * 64 + ((np.arange(128) % 64) + 32) % 64
    permT = np.zeros((128, 128), np.float32)
    permT[swp, np.arange(128)] = 1.0
    cbf = np.concatenate([ident, negM2T, Dm, dlast, permT], axis=1).astype(bf)
    M1 = (m <= k).astype(np.float32)
    cf = np.concatenate([permT, M1, np.zeros((128, 1936), np.float32)], axis=1).astype(np.float32)
    inv = (1.0 / (np.float32(10000.0) ** (np.arange(0, 64, 2, dtype=np.float32) / np.float32(64)))).astype(np.float32)
    pos = (NTOK - 1 - np.arange(NTOK)).astype(np.float32)
    ang = (pos[None, :] * inv[:, None]).astype(np.float32)
    cos = np.cos(ang).astype(np.float32)
    sin = np.sin(ang).astype(np.float32)
    p = np.arange(128)
    ropec = cos[p % 32, :]
    sign = np.where((p % 64) < 32, -1.0, 1.0).astype(np.float32)[:, None]
    ropes = (sin[p % 32, :] * sign).astype(np.float32)
    return cbf, cf, np.ascontiguousarray(ropec), np.ascontiguousarray(ropes)


_NC_CACHE = {}


def kernel(x, meta_tokens, norm_gain, w_in, w_out, lambda_q1, lambda_k1, lambda_q2, lambda_k2,
           subln_gain, final_norm_gain):
    x = np.asarray(x, np.float32)
    B = x.shape[0]
    w = np.asarray(w_in, np.float32)[0]
    wext = np.ascontiguousarray(w)
    wout = np.ascontiguousarray(np.asarray(w_out, np.float32)[0])
    rep = lambda v, n: np.ascontiguousarray(np.broadcast_to(np.asarray(v, np.float32).reshape(1, n), (128, n)))
    gbc = rep(norm_gain[0], D)
    gfin = rep(final_norm_gain, D)
    gsub = rep(subln_gain[0], 128)
    lamv = np.ascontiguousarray(np.concatenate(
        [rep(lambda_q1[0], 64), rep(lambda_k1[0], 64), rep(lambda_q2[0], 64), rep(lambda_k2[0], 64)], axis=1))
    cbf, cf, ropec, ropes = _consts()
    metar = np.ascontiguousarray(np.asarray(meta_tokens, np.float32)[::-1])
    if "nc" not in _NC_CACHE:
        _NC_CACHE["nc"] = build_nc(DEBUG)
    nc = _NC_CACHE["nc"]
    in_maps = []
    for b in range(B):
        in_maps.append({
            "xr": np.ascontiguousarray(x[b, ::-1, :]), "metar": metar, "wext": wext, "wout": wout,
            "gbc": gbc, "gfin": gfin, "gsub": gsub, "lamv": lamv, "ropec": ropec, "ropes": ropes,
            "cbf": cbf, "cf": cf,
        })
    res = run_bass_kernel_spmd(nc, in_maps, core_ids=list(range(B)))
    outs = [np.asarray(r["out"], np.float32)[::-1] for r in res.results]
    return np.ascontiguousarray(np.stack(outs, axis=0))
```

```python
import contextlib
import numpy as np
import ml_dtypes
import concourse.bass as bass
import concourse.mybir as mybir
from concourse.bass_utils import run_bass_kernel_spmd

F32 = mybir.dt.float32
BF16 = mybir.dt.bfloat16
AF = mybir.ActivationFunctionType
ALU = mybir.AluOpType
AX = mybir.AxisListType

SEQ = 2048
NMETA = 16
NTOK = SEQ + NMETA
D = 1024
NB = 17
EPS = 1e-6
LAMBDA_INIT = 0.8 - 0.6 * float(np.exp(-0.3 * 0))

DEBUG = False
STOP_AFTER = 99
DF_META = True
DFP_LEVEL = 9
DF_ODB = True
DF_DSK = 4
DF_ZP = 4
DF_LEVEL = 9
DMAT_MOD = 2
EPI_ACT_FROM_TILE = 16


def blk_rows(b):
    return 128 if b < 16 else 16


class Prog:
    ENG = ("pe", "act", "dve", "pool", "sp")

    def __init__(self, nc, stack):
        self.nc = nc
        self.stack = stack
        self.streams = {e: [] for e in self.ENG}
        self.sems = {}
        self.semval = {}
        for e in self.ENG:
            self.sems[e] = stack.enter_context(nc.semaphore("s_" + e))
            self.semval[e] = 0
        self.known = {e: {} for e in self.ENG}
        self.snap = {}
        self.res = {}
        self.ninstr = 0
        self.enabled = True
        self.nbar = 0

    def dma_sem(self, key):
        if key not in self.sems:
            self.sems[key] = self.stack.enter_context(self.nc.semaphore("d_" + key))
            self.semval[key] = 0
        return key

    def _wait(self, eng, ev):
        key, val = ev
        if self.known[eng].get(key, 0) >= val:
            return
        self.known[eng][key] = val
        inherited = self.snap.get((key, val))
        if inherited:
            kn = self.known[eng]
            for k2, v2 in inherited.items():
                if k2 != eng and kn.get(k2, 0) < v2:
                    kn[k2] = v2
        sem = self.sems[key]
        self.streams[eng].append(lambda e, sem=sem, val=val: e.wait_ge(sem, val))

    def _deps(self, eng, reads, writes):
        evs = []
        for r in reads:
            st = self.res.get(r)
            if st and st["w"]:
                evs.append((st["w"], "raw"))
        for w in writes:
            st = self.res.get(w)
            if st:
                if st["w"]:
                    evs.append((st["w"], "waw"))
                for k, v in st["r"].items():
                    evs.append(((k, v), "war"))
        for ev, kind in evs:
            key = ev[0]
            if key == eng:
                if eng in ("pe", "sp") or kind == "war":
                    continue
            self._wait(eng, ev)

    def _commit(self, ev, reads, writes):
        for r in reads:
            st = self.res.setdefault(r, {"w": None, "r": {}})
            k, v = ev
            if st["r"].get(k, 0) < v:
                st["r"][k] = v
        for w in writes:
            self.res[w] = {"w": ev, "r": {}}

    def op(self, eng, fn, reads=(), writes=()):
        if not self.enabled:
            return
        self._deps(eng, reads, writes)
        self.semval[eng] += 1
        ev = (eng, self.semval[eng])
        self.snap[ev] = dict(self.known[eng])
        sem = self.sems[eng]
        self.streams[eng].append(lambda e, fn=fn, sem=sem: fn(e).then_inc(sem, 1))
        self._commit(ev, reads, writes)
        self.ninstr += 1

    def dma(self, eng, slot, out, in_, reads=(), writes=(), transpose=False):
        if not self.enabled:
            return
        key = self.dma_sem(slot)
        self._deps(eng, reads, writes)
        self.semval[key] += 16
        ev = (key, self.semval[key])
        sem = self.sems[key]
        if transpose:
            self.streams[eng].append(
                lambda e, out=out, in_=in_, sem=sem: e.dma_start_transpose(out=out, in_=in_).then_inc(sem, 16))
        else:
            self.streams[eng].append(
                lambda e, out=out, in_=in_, sem=sem: e.dma_start(out=out, in_=in_).then_inc(sem, 16))
        self._commit(ev, reads, writes)
        self.ninstr += 1

    def barrier(self):
        if not self.enabled:
            return
        self.nbar += 1
        if self.nbar > STOP_AFTER:
            self.enabled = False
        for e in self.ENG:
            for k, v in self.semval.items():
                if k != e and v > 0:
                    self._wait(e, (k, v))
        self.res = {}


def build_nc(dbg=False):
    nc = bass.Bass("TRN2", target_bir_lowering=False)

    def din(name, shape, dt=F32):
        return nc.dram_tensor(name, list(shape), dt, kind="ExternalInput").ap()

    xr = din("xr", [SEQ, D])
    metar = din("metar", [NMETA, D])
    wext = din("wext", [D, 4096])
    wout = din("wout", [D, D])
    gbc_d = din("gbc", [128, D])
    gfin_d = din("gfin", [128, D])
    gsub_d = din("gsub", [128, 128])
    lamv_d = din("lamv", [128, 256])
    ropec_d = din("ropec", [128, NTOK])
    ropes_d = din("ropes", [128, NTOK])
    cbf_d = din("cbf", [128, 640], BF16)
    cf_d = din("cf", [128, 2192])
    out_d = nc.dram_tensor("out", [SEQ, D], F32, kind="ExternalOutput").ap()
    if dbg:
        dbg_d = nc.dram_tensor("dbg", [128, 8 * NTOK], F32, kind="ExternalOutput").ap()

    with contextlib.ExitStack() as st:
        def sb(name, shape, dt):
            return st.enter_context(nc.sbuf_tensor(name, list(shape), dt))

        uT = sb("uT", [128, 8, NTOK], BF16)
        A1 = sb("A1", [128, 4, NTOK], BF16)
        A2 = sb("A2", [128, 4, NTOK], BF16)
        A3 = sb("A3", [128, NB, 520], BF16)
        A4 = sb("A4", [128, 16, 512], BF16)
        A5flat = sb("A5", [128, NB * 512], BF16)
        A5 = A5flat[:, :].rearrange("p (b n) -> p b n", n=512)
        A5f = A5flat.bitcast(F32)
        mixT = sb("mixT", [128, 8, SEQ], BF16)
        scr = sb("scr", [128, 16384], BF16)
        scrf = scr.bitcast(F32)
        tmpf = sb("tmpf", [128, 2048], F32)
        gbc = sb("gbcs", [128, D], F32)
        cbf = sb("cbfs", [128, 640], BF16)
        cf = sb("cfs", [128, 2192], F32)
        gsub = sb("gsubs", [128, 128], F32)
        lamv = sb("lamvs", [128, 256], F32)
        small = sb("small", [128, 256], F32)
        ps = [st.enter_context(nc.psum_tensor(f"ps{i}", [128, 512], F32)) for i in range(8)]
        psb = [p.bitcast(BF16) for p in ps]

        P = Prog(nc, st)

        ident_bf = cbf[:, 0:128]
        negM2T = cbf[:, 128:256]
        Dmat = cbf[:, 256:384]
        dlast = cbf[0:1, 384:512]
        permT_bf = cbf[:, 512:640]
        permT = cf[:, 0:128]
        M1Z = cf[:, 128:2192]

        P.dma("sp", "c_gbc", gbc[:, :], gbc_d, writes=["gbc"])
        P.dma("sp", "c_cbf", cbf[:, :], cbf_d, writes=["cbf"])
        P.dma("sp", "c_cf", cf[:, :], cf_d, writes=["cf"])
        P.dma("sp", "c_gsub", gsub[:, :], gsub_d, writes=["gsub"])
        P.dma("sp", "c_lamv", lamv[:, :], lamv_d, writes=["lamv"])

        P.op("pool", lambda e: e.memset(small[:, 65:66], EPS), writes=["epsc"])
        epsc = small[:, 65:66]
        P.op("pool", lambda e: e.memset(small[:, 66:68], -0.5), writes=["neghalf"])
        neghalf = small[:, 66:67]
        neghalf2 = small[:, 66:68]
        P.op("dve", lambda e: e.tensor_tensor(out=tmpf[:, 0:64], in0=lamv[:, 0:64], in1=lamv[:, 64:128], op=ALU.mult),
             reads=["lamv"], writes=["lt0"])
        P.op("dve", lambda e: e.reduce_sum(out=small[:, 60:61], in_=tmpf[:, 0:64], axis=AX.X),
             reads=["lt0"], writes=["s1"])
        P.op("dve", lambda e: e.tensor_tensor(out=tmpf[:, 64:128], in0=lamv[:, 128:192], in1=lamv[:, 192:256], op=ALU.mult),
             reads=["lamv"], writes=["lt1"])
        P.op("dve", lambda e: e.reduce_sum(out=small[:, 61:62], in_=tmpf[:, 64:128], axis=AX.X),
             reads=["lt1"], writes=["s2"])
        P.op("act", lambda e: e.activation(out=small[:, 62:64], in_=small[:, 60:62], func=AF.Exp),
             reads=["s1", "s2"], writes=["e12"])
        P.op("dve", lambda e: e.tensor_tensor(out=small[:, 64:65], in0=small[:, 63:64], in1=small[:, 62:63], op=ALU.subtract),
             reads=["e12"], writes=["nl0"])
        P.op("dve", lambda e: e.tensor_scalar(out=small[:, 64:65], in0=small[:, 64:65], scalar1=-LAMBDA_INIT, scalar2=None, op0=ALU.add),
             reads=["nl0"], writes=["neglam"])
        neglam = small[:, 64:65]
        P.op("dve", lambda e: e.tensor_scalar(out=gsub[:, :], in0=gsub[:, :], scalar1=1.0 - LAMBDA_INIT, scalar2=None, op0=ALU.mult),
             reads=["gsub"], writes=["gsub"])

        wbuf = [scr[:, 8192 + s * 4096: 8192 + (s + 1) * 4096].rearrange("p (c n) -> p c n", n=512) for s in range(2)] + \
               [scr[:, s * 4096:(s + 1) * 4096].rearrange("p (c n) -> p c n", n=512) for s in range(2)]

        def load_w(slot, g):
            src = wext[:, g * 512:(g + 1) * 512].rearrange("(c p) n -> p c n", p=128)
            P.dma("pool", f"w{slot}", wbuf[slot], src, writes=[f"w{slot}"])

        load_w(0, 0)
        load_w(1, 1)

        xs = [A5f[:, s * 1024:(s + 1) * 1024] for s in range(2)]
        ub = [A5flat[:, 4096 + s * 1024: 4096 + (s + 1) * 1024] for s in range(2)]
        sqj = A5flat[:, 6144:7168]
        def p0_front(b):
            rows = blk_rows(b)
            s = b % 2
            src = xr[b * 128:(b + 1) * 128, :] if b < 16 else metar
            P.dma("sp", f"xs{s}", xs[s][:rows, :], src, writes=[f"xs{s}"])
            P.op("act", lambda e: e.activation(
                out=sqj[:rows, :], in_=xs[s][:rows, :], func=AF.Square, accum_out=small[:rows, b:b + 1]),
                reads=[f"xs{s}"], writes=["sqj", f"ss{b}"])
            P.op("dve", lambda e: e.tensor_scalar(
                out=small[:rows, 17 + b:18 + b], in0=small[:rows, b:b + 1], scalar1=1.0 / D, scalar2=EPS,
                op0=ALU.mult, op1=ALU.add),
                reads=[f"ss{b}"], writes=[f"sr{b}"])
            P.op("pool", lambda e: e.tensor_tensor(
                out=small[:rows, 34 + b:35 + b], in0=small[:rows, 17 + b:18 + b], in1=neghalf[:rows, :], op=ALU.pow),
                reads=[f"sr{b}", "neghalf"], writes=[f"rstd{b}"])
            P.op("dve", lambda e: e.scalar_tensor_tensor(
                out=ub[s][:rows, :], in0=xs[s][:rows, :], scalar=small[:rows, 34 + b:35 + b], in1=gbc[:rows, :],
                op0=ALU.mult, op1=ALU.mult),
                reads=[f"xs{s}", f"rstd{b}", "gbc"], writes=[f"ub{s}"])

        def p0_tr(b):
            rows = blk_rows(b)
            s = b % 2
            pi = 4 + b % 2
            for c in range(8):
                P.op("pe", lambda e, c=c: e.transpose(
                    out=psb[pi][:, c * 128:c * 128 + rows], in_=ub[s][:rows, c * 128:(c + 1) * 128],
                    identity=ident_bf[:rows, :rows]),
                    reads=[f"ub{s}", "cbf"], writes=[f"ps{pi}"])

        def p0_back(b):
            rows = blk_rows(b)
            pi = 4 + b % 2
            srcv = psb[pi][:, :].rearrange("p (c t) -> p c t", t=128)[:, :, :rows]
            dstv = uT[:, :, b * 128:b * 128 + rows]
            if b % 2 == 0:
                P.op("act", lambda e: e.activation(out=dstv, in_=srcv, func=AF.Copy),
                     reads=[f"ps{pi}"], writes=[f"uT{b}"])
            else:
                P.op("dve", lambda e: e.tensor_copy(out=dstv, in_=srcv),
                     reads=[f"ps{pi}"], writes=[f"uT{b}"])


        TR = [(0, 512), (512, 512), (1024, 512), (1536, 512), (2048, 16)]
        pcount = [0]
        pspool = [[0, 1, 2, 3, 6, 7]]

        def nextps(pool):
            i = pool[pcount[0] % len(pool)]
            pcount[0] += 1
            return i

        def feat_unit(slot, evac, cc, t0, n):
            pi = nextps(pspool[0])
            ub_ = [f"uT{b}" for b in range(t0 // 128, (t0 + n + 127) // 128)]
            for dc in range(8):
                P.op("pe", lambda e, dc=dc: e.matmul(
                    out=ps[pi][:, :n], lhsT=wbuf[slot][:, dc, cc * 128:(cc + 1) * 128],
                    rhs=uT[:, dc, t0:t0 + n], start=(dc == 0), stop=(dc == 7)),
                    reads=[f"w{slot}"] + ub_, writes=[f"ps{pi}"])
            evac(cc, t0, n, pi)

        def tok_unit(slot, evac, b):
            rows = blk_rows(b)
            pi = nextps(pspool[0])
            for dc in range(8):
                P.op("pe", lambda e, dc=dc: e.matmul(
                    out=ps[pi][:rows, :512], lhsT=uT[:, dc, b * 128:b * 128 + rows],
                    rhs=wbuf[slot][:, dc, :], start=(dc == 0), stop=(dc == 7)),
                    reads=[f"w{slot}", f"uT{b}"], writes=[f"ps{pi}"])
            evac(b, rows, pi)

        def proj_feat(slot, evac, tr=None):
            for cc in range(4):
                for (t0, n) in (tr or TR):
                    feat_unit(slot, evac, cc, t0, n)

        def proj_tok(slot, evac, nblk):
            for b in range(nblk):
                tok_unit(slot, evac, b)

        flip = [0]

        def evac_copy_to(dst3):
            def f(cc, t0, n, pi):
                flip[0] ^= 1
                if flip[0]:
                    P.op("act", lambda e: e.activation(out=dst3[:, cc, t0:t0 + n], in_=ps[pi][:, :n], func=AF.Copy),
                         reads=[f"ps{pi}"], writes=[])
                else:
                    P.op("dve", lambda e: e.tensor_copy(out=dst3[:, cc, t0:t0 + n], in_=ps[pi][:, :n]),
                         reads=[f"ps{pi}"], writes=[])
            return f

        QP = [A1[:, h, 0:SEQ] for h in range(4)] + [mixT[:, 4 + h, :] for h in range(4)]
        for h in range(8):
            zr = slice(64, 128) if h % 2 == 0 else slice(0, 64)
            P.op("pool", lambda e, h=h, zr=zr: e.memset(QP[h][zr, :], 0.0), writes=[f"qpz{h}"])

        def evac_q(cc, t0, n, pi):
            if t0 >= SEQ:
                return
            P.op("act", lambda e: e.activation(out=QP[2 * cc][0:64, t0:t0 + n], in_=ps[pi][0:64, :n], func=AF.Copy),
                 reads=[f"ps{pi}"], writes=[])
            P.op("dve", lambda e: e.tensor_copy(out=QP[2 * cc + 1][64:128, t0:t0 + n], in_=ps[pi][64:128, :n]),
                 reads=[f"ps{pi}"], writes=[])
        def evac_v(b, rows, pi):
            P.op("act", lambda e: e.activation(out=A3[:rows, b, 0:512], in_=ps[pi][:rows, :512], func=AF.Copy),
                 reads=[f"ps{pi}"], writes=[f"v{b}"])

        def evac_g(b, rows, pi):
            P.op("act", lambda e: e.activation(out=A4[:rows, b, :], in_=ps[pi][:rows, :512], func=AF.Silu),
                 reads=[f"ps{pi}"], writes=[])

        load_w(2, 2)
        load_w(3, 3)
        evac_k = evac_copy_to(A2)
        fifo = []
        p0_front(0)
        p0_tr(0)
        for b in range(1, NB + 1):
            if b < NB:
                p0_front(b)
            if b >= 1:
                p0_back(b - 1)
                bb = b - 1
                if bb % 4 == 3 or bb == 16:
                    r = bb // 4
                    t0, n = TR[r]
                    for cc in range(4):
                        if r < 4:
                            fifo.append(lambda cc=cc, t0=t0, n=n: feat_unit(0, evac_q, cc, t0, n))
                        fifo.append(lambda cc=cc, t0=t0, n=n: feat_unit(1, evac_k, cc, t0, n))
                    for b2 in range(4 * r, min(4 * r + 4, NB)):
                        fifo.append(lambda b2=b2: tok_unit(2, evac_v, b2))
                        if b2 < 16:
                            fifo.append(lambda b2=b2: tok_unit(3, evac_g, b2))
            for _ in range(4):
                if fifo:
                    fifo.pop(0)()
            if b < NB:
                p0_tr(b)
        while fifo:
            fifo.pop(0)()
        P.barrier()
        P.op("pool", lambda e: e.memset(A5[:, 16, :], 0.0), writes=["dvmeta"])
        for b in range(NB):
            rows = blk_rows(b)
            pi = nextps([0, 1, 2, 3])
            last = (b == NB - 1)
            P.op("pe", lambda e, b=b, rows=rows, pi=pi, last=last: e.matmul(
                out=ps[pi][:rows, :512], lhsT=Dmat[:rows, :rows], rhs=A3[:rows, b, 0:512], start=True, stop=last),
                reads=[f"v{b}", "cbf"], writes=[f"ps{pi}"])
            if not last:
                P.op("pe", lambda e, b=b, pi=pi: e.matmul(
                    out=ps[pi][:128, :512], lhsT=dlast, rhs=A3[0:1, b + 1, 0:512], start=False, stop=True),
                    reads=[f"v{b + 1}"], writes=[f"ps{pi}"])
            P.op("dve", lambda e, b=b, rows=rows, pi=pi: e.tensor_copy(out=A5[:rows, b, :], in_=ps[pi][:rows, :512]),
                 reads=[f"ps{pi}"] + (["dvmeta"] if b == 16 else []), writes=[f"dv{b}"] + (["dvmeta"] if b == 16 else []))

        keepL = [scrf[:, i * 2064:(i + 1) * 2064] for i in range(2)]
        Pb = [scr[:, 8256 + i * 2176: 8256 + (i + 1) * 2176] for i in range(3)]
        tmpb = tmpf.bitcast(BF16)
        if DMAT_MOD:
            PTDs = [tmpb[:, 512 + i * 1152: 512 + (i + 1) * 1152].rearrange("p (b t) -> p b t", t=128) for i in range(2)]
            PTb = [scr[:, 14784 + i * 512: 14784 + (i + 1) * 512] for i in range(3)] + \
                  [tmpb[:, 2816 + i * 512: 2816 + (i + 1) * 512] for i in range(2)]
        else:
            PTb = [scr[:, 14784 + i * 512: 14784 + (i + 1) * 512] for i in range(3)] + \
                  [tmpb[:, 512 + i * 512: 512 + (i + 1) * 512] for i in range(7)]
        NPT = len(PTb)

        def is_dma_chunk(k, c):
            return bool(DMAT_MOD) and c >= 2
        for i in range(3):
            P.op("pool", lambda e, i=i: e.memset(Pb[i][:, :], 0.0), writes=[f"Pb{i}"])
        mixtok = [tmpf.bitcast(BF16)[:, 0:512]]
        cnt = {"z": 0, "k": 0, "p": 0, "t": 0, "pt": 0, "ev": 0}

        def rot(name, n):
            i = cnt[name] % n
            cnt[name] += 1
            return i

        def sb_chunks(j):
            blocks = list(range(j, 16))
            chunks = []
            while blocks:
                cb = blocks[:4]
                blocks = blocks[4:]
                chunks.append([(kb, 128) for kb in cb])
            if len(chunks[-1]) < 4:
                chunks[-1].append((16, 16))
            else:
                chunks.append([(16, 16)])
            return chunks

        items = [(j, h) for j in range(16) for h in range(8)]
        NI = len(items)
        ptslot = {}

        def st_z(k, c):
            j, h = items[k]
            ch = sb_chunks(j)[c]
            cc, po = h // 2, (h % 2) * 64
            n = sum(r for _, r in ch)
            t0 = ch[0][0] * 128
            off = t0 - 128 * j
            zi = rot("z", 2)
            ks = k % 2
            P.op("pe", lambda e: e.matmul(
                out=ps[zi][:, :n], lhsT=QP[h][:, j * 128:(j + 1) * 128],
                rhs=A2[:, cc, t0:t0 + n], start=True, stop=True),
                reads=[], writes=[f"ps{zi}"])
            P.op("act", lambda e: e.activation(
                out=keepL[ks][:, off:off + n], in_=ps[zi][:, :n], func=AF.Sigmoid, scale=-0.125),
                reads=[f"ps{zi}"], writes=[f"keep{ks}"])

        def st_scan(k):
            j, h = items[k]
            ntot = NTOK - 128 * j
            ks, pslot = k % 2, k % 3
            P.op("dve", lambda e: e.tensor_tensor_scan(
                out=Pb[pslot][:, :ntot], data0=keepL[ks][:, :ntot], data1=M1Z[:, :ntot], initial=1.0,
                op0=ALU.mult, op1=ALU.max),
                reads=[f"keep{ks}", "cf"], writes=[f"Pb{pslot}"])

        def st_t(k, c):
            j, h = items[k]
            ch = sb_chunks(j)[c]
            pslot = k % 3
            if is_dma_chunk(k, c):
                return
            ti = 2 + rot("t", 2)
            off = ch[0][0] * 128 - 128 * j
            col = off
            for bi, (kb, r) in enumerate(ch):
                P.op("pe", lambda e, bi=bi, col=col: e.transpose(
                    out=psb[ti][:, bi * 128:(bi + 1) * 128], in_=Pb[pslot][:, col:col + 128],
                    identity=ident_bf),
                    reads=[f"Pb{pslot}", "cbf"], writes=[f"ps{ti}"])
                col += r
            pti = rot("pt", NPT)
            ptslot[(k, c)] = pti
            nb_ = len(ch)
            if ((not DMAT_MOD) and rot("ev", 4) == 3) or (DMAT_MOD and j >= 11 and c == 1):
                P.op("dve", lambda e: e.tensor_copy(out=PTb[pti][:, :nb_ * 128], in_=psb[ti][:, :nb_ * 128]),
                     reads=[f"ps{ti}"], writes=[f"PT{pti}"])
            else:
                P.op("act", lambda e: e.activation(
                    out=PTb[pti][:, :nb_ * 128], in_=psb[ti][:, :nb_ * 128], func=AF.Copy),
                    reads=[f"ps{ti}"], writes=[f"PT{pti}"])

        def st_dma(k):
            j, h = items[k]
            if not DMAT_MOD or nch(k) < 3:
                return
            pslot = k % 3
            nfar = 17 - j - 8
            P.dma("sp", "dmat", PTDs[k % 2][:, 0:nfar, :], Pb[pslot][:, 1024:1024 + nfar * 128],
                  reads=[f"Pb{pslot}"], writes=[f"PTD{k % 2}", "dmat_serial"], transpose=True)

        def st_av(k, c):
            j, h = items[k]
            ch = sb_chunks(j)[c]
            oi = 6 + (j % 2)
            hc = slice(h * 64, (h + 1) * 64)
            if is_dma_chunk(k, c):
                for bi, (kb, r) in enumerate(ch):
                    P.op("pe", lambda e, bi=bi, kb=kb: e.matmul(
                        out=ps[oi][:, hc], lhsT=PTDs[k % 2][:, kb - j - 8, :],
                        rhs=A5[:, kb, hc], start=(c == 0 and bi == 0), stop=False),
                        reads=[f"PTD{k % 2}", f"dv{kb}"], writes=[f"ps{oi}"])
                return
            pti = ptslot[(k, c)]
            for bi, (kb, r) in enumerate(ch):
                P.op("pe", lambda e, bi=bi, kb=kb: e.matmul(
                    out=ps[oi][:, hc], lhsT=PTb[pti][:, bi * 128:(bi + 1) * 128],
                    rhs=A5[:, kb, hc], start=(c == 0 and bi == 0), stop=False),
                    reads=[f"PT{pti}", f"dv{kb}"], writes=[f"ps{oi}"])

        def st_tail(k):
            j, h = items[k]
            oi = 6 + (j % 2)
            hc = slice(h * 64, (h + 1) * 64)
            P.op("pe", lambda e: e.matmul(
                out=ps[oi][:, hc], lhsT=negM2T, rhs=A5[:, j, hc], start=False, stop=False),
                reads=["cbf", f"dv{j}"], writes=[f"ps{oi}"])
            P.op("pe", lambda e: e.matmul(
                out=ps[oi][:, hc], lhsT=ident_bf, rhs=A3[:, j, hc], start=False, stop=True),
                reads=["cbf"], writes=[f"ps{oi}"])
            if h == 7:
                P.op("dve", lambda e: e.tensor_tensor(
                    out=mixtok[0][:, :], in0=ps[oi][:, :512], in1=A4[:, j, :], op=ALU.mult),
                    reads=[f"ps{oi}"], writes=["mixtok"])
                ti = 4 + (j % 2)
                for c in range(4):
                    P.op("pe", lambda e, c=c: e.transpose(
                        out=psb[ti][:, c * 128:(c + 1) * 128], in_=mixtok[0][:, c * 128:(c + 1) * 128], identity=ident_bf),
                        reads=["mixtok", "cbf"], writes=[f"ps{ti}"])
                P.op("act", lambda e: e.activation(
                    out=mixT[:, 0:4, j * 128:(j + 1) * 128],
                    in_=psb[ti][:, 0:512].rearrange("p (c t) -> p c t", t=128), func=AF.Copy),
                    reads=[f"ps{ti}"], writes=[])

        def nch(k):
            return len(sb_chunks(items[k][0])) if 0 <= k < NI else 0

        SK = 2
        for k in range(NI + SK + 1):
            if 0 <= k - 1 < NI:
                st_scan(k - 1)
            na, nt_, nv_ = nch(k), nch(k - SK), nch(k - SK - 1)
            for c in range(na):
                st_z(k, c)
            for c in range(max(nt_, nv_)):
                if c < nt_:
                    st_t(k - SK, c)
                if c < nv_:
                    st_av(k - SK - 1, c)
            if 0 <= k - SK - 1 < NI:
                st_tail(k - SK - 1)
            if 0 <= k - 1 < NI:
                st_dma(k - 1)
            if k == NI:
                for slot, g in ((2, 6), (3, 7)):
                    src = wext[:, g * 512:(g + 1) * 512].rearrange("(c p) n -> p c n", p=128)
                    P.dma("pool", f"w{slot}", wbuf[slot], src, writes=[f"w{slot}", "keep0", "keep1", "dmat_serial"])
        P.barrier()
        load_w(0, 4)
        load_w(1, 5)

        pspool[0] = [0, 1, 2, 3, 4, 5, 6, 7]
        ropec = A5f[:, 0:NTOK]
        ropes = A5f[:, NTOK:2 * NTOK]
        P.dma("sp", "c_rc", ropec, ropec_d, writes=["ropec"])
        P.dma("sp", "c_rs", ropes, ropes_d, writes=["ropes"])
        vaug4 = A3[:, :, :].rearrange("p b (h e) -> p b h e", e=130)
        P.op("pool", lambda e: e.memset(vaug4[:, :, :, 128:129], 1.0), writes=["vones"])
        P.op("pool", lambda e: e.memset(vaug4[:, :, :, 129:130], 0.0), writes=["vzero"])

        QPD = [A1[:, h, 0:SEQ] for h in range(4)] + [mixT[:, 4 + h, :] for h in range(4)]
        for h in range(8):
            zr = slice(64, 128) if h % 2 == 0 else slice(0, 64)
            P.op("pool", lambda e, h=h, zr=zr: e.memset(QPD[h][zr, :], 0.0), writes=[f"qpdz{h}"])

        qhb = [scr[:, i * 512:(i + 1) * 512] for i in range(4)]
        qlb = [scr[:, 2048 + i * 512: 2048 + (i + 1) * 512] for i in range(4)]

        def proj_rope(dst3, slot, padded=False):
            units = [(cc, t0, n) for cc in range(4) for (t0, n) in (TR[:4] if padded else TR)]
            pend = None

            def finish(u):
                cc, t0, n, pa, qi = u
                pb = nextps([0, 1, 2, 3, 4, 5, 6, 7])
                P.op("pe", lambda e: e.matmul(out=ps[pb][:, :n], lhsT=permT_bf, rhs=qhb[qi][:, :n], start=True, stop=False),
                     reads=[f"qs{qi}", "cbf"], writes=[f"ps{pb}"])
                P.op("pe", lambda e: e.matmul(out=ps[pb][:, :n], lhsT=permT_bf, rhs=qlb[qi][:, :n], start=False, stop=True),
                     reads=[f"ql{qi}", "cbf"], writes=[f"ps{pb}"])
                ta = rot("k", 2)
                t1 = tmpf[:, ta * 1024: ta * 1024 + n]
                t2 = tmpf[:, ta * 1024 + 512: ta * 1024 + 512 + n]
                P.op("dve", lambda e: e.tensor_tensor(out=t1, in0=ps[pa][:, :n], in1=ropec[:, t0:t0 + n], op=ALU.mult),
                     reads=[f"ps{pa}", "ropec"], writes=[f"t1{ta}"])
                P.op("dve", lambda e: e.tensor_tensor(out=t2, in0=ps[pb][:, :n], in1=ropes[:, t0:t0 + n], op=ALU.mult),
                     reads=[f"ps{pb}", "ropes"], writes=[f"t2{ta}"])
                if padded:
                    P.op("pool", lambda e: e.tensor_tensor(
                        out=QPD[2 * cc][0:64, t0:t0 + n], in0=t1[0:64, :], in1=t2[0:64, :], op=ALU.add),
                        reads=[f"t1{ta}", f"t2{ta}", f"qpdz{2 * cc}"], writes=[f"qpa{ta}"])
                    P.op("pool", lambda e: e.tensor_tensor(
                        out=QPD[2 * cc + 1][64:128, t0:t0 + n], in0=t1[64:128, :], in1=t2[64:128, :], op=ALU.add),
                        reads=[f"t1{ta}", f"t2{ta}", f"qpdz{2 * cc + 1}"], writes=[f"qpb{ta}"])
                else:
                    P.op("pool", lambda e: e.tensor_tensor(
                        out=dst3[:, cc, t0:t0 + n], in0=t1, in1=t2, op=ALU.add),
                        reads=[f"t1{ta}", f"t2{ta}"], writes=[f"qpa{ta}"])

            for (cc, t0, n) in units:
                pa = nextps([0, 1, 2, 3, 4, 5, 6, 7])
                for dc in range(8):
                    P.op("pe", lambda e, dc=dc, pa=pa, cc=cc, t0=t0, n=n: e.matmul(
                        out=ps[pa][:, :n], lhsT=wbuf[slot][:, dc, cc * 128:(cc + 1) * 128],
                        rhs=uT[:, dc, t0:t0 + n], start=(dc == 0), stop=(dc == 7)),
                        reads=[f"w{slot}"], writes=[f"ps{pa}"])
                qi = rot("p", 4)
                P.op("act", lambda e, pa=pa, qi=qi, n=n: e.activation(out=qhb[qi][:, :n], in_=ps[pa][:, :n], func=AF.Copy),
                     reads=[f"ps{pa}"], writes=[f"qs{qi}"])
                P.op("dve", lambda e, pa=pa, qi=qi, n=n: e.tensor_tensor(
                    out=qlb[qi][:, :n], in0=ps[pa][:, :n], in1=qhb[qi][:, :n], op=ALU.subtract),
                    reads=[f"ps{pa}", f"qs{qi}"], writes=[f"ql{qi}"])
                if pend is not None:
                    finish(pend)
                pend = (cc, t0, n, pa, qi)
            finish(pend)

        def evac_vd(b, rows, pi):
            P.op("act", lambda e: e.activation(
                out=vaug4[:rows, b, :, 0:128], in_=ps[pi][:rows, :512].rearrange("p (h e) -> p h e", e=128), func=AF.Copy),
                reads=[f"ps{pi}"], writes=[])
        if DFP_LEVEL >= 1:
            proj_tok(2, evac_vd, NB)
        if DFP_LEVEL >= 2:
            proj_tok(3, evac_g, 16)
        if DFP_LEVEL >= 3:
            proj_rope(None, 0, padded=True)
        if DFP_LEVEL >= 4:
            proj_rope(A2, 1)
        P.barrier()

        wo = scr[:, 8192:16384].rearrange("p (c n) -> p c n", n=1024)
        P.dma("pool", "w0", wo, wout.rearrange("(c p) n -> p c n", p=128), writes=["wo"])
        gfin = A5f[:, 0:1024]
        P.dma("sp", "c_rc", gfin, gfin_d, writes=["gfin"])
        ETb = [scr[:, i * 512:(i + 1) * 512] for i in range(8)]
        of32 = [tmpf[:, i * 128:(i + 1) * 128] for i in range(2)]
        yf32 = [tmpf[:, 256 + i * 128: 256 + (i + 1) * 128] for i in range(2)]
        sqj2 = tmpf[:, 0:128]
        mixtok2s = [scr[:, 4096:4608], scr[:, 4608:5120]]
        sc = {"i": 0}
        dsteps = []
        for j in range(16):
            for hp in range(2):
                st_l = [[kb] for kb in range(j, NB)]
                for si, kbs in enumerate(st_l):
                    dsteps.append({"j": j, "hp": hp, "kbs": kbs, "last": si == len(st_l) - 1,
                                   "par": ((2 * j + hp) % 2) if DF_ODB else 1})

        def d_zexp(stp):
            j, hp, kbs = stp["j"], stp["hp"], stp["kbs"]
            rows = blk_rows(kbs[0])
            zb = [rot("z", DF_ZP) for _ in kbs]
            eb = [rot("k", 8) for _ in kbs]
            stp["eb"] = eb
            for bi, kb in enumerate(kbs):
                for p2 in range(2):
                    hc0 = 4 * hp + 2 * p2
                    cc = hc0 // 2
                    if hc0 < 4:
                        rhs2 = A1[:, hc0:hc0 + 2, j * 128:(j + 1) * 128]
                    else:
                        rhs2 = mixT[:, hc0:hc0 + 2, j * 128:(j + 1) * 128]
                    P.op("pe", lambda e, zi=zb[bi], p2=p2, cc=cc, kb=kb, rhs2=rhs2: e.matmul(
                        out=ps[zi][:rows, p2 * 256:(p2 + 1) * 256], lhsT=A2[:, cc, kb * 128:kb * 128 + rows],
                        rhs=rhs2, start=True, stop=True),
                        reads=[f"qpd{j}"], writes=[f"ps{zb[bi]}"])
                zi, ei = zb[bi], eb[bi]
                P.op("act", lambda e, zi=zi, ei=ei: e.activation(
                    out=ETb[ei][:rows, :], in_=ps[zi][:rows, :], func=AF.Exp, scale=0.125),
                    reads=[f"ps{zi}"], writes=[f"ET{ei}"])
                if kb == j:
                    rect = ETb[ei][0:64, :].rearrange("p (i t) -> p i t", t=128)[:, :, 64:128]
                    P.op("pool", lambda e, rect=rect: e.memset(rect, 0.0),
                         reads=[], writes=[f"ET{ei}"])

        def d_av(stp):
            j, hp, kbs, eb = stp["j"], stp["hp"], stp["kbs"], stp["eb"]
            rows = blk_rows(kbs[0])
            for bi, kb in enumerate(kbs):
                for i in range(4):
                    head = (4 * hp + i) // 2
                    ei = eb[bi]
                    col = i * 128
                    ob, oc = 4 + 2 * stp["par"] + i // 2, (i % 2) * 256
                    P.op("pe", lambda e, i=i, ei=ei, col=col, kb=kb, head=head, ob=ob, oc=oc: e.matmul(
                        out=ps[ob][:, oc:oc + 129], lhsT=ETb[ei][:rows, col:col + 128],
                        rhs=A3[:rows, kb, head * 130:head * 130 + 129],
                        start=(kb == j and i % 2 == 0), stop=(kb == NB - 1), skip_group_check=True),
                        reads=[f"ET{ei}", "vones"], writes=[f"ps{ob}"])
            if stp["last"]:
                d_epilogue(j, hp, stp["par"])

        def d_epilogue(j, hp, par):
            k = sc["i"] % 4
            sc["i"] += 1
            q = k % 2
            cb = 80 + k * 16
            banks = [4 + 2 * par, 5 + 2 * par]
            tb = 512 + q * 768
            t1b = [tmpf[:, tb + hl * 128: tb + (hl + 1) * 128] for hl in range(2)]
            ofb = [tmpf[:, tb + 256 + hl * 128: tb + 256 + (hl + 1) * 128] for hl in range(2)]
            yfb = [tmpf[:, tb + 512 + hl * 128: tb + 512 + (hl + 1) * 128] for hl in range(2)]
            for hl in range(2):
                pb_ = banks[hl]
                P.op("dve", lambda e, pb_=pb_, hl=hl: e.reciprocal(
                    out=small[:, cb + 2 * hl:cb + 2 * hl + 2], in_=ps[pb_][:, 128:512:256]),
                    reads=[f"ps{pb_}"], writes=[f"rz{k}_{hl}"])
            P.op("dve", lambda e: e.tensor_scalar(
                out=small[:, cb + 4:cb + 6], in0=small[:, cb + 1:cb + 4:2], scalar1=neglam, scalar2=None, op0=ALU.mult),
                reads=[f"rz{k}_0", f"rz{k}_1", "neglam"], writes=[f"c1{k}"])
            act_t1 = j >= EPI_ACT_FROM_TILE
            for hl in range(2):
                pb_ = banks[hl]
                if act_t1:
                    P.op("act", lambda e, pb_=pb_, hl=hl: e.activation(
                        out=t1b[hl], in_=ps[pb_][:, 256:384], func=AF.Copy, scale=small[:, cb + 4 + hl:cb + 5 + hl]),
                        reads=[f"ps{pb_}", f"c1{k}"], writes=[f"t1e{q}_{hl}"])
                else:
                    P.op("dve", lambda e, pb_=pb_, hl=hl: e.tensor_scalar(
                        out=t1b[hl], in0=ps[pb_][:, 256:384], scalar1=small[:, cb + 4 + hl:cb + 5 + hl], scalar2=None, op0=ALU.mult),
                        reads=[f"ps{pb_}", f"c1{k}"], writes=[f"t1e{q}_{hl}"])
            for hl in range(2):
                pb_ = banks[hl]
                P.op("dve", lambda e, pb_=pb_, hl=hl: e.scalar_tensor_tensor(
                    out=ofb[hl], in0=ps[pb_][:, 0:128], scalar=small[:, cb + 2 * hl:cb + 2 * hl + 1], in1=t1b[hl],
                    op0=ALU.mult, op1=ALU.add),
                    reads=[f"ps{pb_}", f"rz{k}_{hl}", f"t1e{q}_{hl}"], writes=[f"of{q}_{hl}"])
            for hl in range(2):
                P.op("dve", lambda e, hl=hl: e.scalar_tensor_tensor(
                    out=sqj2, in0=ofb[hl], scalar=1.0, in1=ofb[hl], op0=ALU.mult, op1=ALU.mult,
                    accum_out=small[:, cb + 6 + hl:cb + 7 + hl]),
                    reads=[f"of{q}_{hl}"], writes=["sqj2", f"ss2{k}_{hl}"])
            P.op("dve", lambda e: e.tensor_scalar(
                out=small[:, cb + 8:cb + 10], in0=small[:, cb + 6:cb + 8], scalar1=1.0 / 128, scalar2=EPS,
                op0=ALU.mult, op1=ALU.add),
                reads=[f"ss2{k}_0", f"ss2{k}_1"], writes=[f"sq2{k}"])
            P.op("pool", lambda e: e.tensor_tensor(
                out=small[:, cb + 10:cb + 12], in0=small[:, cb + 8:cb + 10], in1=neghalf2, op=ALU.pow),
                reads=[f"sq2{k}", "neghalf"], writes=[f"rs2{k}"])
            def part_b():
                for hl in range(2):
                    head = 2 * hp + hl
                    P.op("dve", lambda e, hl=hl: e.scalar_tensor_tensor(
                        out=yfb[hl], in0=ofb[hl], scalar=small[:, cb + 10 + hl:cb + 11 + hl], in1=gsub[:, :],
                        op0=ALU.mult, op1=ALU.mult),
                        reads=[f"of{q}_{hl}", f"rs2{k}", "gsub"], writes=[f"yf{q}_{hl}"])
                    P.op("pool", lambda e, hl=hl, head=head: e.tensor_tensor(
                        out=mixtok2s[j % 2][:, head * 128:(head + 1) * 128], in0=yfb[hl], in1=A4[:, j, head * 128:(head + 1) * 128], op=ALU.mult),
                        reads=[f"yf{q}_{hl}"], writes=[f"mt2_{j % 2}_{head}"])
                if hp == 1:
                    pending_mix.append([j, 3])
            while pend_b:
                pend_b.pop(0)()
            pend_b.append(part_b)

        pend_b = []

        def d_mix(j):
            ti = rot("z", DF_ZP)
            mt = mixtok2s[j % 2]
            for c in range(4):
                P.op("pe", lambda e, ti=ti, c=c: e.transpose(
                    out=psb[ti][:, c * 128:(c + 1) * 128], in_=mt[:, c * 128:(c + 1) * 128], identity=ident_bf),
                    reads=[f"mt2_{j % 2}_{c}", "cbf"], writes=[f"ps{ti}"])
            P.op("dve", lambda e, ti=ti, j=j: e.tensor_copy(
                out=mixT[:, 4:8, j * 128:(j + 1) * 128],
                in_=psb[ti][:, 0:512].rearrange("p (c t) -> p c t", t=128)),
                reads=[f"ps{ti}"], writes=[f"qpd{j}"])

        pending_mix = []
        DSK = DF_DSK
        for idx in range(len(dsteps) + DSK):
            if idx < len(dsteps):
                d_zexp(dsteps[idx])
            for pm in list(pending_mix):
                pm[1] -= 1
                if pm[1] <= 0:
                    pending_mix.remove(pm)
                    d_mix(pm[0])
            if idx >= DSK:
                d_av(dsteps[idx - DSK])
        while pend_b:
            pend_b.pop(0)()
        for pm in pending_mix:
            d_mix(pm[0])
        P.barrier()

        xs2 = [scrf[:, s * 1024:(s + 1) * 1024] for s in range(4)]
        yo = [tmpf[:, 0:1024], tmpf[:, 1024:2048]]
        sqj3 = A4[:, 0:2, :].rearrange("p a n -> p (a n)")

        def p5_front(j):
            s = j % 4
            P.dma("sp", f"xs{s}", xs2[s], xr[j * 128:(j + 1) * 128, :], writes=[f"xs2{s}"])
            pa, pb = 2 * (j % 4), 2 * (j % 4) + 1
            for (pi, half) in ((pa, 0), (pb, 1)):
                for c in range(8):
                    P.op("pe", lambda e, pi=pi, half=half, c=c: e.matmul(
                        out=ps[pi][:, :512], lhsT=mixT[:, c, j * 128:(j + 1) * 128],
                        rhs=wo[:, c, half * 512:(half + 1) * 512], start=(c == 0), stop=(c == 7)),
                        reads=["wo"], writes=[f"ps{pi}"])
            for (pi, half) in ((pa, 0), (pb, 1)):
                P.op("dve", lambda e, pi=pi, half=half: e.tensor_tensor(
                    out=xs2[s][:, half * 512:(half + 1) * 512], in0=ps[pi][:, :512],
                    in1=xs2[s][:, half * 512:(half + 1) * 512], op=ALU.add),
                    reads=[f"ps{pi}", f"xs2{s}"], writes=[f"xs2{s}"])
            cb = 200 + (j % 4) * 4
            P.op("act", lambda e: e.activation(
                out=sqj3, in_=xs2[s], func=AF.Square, accum_out=small[:, cb:cb + 1]),
                reads=[f"xs2{s}"], writes=["sqj3", f"ss3{j % 4}"])

        def p5_back(j):
            s = j % 4
            so = j % 2
            cb = 200 + (j % 4) * 4
            P.op("dve", lambda e: e.tensor_scalar(
                out=small[:, cb + 1:cb + 2], in0=small[:, cb:cb + 1], scalar1=1.0 / D, scalar2=EPS,
                op0=ALU.mult, op1=ALU.add),
                reads=[f"ss3{j % 4}"], writes=[f"sq3{j % 4}"])
            P.op("pool", lambda e: e.tensor_tensor(
                out=small[:, cb + 2:cb + 3], in0=small[:, cb + 1:cb + 2], in1=neghalf, op=ALU.pow),
                reads=[f"sq3{j % 4}", "neghalf"], writes=[f"rs3{j % 4}"])
            P.op("dve", lambda e: e.scalar_tensor_tensor(
                out=yo[so], in0=xs2[s], scalar=small[:, cb + 2:cb + 3], in1=gfin, op0=ALU.mult, op1=ALU.mult),
                reads=[f"xs2{s}", f"rs3{j % 4}", "gfin"], writes=[f"yo{so}"])
            P.dma("pool", f"o{so}", out_d[j * 128:(j + 1) * 128, :], yo[so], reads=[f"yo{so}"], writes=[f"out{j}"])

        for j in range(17):
            if j < 16:
                p5_front(j)
            if j >= 1:
                p5_back(j - 1)
        P.barrier()

        with nc.Block() as block:
            @block.tensor
            def _(e):
                for f in P.streams["pe"]:
                    f(e)

            @block.scalar
            def _(e):
                for f in P.streams["act"]:
                    f(e)

            @block.vector
            def _(e):
                for f in P.streams["dve"]:
                    f(e)

            @block.gpsimd
            def _(e):
                for f in P.streams["pool"]:
                    f(e)

            @block.sync
            def _(e):
                for f in P.streams["sp"]:
                    f(e)
    return nc


def _consts():
    bf = ml_dtypes.bfloat16
    k = np.arange(128)[:, None]
    m = np.arange(128)[None, :]
    ident = (k == m).astype(np.float32)
    negM2T = -(k < m).astype(np.float32)
    Dm = (k == m + 1).astype(np.float32) - (k == m).astype(np.float32)
    dlast = np.zeros((128, 128), np.float32)
    dlast[0, 127] = 1.0
    swp = (np.arange(128) // 64) * 64 + ((np.arange(128) % 64) + 32) % 64
    permT = np.zeros((128, 128), np.float32)
    permT[swp, np.arange(128)] = 1.0
    cbf = np.concatenate([ident, negM2T, Dm, dlast, permT], axis=1).astype(bf)
    M1 = (m <= k).astype(np.float32)
    cf = np.concatenate([permT, M1, np.zeros((128, 1936), np.float32)], axis=1).astype(np.float32)
    inv = (1.0 / (np.float32(10000.0) ** (np.arange(0, 64, 2, dtype=np.float32) / np.float32(64)))).astype(np.float32)
    pos = (NTOK - 1 - np.arange(NTOK)).astype(np.float32)
    ang = (pos[None, :] * inv[:, None]).astype(np.float32)
    cos = np.cos(ang).astype(np.float32)
    sin = np.sin(ang).astype(np.float32)
    p = np.arange(128)
    ropec = cos[p % 32, :]
    sign = np.where((p % 64) < 32, -1.0, 1.0).astype(np.float32)[:, None]
    ropes = (sin[p % 32, :] * sign).astype(np.float32)
    return cbf, cf, np.ascontiguousarray(ropec), np.ascontiguousarray(ropes)


_NC_CACHE = {}


def kernel(x, meta_tokens, norm_gain, w_in, w_out, lambda_q1, lambda_k1, lambda_q2, lambda_k2,
           subln_gain, final_norm_gain):
    x = np.asarray(x, np.float32)
    B = x.shape[0]
    w = np.asarray(w_in, np.float32)[0]
    wext = np.ascontiguousarray(w)
    wout = np.ascontiguousarray(np.asarray(w_out, np.float32)[0])
    rep = lambda v, n: np.ascontiguousarray(np.broadcast_to(np.asarray(v, np.float32).reshape(1, n), (128, n)))
    gbc = rep(norm_gain[0], D)
    gfin = rep(final_norm_gain, D)
    gsub = rep(subln_gain[0], 128)
    lamv = np.ascontiguousarray(np.concatenate(
        [rep(lambda_q1[0], 64), rep(lambda_k1[0], 64), rep(lambda_q2[0], 64), rep(lambda_k2[0], 64)], axis=1))
    cbf, cf, ropec, ropes = _consts()
    metar = np.ascontiguousarray(np.asarray(meta_tokens, np.float32)[::-1])
    if "nc" not in _NC_CACHE:
        _NC_CACHE["nc"] = build_nc(DEBUG)
    nc = _NC_CACHE["nc"]
    in_maps = []
    for b in range(B):
        in_maps.append({
            "xr": np.ascontiguousarray(x[b, ::-1, :]), "metar": metar, "wext": wext, "wout": wout,
            "gbc": gbc, "gfin": gfin, "gsub": gsub, "lamv": lamv, "ropec": ropec, "ropes": ropes,
            "cbf": cbf, "cf": cf,
        })
    res = run_bass_kernel_spmd(nc, in_maps, core_ids=list(range(B)))
    outs = [np.asarray(r["out"], np.float32)[::-1] for r in res.results]
    return np.ascontiguousarray(np.stack(outs, axis=0))
```

```python
import contextlib
import numpy as np
import ml_dtypes
import concourse.bass as bass
import concourse.mybir as mybir
from concourse.bass_utils import run_bass_kernel_spmd

F32 = mybir.dt.float32
BF16 = mybir.dt.bfloat16
AF = mybir.ActivationFunctionType
ALU = mybir.AluOpType
AX = mybir.AxisListType

SEQ = 2048
NMETA = 16
NTOK = SEQ + NMETA
D = 1024
NB = 17
EPS = 1e-6
LAMBDA_INIT = 0.8 - 0.6 * float(np.exp(-0.3 * 0))

DEBUG = False
STOP_AFTER = 99
DF_META = True
DFP_LEVEL = 9
DF_ODB = True
DF_DSK = 4
DF_ZP = 4
DF_LEVEL = 9
DMAT_MOD = 2
EPI_ACT_FROM_TILE = 16


def blk_rows(b):
    return 128 if b < 16 else 16


class Prog:
    ENG = ("pe", "act", "dve", "pool", "sp")

    def __init__(self, nc, stack):
        self.nc = nc
        self.stack = stack
        self.streams = {e: [] for e in self.ENG}
        self.sems = {}
        self.semval = {}
        for e in self.ENG:
            self.sems[e] = stack.enter_context(nc.semaphore("s_" + e))
            self.semval[e] = 0
        self.known = {e: {} for e in self.ENG}
        self.snap = {}
        self.res = {}
        self.ninstr = 0
        self.enabled = True
        self.nbar = 0

    def dma_sem(self, key):
        if key not in self.sems:
            self.sems[key] = self.stack.enter_context(self.nc.semaphore("d_" + key))
            self.semval[key] = 0
        return key

    def _wait(self, eng, ev):
        key, val = ev
        if self.known[eng].get(key, 0) >= val:
            return
        self.known[eng][key] = val
        inherited = self.snap.get((key, val))
        if inherited:
            kn = self.known[eng]
            for k2, v2 in inherited.items():
                if k2 != eng and kn.get(k2, 0) < v2:
                    kn[k2] = v2
        sem = self.sems[key]
        self.streams[eng].append(lambda e, sem=sem, val=val: e.wait_ge(sem, val))

    def _deps(self, eng, reads, writes):
        evs = []
        for r in reads:
            st = self.res.get(r)
            if st and st["w"]:
                evs.append((st["w"], "raw"))
        for w in writes:
            st = self.res.get(w)
            if st:
                if st["w"]:
                    evs.append((st["w"], "waw"))
                for k, v in st["r"].items():
                    evs.append(((k, v), "war"))
        for ev, kind in evs:
            key = ev[0]
            if key == eng:
                if eng in ("pe", "sp") or kind == "war":
                    continue
            self._wait(eng, ev)

    def _commit(self, ev, reads, writes):
        for r in reads:
            st = self.res.setdefault(r, {"w": None, "r": {}})
            k, v = ev
            if st["r"].get(k, 0) < v:
                st["r"][k] = v
        for w in writes:
            self.res[w] = {"w": ev, "r": {}}

    def op(self, eng, fn, reads=(), writes=()):
        if not self.enabled:
            return
        self._deps(eng, reads, writes)
        self.semval[eng] += 1
        ev = (eng, self.semval[eng])
        self.snap[ev] = dict(self.known[eng])
        sem = self.sems[eng]
        self.streams[eng].append(lambda e, fn=fn, sem=sem: fn(e).then_inc(sem, 1))
        self._commit(ev, reads, writes)
        self.ninstr += 1

    def dma(self, eng, slot, out, in_, reads=(), writes=(), transpose=False):
        if not self.enabled:
            return
        key = self.dma_sem(slot)
        self._deps(eng, reads, writes)
        self.semval[key] += 16
        ev = (key, self.semval[key])
        sem = self.sems[key]
        if transpose:
            self.streams[eng].append(
                lambda e, out=out, in_=in_, sem=sem: e.dma_start_transpose(out=out, in_=in_).then_inc(sem, 16))
        else:
            self.streams[eng].append(
                lambda e, out=out, in_=in_, sem=sem: e.dma_start(out=out, in_=in_).then_inc(sem, 16))
        self._commit(ev, reads, writes)
        self.ninstr += 1

    def barrier(self):
        if not self.enabled:
            return
        self.nbar += 1
        if self.nbar > STOP_AFTER:
            self.enabled = False
        for e in self.ENG:
            for k, v in self.semval.items():
                if k != e and v > 0:
                    self._wait(e, (k, v))
        self.res = {}


def build_nc(dbg=False):
    nc = bass.Bass("TRN2", target_bir_lowering=False)

    def din(name, shape, dt=F32):
        return nc.dram_tensor(name, list(shape), dt, kind="ExternalInput").ap()

    xr = din("xr", [SEQ, D])
    metar = din("metar", [NMETA, D])
    wext = din("wext", [D, 4096])
    wout = din("wout", [D, D])
    gbc_d = din("gbc", [128, D])
    gfin_d = din("gfin", [128, D])
    gsub_d = din("gsub", [128, 128])
    lamv_d = din("lamv", [128, 256])
    ropec_d = din("ropec", [128, NTOK])
    ropes_d = din("ropes", [128, NTOK])
    cbf_d = din("cbf", [128, 640], BF16)
    cf_d = din("cf", [128, 2192])
    out_d = nc.dram_tensor("out", [SEQ, D], F32, kind="ExternalOutput").ap()
    if dbg:
        dbg_d = nc.dram_tensor("dbg", [128, 8 * NTOK], F32, kind="ExternalOutput").ap()

    with contextlib.ExitStack() as st:
        def sb(name, shape, dt):
            return st.enter_context(nc.sbuf_tensor(name, list(shape), dt))

        uT = sb("uT", [128, 8, NTOK], BF16)
        A1 = sb("A1", [128, 4, NTOK], BF16)
        A2 = sb("A2", [128, 4, NTOK], BF16)
        A3 = sb("A3", [128, NB, 520], BF16)
        A4 = sb("A4", [128, 16, 512], BF16)
        A5flat = sb("A5", [128, NB * 512], BF16)
        A5 = A5flat[:, :].rearrange("p (b n) -> p b n", n=512)
        A5f = A5flat.bitcast(F32)
        mixT = sb("mixT", [128, 8, SEQ], BF16)
        scr = sb("scr", [128, 16384], BF16)
        scrf = scr.bitcast(F32)
        tmpf = sb("tmpf", [128, 2048], F32)
        gbc = sb("gbcs", [128, D], F32)
        cbf = sb("cbfs", [128, 640], BF16)
        cf = sb("cfs", [128, 2192], F32)
        gsub = sb("gsubs", [128, 128], F32)
        lamv = sb("lamvs", [128, 256], F32)
        small = sb("small", [128, 256], F32)
        ps = [st.enter_context(nc.psum_tensor(f"ps{i}", [128, 512], F32)) for i in range(8)]
        psb = [p.bitcast(BF16) for p in ps]

        P = Prog(nc, st)

        ident_bf = cbf[:, 0:128]
        negM2T = cbf[:, 128:256]
        Dmat = cbf[:, 256:384]
        dlast = cbf[0:1, 384:512]
        permT_bf = cbf[:, 512:640]
        permT = cf[:, 0:128]
        M1Z = cf[:, 128:2192]

        P.dma("sp", "c_gbc", gbc[:, :], gbc_d, writes=["gbc"])
        P.dma("sp", "c_cbf", cbf[:, :], cbf_d, writes=["cbf"])
        P.dma("sp", "c_cf", cf[:, :], cf_d, writes=["cf"])
        P.dma("sp", "c_gsub", gsub[:, :], gsub_d, writes=["gsub"])
        P.dma("sp", "c_lamv", lamv[:, :], lamv_d, writes=["lamv"])

        P.op("pool", lambda e: e.memset(small[:, 65:66], EPS), writes=["epsc"])
        epsc = small[:, 65:66]
        P.op("pool", lambda e: e.memset(small[:, 66:68], -0.5), writes=["neghalf"])
        neghalf = small[:, 66:67]
        neghalf2 = small[:, 66:68]
        P.op("dve", lambda e: e.tensor_tensor(out=tmpf[:, 0:64], in0=lamv[:, 0:64], in1=lamv[:, 64:128], op=ALU.mult),
             reads=["lamv"], writes=["lt0"])
        P.op("dve", lambda e: e.reduce_sum(out=small[:, 60:61], in_=tmpf[:, 0:64], axis=AX.X),
             reads=["lt0"], writes=["s1"])
        P.op("dve", lambda e: e.tensor_tensor(out=tmpf[:, 64:128], in0=lamv[:, 128:192], in1=lamv[:, 192:256], op=ALU.mult),
             reads=["lamv"], writes=["lt1"])
        P.op("dve", lambda e: e.reduce_sum(out=small[:, 61:62], in_=tmpf[:, 64:128], axis=AX.X),
             reads=["lt1"], writes=["s2"])
        P.op("act", lambda e: e.activation(out=small[:, 62:64], in_=small[:, 60:62], func=AF.Exp),
             reads=["s1", "s2"], writes=["e12"])
        P.op("dve", lambda e: e.tensor_tensor(out=small[:, 64:65], in0=small[:, 63:64], in1=small[:, 62:63], op=ALU.subtract),
             reads=["e12"], writes=["nl0"])
        P.op("dve", lambda e: e.tensor_scalar(out=small[:, 64:65], in0=small[:, 64:65], scalar1=-LAMBDA_INIT, scalar2=None, op0=ALU.add),
             reads=["nl0"], writes=["neglam"])
        neglam = small[:, 64:65]
        P.op("dve", lambda e: e.tensor_scalar(out=gsub[:, :], in0=gsub[:, :], scalar1=1.0 - LAMBDA_INIT, scalar2=None, op0=ALU.mult),
             reads=["gsub"], writes=["gsub"])

        wbuf = [scr[:, 8192 + s * 4096: 8192 + (s + 1) * 4096].rearrange("p (c n) -> p c n", n=512) for s in range(2)] + \
               [scr[:, s * 4096:(s + 1) * 4096].rearrange("p (c n) -> p c n", n=512) for s in range(2)]

        def load_w(slot, g):
            src = wext[:, g * 512:(g + 1) * 512].rearrange("(c p) n -> p c n", p=128)
            P.dma("pool", f"w{slot}", wbuf[slot], src, writes=[f"w{slot}"])

        load_w(0, 0)
        load_w(1, 1)

        xs = [A5f[:, s * 1024:(s + 1) * 1024] for s in range(3)]
        ub = [A5flat[:, 6144 + s * 1024: 6144 + (s + 1) * 1024] for s in range(2)]
        sqj = tmpf.bitcast(BF16)[:, 2048:3072]
        def p0_front(b):
            rows = blk_rows(b)
            s = b % 2
            x3 = b % 3
            src = xr[b * 128:(b + 1) * 128, :] if b < 16 else metar
            P.dma("sp", f"xs{x3}", xs[x3][:rows, :], src, writes=[f"xs{x3}"])
            P.op("act", lambda e: e.activation(
                out=sqj[:rows, :], in_=xs[x3][:rows, :], func=AF.Square, accum_out=small[:rows, b:b + 1]),
                reads=[f"xs{x3}"], writes=["sqj", f"ss{b}"])
            P.op("dve", lambda e: e.tensor_scalar(
                out=small[:rows, 17 + b:18 + b], in0=small[:rows, b:b + 1], scalar1=1.0 / D, scalar2=EPS,
                op0=ALU.mult, op1=ALU.add),
                reads=[f"ss{b}"], writes=[f"sr{b}"])
            P.op("pool", lambda e: e.tensor_tensor(
                out=small[:rows, 34 + b:35 + b], in0=small[:rows, 17 + b:18 + b], in1=neghalf[:rows, :], op=ALU.pow),
                reads=[f"sr{b}", "neghalf"], writes=[f"rstd{b}"])
            P.op("dve", lambda e: e.scalar_tensor_tensor(
                out=ub[s][:rows, :], in0=xs[x3][:rows, :], scalar=small[:rows, 34 + b:35 + b], in1=gbc[:rows, :],
                op0=ALU.mult, op1=ALU.mult),
                reads=[f"xs{x3}", f"rstd{b}", "gbc"], writes=[f"ub{s}"])

        def p0_tr(b):
            rows = blk_rows(b)
            s = b % 2
            pi = 4 + b % 2
            for c in range(8):
                P.op("pe", lambda e, c=c: e.transpose(
                    out=psb[pi][:, c * 128:c * 128 + rows], in_=ub[s][:rows, c * 128:(c + 1) * 128],
                    identity=ident_bf[:rows, :rows]),
                    reads=[f"ub{s}", "cbf"], writes=[f"ps{pi}"])

        def p0_back(b):
            rows = blk_rows(b)
            pi = 4 + b % 2
            srcv = psb[pi][:, :].rearrange("p (c t) -> p c t", t=128)[:, :, :rows]
            dstv = uT[:, :, b * 128:b * 128 + rows]
            if b % 2 == 0:
                P.op("act", lambda e: e.activation(out=dstv, in_=srcv, func=AF.Copy),
                     reads=[f"ps{pi}"], writes=[f"uT{b}"])
            else:
                P.op("dve", lambda e: e.tensor_copy(out=dstv, in_=srcv),
                     reads=[f"ps{pi}"], writes=[f"uT{b}"])


        TR = [(0, 512), (512, 512), (1024, 512), (1536, 512), (2048, 16)]
        pcount = [0]
        pspool = [[0, 1, 2, 3, 6, 7]]

        def nextps(pool):
            i = pool[pcount[0] % len(pool)]
            pcount[0] += 1
            return i

        def feat_unit(slot, evac, cc, t0, n):
            pi = nextps(pspool[0])
            ub_ = [f"uT{b}" for b in range(t0 // 128, (t0 + n + 127) // 128)]
            for dc in range(8):
                P.op("pe", lambda e, dc=dc: e.matmul(
                    out=ps[pi][:, :n], lhsT=wbuf[slot][:, dc, cc * 128:(cc + 1) * 128],
                    rhs=uT[:, dc, t0:t0 + n], start=(dc == 0), stop=(dc == 7)),
                    reads=[f"w{slot}"] + ub_, writes=[f"ps{pi}"])
            evac(cc, t0, n, pi)

        def tok_unit(slot, evac, b):
            rows = blk_rows(b)
            pi = nextps(pspool[0])
            for dc in range(8):
                P.op("pe", lambda e, dc=dc: e.matmul(
                    out=ps[pi][:rows, :512], lhsT=uT[:, dc, b * 128:b * 128 + rows],
                    rhs=wbuf[slot][:, dc, :], start=(dc == 0), stop=(dc == 7)),
                    reads=[f"w{slot}", f"uT{b}"], writes=[f"ps{pi}"])
            evac(b, rows, pi)

        def proj_feat(slot, evac, tr=None):
            for cc in range(4):
                for (t0, n) in (tr or TR):
                    feat_unit(slot, evac, cc, t0, n)

        def proj_tok(slot, evac, nblk):
            for b in range(nblk):
                tok_unit(slot, evac, b)

        flip = [0]

        def evac_copy_to(dst3):
            def f(cc, t0, n, pi):
                flip[0] ^= 1
                if flip[0]:
                    P.op("act", lambda e: e.activation(out=dst3[:, cc, t0:t0 + n], in_=ps[pi][:, :n], func=AF.Copy),
                         reads=[f"ps{pi}"], writes=[])
                else:
                    P.op("dve", lambda e: e.tensor_copy(out=dst3[:, cc, t0:t0 + n], in_=ps[pi][:, :n]),
                         reads=[f"ps{pi}"], writes=[])
            return f

        QP = [A1[:, h, 0:SEQ] for h in range(4)] + [mixT[:, 4 + h, :] for h in range(4)]
        for h in range(8):
            zr = slice(64, 128) if h % 2 == 0 else slice(0, 64)
            P.op("pool", lambda e, h=h, zr=zr: e.memset(QP[h][zr, :], 0.0), writes=[f"qpz{h}"])

        def evac_q(cc, t0, n, pi):
            if t0 >= SEQ:
                return
            P.op("act", lambda e: e.activation(out=QP[2 * cc][0:64, t0:t0 + n], in_=ps[pi][0:64, :n], func=AF.Copy),
                 reads=[f"ps{pi}"], writes=[])
            P.op("dve", lambda e: e.tensor_copy(out=QP[2 * cc + 1][64:128, t0:t0 + n], in_=ps[pi][64:128, :n]),
                 reads=[f"ps{pi}"], writes=[])
        def evac_v(b, rows, pi):
            P.op("act", lambda e: e.activation(out=A3[:rows, b, 0:512], in_=ps[pi][:rows, :512], func=AF.Copy),
                 reads=[f"ps{pi}"], writes=[f"v{b}"])

        def evac_g(b, rows, pi):
            P.op("act", lambda e: e.activation(out=A4[:rows, b, :], in_=ps[pi][:rows, :512], func=AF.Silu),
                 reads=[f"ps{pi}"], writes=[])

        load_w(2, 2)
        load_w(3, 3)
        evac_k = evac_copy_to(A2)
        fifo = []
        p0_front(0)
        p0_tr(0)
        for b in range(1, NB + 1):
            if b < NB:
                p0_front(b)
            if b >= 1:
                p0_back(b - 1)
                bb = b - 1
                if bb % 4 == 3 or bb == 16:
                    r = bb // 4
                    t0, n = TR[r]
                    for cc in range(4):
                        if r < 4:
                            fifo.append(lambda cc=cc, t0=t0, n=n: feat_unit(0, evac_q, cc, t0, n))
                        fifo.append(lambda cc=cc, t0=t0, n=n: feat_unit(1, evac_k, cc, t0, n))
                    for b2 in range(4 * r, min(4 * r + 4, NB)):
                        fifo.append(lambda b2=b2: tok_unit(2, evac_v, b2))
                        if b2 < 16:
                            fifo.append(lambda b2=b2: tok_unit(3, evac_g, b2))
            for _ in range(4):
                if fifo:
                    fifo.pop(0)()
            if b < NB:
                p0_tr(b)
        while fifo:
            fifo.pop(0)()
        P.barrier()
        P.op("pool", lambda e: e.memset(A5[:, 16, :], 0.0), writes=["dvmeta"])
        for b in range(NB):
            rows = blk_rows(b)
            pi = nextps([0, 1, 2, 3])
            last = (b == NB - 1)
            P.op("pe", lambda e, b=b, rows=rows, pi=pi, last=last: e.matmul(
                out=ps[pi][:rows, :512], lhsT=Dmat[:rows, :rows], rhs=A3[:rows, b, 0:512], start=True, stop=last),
                reads=[f"v{b}", "cbf"], writes=[f"ps{pi}"])
            if not last:
                P.op("pe", lambda e, b=b, pi=pi: e.matmul(
                    out=ps[pi][:128, :512], lhsT=dlast, rhs=A3[0:1, b + 1, 0:512], start=False, stop=True),
                    reads=[f"v{b + 1}"], writes=[f"ps{pi}"])
            P.op("dve", lambda e, b=b, rows=rows, pi=pi: e.tensor_copy(out=A5[:rows, b, :], in_=ps[pi][:rows, :512]),
                 reads=[f"ps{pi}"] + (["dvmeta"] if b == 16 else []), writes=[f"dv{b}"] + (["dvmeta"] if b == 16 else []))

        keepL = [scrf[:, i * 2064:(i + 1) * 2064] for i in range(2)]
        Pb = [scr[:, 8256 + i * 2176: 8256 + (i + 1) * 2176] for i in range(3)]
        tmpb = tmpf.bitcast(BF16)
        if DMAT_MOD:
            PTDs = [tmpb[:, 512 + i * 1152: 512 + (i + 1) * 1152].rearrange("p (b t) -> p b t", t=128) for i in range(2)]
            PTb = [scr[:, 14784 + i * 512: 14784 + (i + 1) * 512] for i in range(3)] + \
                  [tmpb[:, 2816 + i * 512: 2816 + (i + 1) * 512] for i in range(2)]
        else:
            PTb = [scr[:, 14784 + i * 512: 14784 + (i + 1) * 512] for i in range(3)] + \
                  [tmpb[:, 512 + i * 512: 512 + (i + 1) * 512] for i in range(7)]
        NPT = len(PTb)

        def is_dma_chunk(k, c):
            return bool(DMAT_MOD) and c >= 2
        for i in range(3):
            P.op("pool", lambda e, i=i: e.memset(Pb[i][:, :], 0.0), writes=[f"Pb{i}"])
        mixtok = [tmpf.bitcast(BF16)[:, 0:512]]
        cnt = {"z": 0, "k": 0, "p": 0, "t": 0, "pt": 0, "ev": 0}

        def rot(name, n):
            i = cnt[name] % n
            cnt[name] += 1
            return i

        def sb_chunks(j):
            blocks = list(range(j, 16))
            chunks = []
            while blocks:
                cb = blocks[:4]
                blocks = blocks[4:]
                chunks.append([(kb, 128) for kb in cb])
            if len(chunks[-1]) < 4:
                chunks[-1].append((16, 16))
            else:
                chunks.append([(16, 16)])
            return chunks

        items = [(j, h) for j in range(16) for h in range(8)]
        NI = len(items)
        ptslot = {}

        def st_z(k, c):
            j, h = items[k]
            ch = sb_chunks(j)[c]
            cc, po = h // 2, (h % 2) * 64
            n = sum(r for _, r in ch)
            t0 = ch[0][0] * 128
            off = t0 - 128 * j
            zi = rot("z", 2)
            ks = k % 2
            P.op("pe", lambda e: e.matmul(
                out=ps[zi][:, :n], lhsT=QP[h][:, j * 128:(j + 1) * 128],
                rhs=A2[:, cc, t0:t0 + n], start=True, stop=True),
                reads=[], writes=[f"ps{zi}"])
            P.op("act", lambda e: e.activation(
                out=keepL[ks][:, off:off + n], in_=ps[zi][:, :n], func=AF.Sigmoid, scale=-0.125),
                reads=[f"ps{zi}"], writes=[f"keep{ks}"])

        def st_scan(k):
            j, h = items[k]
            ntot = NTOK - 128 * j
            ks, pslot = k % 2, k % 3
            P.op("dve", lambda e: e.tensor_tensor_scan(
                out=Pb[pslot][:, :ntot], data0=keepL[ks][:, :ntot], data1=M1Z[:, :ntot], initial=1.0,
                op0=ALU.mult, op1=ALU.max),
                reads=[f"keep{ks}", "cf"], writes=[f"Pb{pslot}"])

        def st_t(k, c):
            j, h = items[k]
            ch = sb_chunks(j)[c]
            pslot = k % 3
            if is_dma_chunk(k, c):
                if c == 2:
                    nfar = 17 - j - 8
                    P.dma("sp", "dmat", PTDs[k % 2][:, 0:nfar, :], Pb[pslot][:, 1024:1024 + nfar * 128],
                          reads=[f"Pb{pslot}"], writes=[f"PTD{k % 2}", "dmat_serial"], transpose=True)
                return
            ti = 2 + rot("t", 2)
            off = ch[0][0] * 128 - 128 * j
            col = off
            for bi, (kb, r) in enumerate(ch):
                P.op("pe", lambda e, bi=bi, col=col: e.transpose(
                    out=psb[ti][:, bi * 128:(bi + 1) * 128], in_=Pb[pslot][:, col:col + 128],
                    identity=ident_bf),
                    reads=[f"Pb{pslot}", "cbf"], writes=[f"ps{ti}"])
                col += r
            pti = rot("pt", NPT)
            ptslot[(k, c)] = pti
            nb_ = len(ch)
            if ((not DMAT_MOD) and rot("ev", 4) == 3) or (DMAT_MOD and j >= 11 and c == 1):
                P.op("dve", lambda e: e.tensor_copy(out=PTb[pti][:, :nb_ * 128], in_=psb[ti][:, :nb_ * 128]),
                     reads=[f"ps{ti}"], writes=[f"PT{pti}"])
            else:
                P.op("act", lambda e: e.activation(
                    out=PTb[pti][:, :nb_ * 128], in_=psb[ti][:, :nb_ * 128], func=AF.Copy),
                    reads=[f"ps{ti}"], writes=[f"PT{pti}"])

        def st_av(k, c):
            j, h = items[k]
            ch = sb_chunks(j)[c]
            oi = 6 + (j % 2)
            hc = slice(h * 64, (h + 1) * 64)
            if is_dma_chunk(k, c):
                for bi, (kb, r) in enumerate(ch):
                    P.op("pe", lambda e, bi=bi, kb=kb: e.matmul(
                        out=ps[oi][:, hc], lhsT=PTDs[k % 2][:, kb - j - 8, :],
                        rhs=A5[:, kb, hc], start=(c == 0 and bi == 0), stop=False),
                        reads=[f"PTD{k % 2}", f"dv{kb}"], writes=[f"ps{oi}"])
                return
            pti = ptslot[(k, c)]
            for bi, (kb, r) in enumerate(ch):
                P.op("pe", lambda e, bi=bi, kb=kb: e.matmul(
                    out=ps[oi][:, hc], lhsT=PTb[pti][:, bi * 128:(bi + 1) * 128],
                    rhs=A5[:, kb, hc], start=(c == 0 and bi == 0), stop=False),
                    reads=[f"PT{pti}", f"dv{kb}"], writes=[f"ps{oi}"])

        def st_tail(k):
            j, h = items[k]
            oi = 6 + (j % 2)
            hc = slice(h * 64, (h + 1) * 64)
            P.op("pe", lambda e: e.matmul(
                out=ps[oi][:, hc], lhsT=negM2T, rhs=A5[:, j, hc], start=False, stop=False),
                reads=["cbf", f"dv{j}"], writes=[f"ps{oi}"])
            P.op("pe", lambda e: e.matmul(
                out=ps[oi][:, hc], lhsT=ident_bf, rhs=A3[:, j, hc], start=False, stop=True),
                reads=["cbf"], writes=[f"ps{oi}"])
            if h == 7:
                P.op("dve", lambda e: e.tensor_tensor(
                    out=mixtok[0][:, :], in0=ps[oi][:, :512], in1=A4[:, j, :], op=ALU.mult),
                    reads=[f"ps{oi}"], writes=["mixtok"])
                ti = 4 + (j % 2)
                for c in range(4):
                    P.op("pe", lambda e, c=c: e.transpose(
                        out=psb[ti][:, c * 128:(c + 1) * 128], in_=mixtok[0][:, c * 128:(c + 1) * 128], identity=ident_bf),
                        reads=["mixtok", "cbf"], writes=[f"ps{ti}"])
                P.op("act", lambda e: e.activation(
                    out=mixT[:, 0:4, j * 128:(j + 1) * 128],
                    in_=psb[ti][:, 0:512].rearrange("p (c t) -> p c t", t=128), func=AF.Copy),
                    reads=[f"ps{ti}"], writes=[])

        def nch(k):
            return len(sb_chunks(items[k][0])) if 0 <= k < NI else 0

        SK = 2
        for k in range(NI + SK + 1):
            if 0 <= k - 1 < NI:
                st_scan(k - 1)
            na, nt_, nv_ = nch(k), nch(k - SK), nch(k - SK - 1)
            for c in range(na):
                st_z(k, c)
            for c in range(max(nt_, nv_)):
                if c < nt_:
                    st_t(k - SK, c)
                if c < nv_:
                    st_av(k - SK - 1, c)
            if 0 <= k - SK - 1 < NI:
                st_tail(k - SK - 1)
            if k == NI:
                for slot, g in ((2, 6), (3, 7)):
                    src = wext[:, g * 512:(g + 1) * 512].rearrange("(c p) n -> p c n", p=128)
                    P.dma("pool", f"w{slot}", wbuf[slot], src, writes=[f"w{slot}", "keep0", "keep1", "dmat_serial"])
        P.barrier()
        load_w(0, 4)
        load_w(1, 5)

        pspool[0] = [0, 1, 2, 3, 4, 5, 6, 7]
        ropec = A5f[:, 0:NTOK]
        ropes = A5f[:, NTOK:2 * NTOK]
        P.dma("sp", "c_rc", ropec, ropec_d, writes=["ropec"])
        P.dma("sp", "c_rs", ropes, ropes_d, writes=["ropes"])
        vaug4 = A3[:, :, :].rearrange("p b (h e) -> p b h e", e=130)
        P.op("pool", lambda e: e.memset(vaug4[:, :, :, 128:129], 1.0), writes=["vones"])
        P.op("pool", lambda e: e.memset(vaug4[:, :, :, 129:130], 0.0), writes=["vzero"])

        QPD = [A1[:, h, 0:SEQ] for h in range(4)] + [mixT[:, 4 + h, :] for h in range(4)]
        for h in range(8):
            zr = slice(64, 128) if h % 2 == 0 else slice(0, 64)
            P.op("pool", lambda e, h=h, zr=zr: e.memset(QPD[h][zr, :], 0.0), writes=[f"qpdz{h}"])

        qhb = [scr[:, i * 512:(i + 1) * 512] for i in range(4)]
        qlb = [scr[:, 2048 + i * 512: 2048 + (i + 1) * 512] for i in range(4)]

        def proj_rope(dst3, slot, padded=False):
            units = [(cc, t0, n) for cc in range(4) for (t0, n) in (TR[:4] if padded else TR)]
            pend = None

            def finish(u):
                cc, t0, n, pa, qi = u
                pb = nextps([0, 1, 2, 3, 4, 5, 6, 7])
                P.op("pe", lambda e: e.matmul(out=ps[pb][:, :n], lhsT=permT_bf, rhs=qhb[qi][:, :n], start=True, stop=False),
                     reads=[f"qs{qi}", "cbf"], writes=[f"ps{pb}"])
                P.op("pe", lambda e: e.matmul(out=ps[pb][:, :n], lhsT=permT_bf, rhs=qlb[qi][:, :n], start=False, stop=True),
                     reads=[f"ql{qi}", "cbf"], writes=[f"ps{pb}"])
                ta = rot("k", 2)
                t1 = tmpf[:, ta * 1024: ta * 1024 + n]
                t2 = tmpf[:, ta * 1024 + 512: ta * 1024 + 512 + n]
                P.op("dve", lambda e: e.tensor_tensor(out=t1, in0=ps[pa][:, :n], in1=ropec[:, t0:t0 + n], op=ALU.mult),
                     reads=[f"ps{pa}", "ropec"], writes=[f"t1{ta}"])
                P.op("dve", lambda e: e.tensor_tensor(out=t2, in0=ps[pb][:, :n], in1=ropes[:, t0:t0 + n], op=ALU.mult),
                     reads=[f"ps{pb}", "ropes"], writes=[f"t2{ta}"])
                if padded:
                    P.op("pool", lambda e: e.tensor_tensor(
                        out=QPD[2 * cc][0:64, t0:t0 + n], in0=t1[0:64, :], in1=t2[0:64, :], op=ALU.add),
                        reads=[f"t1{ta}", f"t2{ta}", f"qpdz{2 * cc}"], writes=[f"qpa{ta}"])
                    P.op("pool", lambda e: e.tensor_tensor(
                        out=QPD[2 * cc + 1][64:128, t0:t0 + n], in0=t1[64:128, :], in1=t2[64:128, :], op=ALU.add),
                        reads=[f"t1{ta}", f"t2{ta}", f"qpdz{2 * cc + 1}"], writes=[f"qpb{ta}"])
                else:
                    P.op("pool", lambda e: e.tensor_tensor(
                        out=dst3[:, cc, t0:t0 + n], in0=t1, in1=t2, op=ALU.add),
                        reads=[f"t1{ta}", f"t2{ta}"], writes=[f"qpa{ta}"])

            for (cc, t0, n) in units:
                pa = nextps([0, 1, 2, 3, 4, 5, 6, 7])
                for dc in range(8):
                    P.op("pe", lambda e, dc=dc, pa=pa, cc=cc, t0=t0, n=n: e.matmul(
                        out=ps[pa][:, :n], lhsT=wbuf[slot][:, dc, cc * 128:(cc + 1) * 128],
                        rhs=uT[:, dc, t0:t0 + n], start=(dc == 0), stop=(dc == 7)),
                        reads=[f"w{slot}"], writes=[f"ps{pa}"])
                qi = rot("p", 4)
                P.op("act", lambda e, pa=pa, qi=qi, n=n: e.activation(out=qhb[qi][:, :n], in_=ps[pa][:, :n], func=AF.Copy),
                     reads=[f"ps{pa}"], writes=[f"qs{qi}"])
                P.op("dve", lambda e, pa=pa, qi=qi, n=n: e.tensor_tensor(
                    out=qlb[qi][:, :n], in0=ps[pa][:, :n], in1=qhb[qi][:, :n], op=ALU.subtract),
                    reads=[f"ps{pa}", f"qs{qi}"], writes=[f"ql{qi}"])
                if pend is not None:
                    finish(pend)
                pend = (cc, t0, n, pa, qi)
            finish(pend)

        def evac_vd(b, rows, pi):
            P.op("act", lambda e: e.activation(
                out=vaug4[:rows, b, :, 0:128], in_=ps[pi][:rows, :512].rearrange("p (h e) -> p h e", e=128), func=AF.Copy),
                reads=[f"ps{pi}"], writes=[])
        if DFP_LEVEL >= 1:
            proj_tok(2, evac_vd, NB)
        if DFP_LEVEL >= 2:
            proj_tok(3, evac_g, 16)
        if DFP_LEVEL >= 3:
            proj_rope(None, 0, padded=True)
        if DFP_LEVEL >= 4:
            proj_rope(A2, 1)
        P.barrier()

        wo = scr[:, 8192:16384].rearrange("p (c n) -> p c n", n=1024)
        P.dma("pool", "w0", wo, wout.rearrange("(c p) n -> p c n", p=128), writes=["wo"])
        gfin = A5f[:, 0:1024]
        P.dma("sp", "c_rc", gfin, gfin_d, writes=["gfin"])
        ETb = [scr[:, i * 512:(i + 1) * 512] for i in range(8)]
        of32 = [tmpf[:, i * 128:(i + 1) * 128] for i in range(2)]
        yf32 = [tmpf[:, 256 + i * 128: 256 + (i + 1) * 128] for i in range(2)]
        sqj2 = tmpf[:, 0:128]
        mixtok2s = [scr[:, 4096:4608], scr[:, 4608:5120]]
        sc = {"i": 0}
        dsteps = []
        for j in range(16):
            for hp in range(2):
                st_l = [[kb] for kb in range(j, NB)]
                for si, kbs in enumerate(st_l):
                    dsteps.append({"j": j, "hp": hp, "kbs": kbs, "last": si == len(st_l) - 1,
                                   "par": ((2 * j + hp) % 2) if DF_ODB else 1})

        def d_zexp(stp):
            j, hp, kbs = stp["j"], stp["hp"], stp["kbs"]
            rows = blk_rows(kbs[0])
            zb = [rot("z", DF_ZP) for _ in kbs]
            eb = [rot("k", 8) for _ in kbs]
            stp["eb"] = eb
            for bi, kb in enumerate(kbs):
                for p2 in range(2):
                    hc0 = 4 * hp + 2 * p2
                    cc = hc0 // 2
                    if hc0 < 4:
                        rhs2 = A1[:, hc0:hc0 + 2, j * 128:(j + 1) * 128]
                    else:
                        rhs2 = mixT[:, hc0:hc0 + 2, j * 128:(j + 1) * 128]
                    P.op("pe", lambda e, zi=zb[bi], p2=p2, cc=cc, kb=kb, rhs2=rhs2: e.matmul(
                        out=ps[zi][:rows, p2 * 256:(p2 + 1) * 256], lhsT=A2[:, cc, kb * 128:kb * 128 + rows],
                        rhs=rhs2, start=True, stop=True),
                        reads=[f"qpd{j}"], writes=[f"ps{zb[bi]}"])
                zi, ei = zb[bi], eb[bi]
                P.op("act", lambda e, zi=zi, ei=ei: e.activation(
                    out=ETb[ei][:rows, :], in_=ps[zi][:rows, :], func=AF.Exp, scale=0.125),
                    reads=[f"ps{zi}"], writes=[f"ET{ei}"])
                if kb == j:
                    rect = ETb[ei][0:64, :].rearrange("p (i t) -> p i t", t=128)[:, :, 64:128]
                    P.op("pool", lambda e, rect=rect: e.memset(rect, 0.0),
                         reads=[], writes=[f"ET{ei}"])

        def d_av(stp):
            j, hp, kbs, eb = stp["j"], stp["hp"], stp["kbs"], stp["eb"]
            rows = blk_rows(kbs[0])
            for bi, kb in enumerate(kbs):
                for i in range(4):
                    head = (4 * hp + i) // 2
                    ei = eb[bi]
                    col = i * 128
                    ob, oc = 4 + 2 * stp["par"] + i // 2, (i % 2) * 256
                    P.op("pe", lambda e, i=i, ei=ei, col=col, kb=kb, head=head, ob=ob, oc=oc: e.matmul(
                        out=ps[ob][:, oc:oc + 129], lhsT=ETb[ei][:rows, col:col + 128],
                        rhs=A3[:rows, kb, head * 130:head * 130 + 129],
                        start=(kb == j and i % 2 == 0), stop=(kb == NB - 1), skip_group_check=True),
                        reads=[f"ET{ei}", "vones"], writes=[f"ps{ob}"])
            if stp["last"]:
                d_epilogue(j, hp, stp["par"])

        def d_epilogue(j, hp, par):
            k = sc["i"] % 4
            sc["i"] += 1
            q = k % 2
            cb = 80 + k * 16
            banks = [4 + 2 * par, 5 + 2 * par]
            tb = 512 + q * 768
            t1b = [tmpf[:, tb + hl * 128: tb + (hl + 1) * 128] for hl in range(2)]
            ofb = [tmpf[:, tb + 256 + hl * 128: tb + 256 + (hl + 1) * 128] for hl in range(2)]
            yfb = [tmpf[:, tb + 512 + hl * 128: tb + 512 + (hl + 1) * 128] for hl in range(2)]
            for hl in range(2):
                pb_ = banks[hl]
                P.op("dve", lambda e, pb_=pb_, hl=hl: e.reciprocal(
                    out=small[:, cb + 2 * hl:cb + 2 * hl + 2], in_=ps[pb_][:, 128:512:256]),
                    reads=[f"ps{pb_}"], writes=[f"rz{k}_{hl}"])
            P.op("dve", lambda e: e.tensor_scalar(
                out=small[:, cb + 4:cb + 6], in0=small[:, cb + 1:cb + 4:2], scalar1=neglam, scalar2=None, op0=ALU.mult),
                reads=[f"rz{k}_0", f"rz{k}_1", "neglam"], writes=[f"c1{k}"])
            act_t1 = j >= EPI_ACT_FROM_TILE
            for hl in range(2):
                pb_ = banks[hl]
                if act_t1:
                    P.op("act", lambda e, pb_=pb_, hl=hl: e.activation(
                        out=t1b[hl], in_=ps[pb_][:, 256:384], func=AF.Copy, scale=small[:, cb + 4 + hl:cb + 5 + hl]),
                        reads=[f"ps{pb_}", f"c1{k}"], writes=[f"t1e{q}_{hl}"])
                else:
                    P.op("dve", lambda e, pb_=pb_, hl=hl: e.tensor_scalar(
                        out=t1b[hl], in0=ps[pb_][:, 256:384], scalar1=small[:, cb + 4 + hl:cb + 5 + hl], scalar2=None, op0=ALU.mult),
                        reads=[f"ps{pb_}", f"c1{k}"], writes=[f"t1e{q}_{hl}"])
            for hl in range(2):
                pb_ = banks[hl]
                P.op("dve", lambda e, pb_=pb_, hl=hl: e.scalar_tensor_tensor(
                    out=ofb[hl], in0=ps[pb_][:, 0:128], scalar=small[:, cb + 2 * hl:cb + 2 * hl + 1], in1=t1b[hl],
                    op0=ALU.mult, op1=ALU.add),
                    reads=[f"ps{pb_}", f"rz{k}_{hl}", f"t1e{q}_{hl}"], writes=[f"of{q}_{hl}"])
            for hl in range(2):
                P.op("dve", lambda e, hl=hl: e.scalar_tensor_tensor(
                    out=sqj2, in0=ofb[hl], scalar=1.0, in1=ofb[hl], op0=ALU.mult, op1=ALU.mult,
                    accum_out=small[:, cb + 6 + hl:cb + 7 + hl]),
                    reads=[f"of{q}_{hl}"], writes=["sqj2", f"ss2{k}_{hl}"])
            P.op("dve", lambda e: e.tensor_scalar(
                out=small[:, cb + 8:cb + 10], in0=small[:, cb + 6:cb + 8], scalar1=1.0 / 128, scalar2=EPS,
                op0=ALU.mult, op1=ALU.add),
                reads=[f"ss2{k}_0", f"ss2{k}_1"], writes=[f"sq2{k}"])
            P.op("pool", lambda e: e.tensor_tensor(
                out=small[:, cb + 10:cb + 12], in0=small[:, cb + 8:cb + 10], in1=neghalf2, op=ALU.pow),
                reads=[f"sq2{k}", "neghalf"], writes=[f"rs2{k}"])
            def part_b():
                for hl in range(2):
                    head = 2 * hp + hl
                    P.op("dve", lambda e, hl=hl: e.scalar_tensor_tensor(
                        out=yfb[hl], in0=ofb[hl], scalar=small[:, cb + 10 + hl:cb + 11 + hl], in1=gsub[:, :],
                        op0=ALU.mult, op1=ALU.mult),
                        reads=[f"of{q}_{hl}", f"rs2{k}", "gsub"], writes=[f"yf{q}_{hl}"])
                    P.op("pool", lambda e, hl=hl, head=head: e.tensor_tensor(
                        out=mixtok2s[j % 2][:, head * 128:(head + 1) * 128], in0=yfb[hl], in1=A4[:, j, head * 128:(head + 1) * 128], op=ALU.mult),
                        reads=[f"yf{q}_{hl}"], writes=[f"mt2_{j % 2}_{head}"])
                if hp == 1:
                    pending_mix.append([j, 3])
            while pend_b:
                pend_b.pop(0)()
            pend_b.append(part_b)

        pend_b = []

        def d_mix(j):
            ti = rot("z", DF_ZP)
            mt = mixtok2s[j % 2]
            for c in range(4):
                P.op("pe", lambda e, ti=ti, c=c: e.transpose(
                    out=psb[ti][:, c * 128:(c + 1) * 128], in_=mt[:, c * 128:(c + 1) * 128], identity=ident_bf),
                    reads=[f"mt2_{j % 2}_{c}", "cbf"], writes=[f"ps{ti}"])
            P.op("dve", lambda e, ti=ti, j=j: e.tensor_copy(
                out=mixT[:, 4:8, j * 128:(j + 1) * 128],
                in_=psb[ti][:, 0:512].rearrange("p (c t) -> p c t", t=128)),
                reads=[f"ps{ti}"], writes=[f"qpd{j}"])

        pending_mix = []
        DSK = DF_DSK
        for idx in range(len(dsteps) + DSK):
            if idx < len(dsteps):
                d_zexp(dsteps[idx])
            for pm in list(pending_mix):
                pm[1] -= 1
                if pm[1] <= 0:
                    pending_mix.remove(pm)
                    d_mix(pm[0])
            if idx >= DSK:
                d_av(dsteps[idx - DSK])
        while pend_b:
            pend_b.pop(0)()
        for pm in pending_mix:
            d_mix(pm[0])
        P.barrier()

        xs2 = [scrf[:, s * 1024:(s + 1) * 1024] for s in range(4)]
        yo = [tmpf[:, 0:1024], tmpf[:, 1024:2048]]
        sqj3 = A4[:, 0:2, :].rearrange("p a n -> p (a n)")

        def p5_front(j):
            s = j % 4
            P.dma("sp", f"xs{s}", xs2[s], xr[j * 128:(j + 1) * 128, :], writes=[f"xs2{s}"])
            pa, pb = 2 * (j % 4), 2 * (j % 4) + 1
            for (pi, half) in ((pa, 0), (pb, 1)):
                for c in range(8):
                    P.op("pe", lambda e, pi=pi, half=half, c=c: e.matmul(
                        out=ps[pi][:, :512], lhsT=mixT[:, c, j * 128:(j + 1) * 128],
                        rhs=wo[:, c, half * 512:(half + 1) * 512], start=(c == 0), stop=(c == 7)),
                        reads=["wo"], writes=[f"ps{pi}"])
            for (pi, half) in ((pa, 0), (pb, 1)):
                P.op("dve", lambda e, pi=pi, half=half: e.tensor_tensor(
                    out=xs2[s][:, half * 512:(half + 1) * 512], in0=ps[pi][:, :512],
                    in1=xs2[s][:, half * 512:(half + 1) * 512], op=ALU.add),
                    reads=[f"ps{pi}", f"xs2{s}"], writes=[f"xs2{s}"])
            cb = 200 + (j % 4) * 4
            P.op("act", lambda e: e.activation(
                out=sqj3, in_=xs2[s], func=AF.Square, accum_out=small[:, cb:cb + 1]),
                reads=[f"xs2{s}"], writes=["sqj3", f"ss3{j % 4}"])

        def p5_back(j):
            s = j % 4
            so = j % 2
            cb = 200 + (j % 4) * 4
            P.op("dve", lambda e: e.tensor_scalar(
                out=small[:, cb + 1:cb + 2], in0=small[:, cb:cb + 1], scalar1=1.0 / D, scalar2=EPS,
                op0=ALU.mult, op1=ALU.add),
                reads=[f"ss3{j % 4}"], writes=[f"sq3{j % 4}"])
            P.op("pool", lambda e: e.tensor_tensor(
                out=small[:, cb + 2:cb + 3], in0=small[:, cb + 1:cb + 2], in1=neghalf, op=ALU.pow),
                reads=[f"sq3{j % 4}", "neghalf"], writes=[f"rs3{j % 4}"])
            P.op("dve", lambda e: e.scalar_tensor_tensor(
                out=yo[so], in0=xs2[s], scalar=small[:, cb + 2:cb + 3], in1=gfin, op0=ALU.mult, op1=ALU.mult),
                reads=[f"xs2{s}", f"rs3{j % 4}", "gfin"], writes=[f"yo{so}"])
            P.dma("pool", f"o{so}", out_d[j * 128:(j + 1) * 128, :], yo[so], reads=[f"yo{so}"], writes=[f"out{j}"])

        for j in range(17):
            if j < 16:
                p5_front(j)
            if j >= 1:
                p5_back(j - 1)
        P.barrier()

        with nc.Block() as block:
            @block.tensor
            def _(e):
                for f in P.streams["pe"]:
                    f(e)

            @block.scalar
            def _(e):
                for f in P.streams["act"]:
                    f(e)

            @block.vector
            def _(e):
                for f in P.streams["dve"]:
                    f(e)

            @block.gpsimd
            def _(e):
                for f in P.streams["pool"]:
                    f(e)

            @block.sync
            def _(e):
                for f in P.streams["sp"]:
                    f(e)
    return nc


def _consts():
    bf = ml_dtypes.bfloat16
    k = np.arange(128)[:, None]
    m = np.arange(128)[None, :]
    ident = (k == m).astype(np.float32)
    negM2T = -(k < m).astype(np.float32)
    Dm = (k == m + 1).astype(np.float32) - (k == m).astype(np.float32)
    dlast = np.zeros((128, 128), np.float32)
    dlast[0, 127] = 1.0
    swp = (np.arange(128) // 64) * 64 + ((np.arange(128) % 64) + 32) % 64
    permT = np.zeros((128, 128), np.float32)
    permT[swp, np.arange(128)] = 1.0
    cbf = np.concatenate([ident, negM2T, Dm, dlast, permT], axis=1).astype(bf)
    M1 = (m <= k).astype(np.float32)
    cf = np.concatenate([permT, M1, np.zeros((128, 1936), np.float32)], axis=1).astype(np.float32)
    inv = (1.0 / (np.float32(10000.0) ** (np.arange(0, 64, 2, dtype=np.float32) / np.float32(64)))).astype(np.float32)
    pos = (NTOK - 1 - np.arange(NTOK)).astype(np.float32)
    ang = (pos[None, :] * inv[:, None]).astype(np.float32)
    cos = np.cos(ang).astype(np.float32)
    sin = np.sin(ang).astype(np.float32)
    p = np.arange(128)
    ropec = cos[p % 32, :]
    sign = np.where((p % 64) < 32, -1.0, 1.0).astype(np.float32)[:, None]
    ropes = (sin[p % 32, :] * sign).astype(np.float32)
    return cbf, cf, np.ascontiguousarray(ropec), np.ascontiguousarray(ropes)


_NC_CACHE = {}


def kernel(x, meta_tokens, norm_gain, w_in, w_out, lambda_q1, lambda_k1, lambda_q2, lambda_k2,
           subln_gain, final_norm_gain):
    x = np.asarray(x, np.float32)
    B = x.shape[0]
    w = np.asarray(w_in, np.float32)[0]
    wext = np.ascontiguousarray(w)
    wout = np.ascontiguousarray(np.asarray(w_out, np.float32)[0])
    rep = lambda v, n: np.ascontiguousarray(np.broadcast_to(np.asarray(v, np.float32).reshape(1, n), (128, n)))
    gbc = rep(norm_gain[0], D)
    gfin = rep(final_norm_gain, D)
    gsub = rep(subln_gain[0], 128)
    lamv = np.ascontiguousarray(np.concatenate(
        [rep(lambda_q1[0], 64), rep(lambda_k1[0], 64), rep(lambda_q2[0], 64), rep(lambda_k2[0], 64)], axis=1))
    cbf, cf, ropec, ropes = _consts()
    metar = np.ascontiguousarray(np.asarray(meta_tokens, np.float32)[::-1])
    if "nc" not in _NC_CACHE:
        _NC_CACHE["nc"] = build_nc(DEBUG)
    nc = _NC_CACHE["nc"]
    in_maps = []
    for b in range(B):
        in_maps.append({
            "xr": np.ascontiguousarray(x[b, ::-1, :]), "metar": metar, "wext": wext, "wout": wout,
            "gbc": gbc, "gfin": gfin, "gsub": gsub, "lamv": lamv, "ropec": ropec, "ropes": ropes,
            "cbf": cbf, "cf": cf,
        })
    res = run_bass_kernel_spmd(nc, in_maps, core_ids=list(range(B)))
    outs = [np.asarray(r["out"], np.float32)[::-1] for r in res.results]
    return np.ascontiguousarray(np.stack(outs, axis=0))
```

```python
import contextlib
import numpy as np
import ml_dtypes
import concourse.bass as bass
import concourse.mybir as mybir
from concourse.bass_utils import run_bass_kernel_spmd

F32 = mybir.dt.float32
BF16 = mybir.dt.bfloat16
AF = mybir.ActivationFunctionType
ALU = mybir.AluOpType
AX = mybir.AxisListType

SEQ = 2048
NMETA = 16
NTOK = SEQ + NMETA
D = 1024
NB = 17
EPS = 1e-6
LAMBDA_INIT = 0.8 - 0.6 * float(np.exp(-0.3 * 0))

DEBUG = False
STOP_AFTER = 99
DF_META = True
DFP_LEVEL = 9
DF_ODB = True
DF_DSK = 4
DF_ZP = 4
DF_LEVEL = 9
DMAT_MOD = 2
EPI_ACT_FROM_TILE = 16


def blk_rows(b):
    return 128 if b < 16 else 16


class Prog:
    ENG = ("pe", "act", "dve", "pool", "sp")

    def __init__(self, nc, stack):
        self.nc = nc
        self.stack = stack
        self.streams = {e: [] for e in self.ENG}
        self.sems = {}
        self.semval = {}
        for e in self.ENG:
            self.sems[e] = stack.enter_context(nc.semaphore("s_" + e))
            self.semval[e] = 0
        self.known = {e: {} for e in self.ENG}
        self.snap = {}
        self.res = {}
        self.ninstr = 0
        self.enabled = True
        self.nbar = 0

    def dma_sem(self, key):
        if key not in self.sems:
            self.sems[key] = self.stack.enter_context(self.nc.semaphore("d_" + key))
            self.semval[key] = 0
        return key

    def _wait(self, eng, ev):
        key, val = ev
        if self.known[eng].get(key, 0) >= val:
            return
        self.known[eng][key] = val
        inherited = self.snap.get((key, val))
        if inherited:
            kn = self.known[eng]
            for k2, v2 in inherited.items():
                if k2 != eng and kn.get(k2, 0) < v2:
                    kn[k2] = v2
        sem = self.sems[key]
        self.streams[eng].append(lambda e, sem=sem, val=val: e.wait_ge(sem, val))

    def _deps(self, eng, reads, writes):
        evs = []
        for r in reads:
            st = self.res.get(r)
            if st and st["w"]:
                evs.append((st["w"], "raw"))
        for w in writes:
            st = self.res.get(w)
            if st:
                if st["w"]:
                    evs.append((st["w"], "waw"))
                for k, v in st["r"].items():
                    evs.append(((k, v), "war"))
        for ev, kind in evs:
            key = ev[0]
            if key == eng:
                if eng in ("pe", "sp") or kind == "war":
                    continue
            self._wait(eng, ev)

    def _commit(self, ev, reads, writes):
        for r in reads:
            st = self.res.setdefault(r, {"w": None, "r": {}})
            k, v = ev
            if st["r"].get(k, 0) < v:
                st["r"][k] = v
        for w in writes:
            self.res[w] = {"w": ev, "r": {}}

    def op(self, eng, fn, reads=(), writes=()):
        if not self.enabled:
            return
        self._deps(eng, reads, writes)
        self.semval[eng] += 1
        ev = (eng, self.semval[eng])
        self.snap[ev] = dict(self.known[eng])
        sem = self.sems[eng]
        self.streams[eng].append(lambda e, fn=fn, sem=sem: fn(e).then_inc(sem, 1))
        self._commit(ev, reads, writes)
        self.ninstr += 1

    def dma(self, eng, slot, out, in_, reads=(), writes=(), transpose=False):
        if not self.enabled:
            return
        key = self.dma_sem(slot)
        self._deps(eng, reads, writes)
        self.semval[key] += 16
        ev = (key, self.semval[key])
        sem = self.sems[key]
        if transpose:
            self.streams[eng].append(
                lambda e, out=out, in_=in_, sem=sem: e.dma_start_transpose(out=out, in_=in_).then_inc(sem, 16))
        else:
            self.streams[eng].append(
                lambda e, out=out, in_=in_, sem=sem: e.dma_start(out=out, in_=in_).then_inc(sem, 16))
        self._commit(ev, reads, writes)
        self.ninstr += 1

    def barrier(self):
        if not self.enabled:
            return
        self.nbar += 1
        if self.nbar > STOP_AFTER:
            self.enabled = False
        for e in self.ENG:
            for k, v in self.semval.items():
                if k != e and v > 0:
                    self._wait(e, (k, v))
        self.res = {}


def build_nc(dbg=False):
    nc = bass.Bass("TRN2", target_bir_lowering=False)

    def din(name, shape, dt=F32):
        return nc.dram_tensor(name, list(shape), dt, kind="ExternalInput").ap()

    xr = din("xr", [SEQ, D])
    metar = din("metar", [NMETA, D])
    wext = din("wext", [D, 4096])
    wout = din("wout", [D, D])
    gbc_d = din("gbc", [128, D])
    gfin_d = din("gfin", [128, D])
    gsub_d = din("gsub", [128, 128])
    lamv_d = din("lamv", [128, 256])
    ropec_d = din("ropec", [128, NTOK])
    ropes_d = din("ropes", [128, NTOK])
    cbf_d = din("cbf", [128, 640], BF16)
    cf_d = din("cf", [128, 2192])
    out_d = nc.dram_tensor("out", [SEQ, D], F32, kind="ExternalOutput").ap()
    if dbg:
        dbg_d = nc.dram_tensor("dbg", [128, 8 * NTOK], F32, kind="ExternalOutput").ap()

    with contextlib.ExitStack() as st:
        def sb(name, shape, dt):
            return st.enter_context(nc.sbuf_tensor(name, list(shape), dt))

        uT = sb("uT", [128, 8, NTOK], BF16)
        A1 = sb("A1", [128, 4, NTOK], BF16)
        A2 = sb("A2", [128, 4, NTOK], BF16)
        A3 = sb("A3", [128, NB, 520], BF16)
        A4 = sb("A4", [128, 16, 512], BF16)
        A5flat = sb("A5", [128, NB * 512], BF16)
        A5 = A5flat[:, :].rearrange("p (b n) -> p b n", n=512)
        A5f = A5flat.bitcast(F32)
        mixT = sb("mixT", [128, 8, SEQ], BF16)
        scr = sb("scr", [128, 16384], BF16)
        scrf = scr.bitcast(F32)
        tmpf = sb("tmpf", [128, 2048], F32)
        gbc = sb("gbcs", [128, D], F32)
        cbf = sb("cbfs", [128, 640], BF16)
        cf = sb("cfs", [128, 2192], F32)
        gsub = sb("gsubs", [128, 128], F32)
        lamv = sb("lamvs", [128, 256], F32)
        small = sb("small", [128, 256], F32)
        ps = [st.enter_context(nc.psum_tensor(f"ps{i}", [128, 512], F32)) for i in range(8)]
        psb = [p.bitcast(BF16) for p in ps]

        P = Prog(nc, st)

        ident_bf = cbf[:, 0:128]
        negM2T = cbf[:, 128:256]
        Dmat = cbf[:, 256:384]
        dlast = cbf[0:1, 384:512]
        permT_bf = cbf[:, 512:640]
        permT = cf[:, 0:128]
        M1Z = cf[:, 128:2192]

        P.dma("sp", "c_gbc", gbc[:, :], gbc_d, writes=["gbc"])
        P.dma("sp", "c_cbf", cbf[:, :], cbf_d, writes=["cbf"])
        P.dma("sp", "c_cf", cf[:, :], cf_d, writes=["cf"])
        P.dma("sp", "c_gsub", gsub[:, :], gsub_d, writes=["gsub"])
        P.dma("sp", "c_lamv", lamv[:, :], lamv_d, writes=["lamv"])

        P.op("pool", lambda e: e.memset(small[:, 65:66], EPS), writes=["epsc"])
        epsc = small[:, 65:66]
        P.op("pool", lambda e: e.memset(small[:, 66:68], -0.5), writes=["neghalf"])
        neghalf = small[:, 66:67]
        neghalf2 = small[:, 66:68]
        P.op("dve", lambda e: e.tensor_tensor(out=tmpf[:, 0:64], in0=lamv[:, 0:64], in1=lamv[:, 64:128], op=ALU.mult),
             reads=["lamv"], writes=["lt0"])
        P.op("dve", lambda e: e.reduce_sum(out=small[:, 60:61], in_=tmpf[:, 0:64], axis=AX.X),
             reads=["lt0"], writes=["s1"])
        P.op("dve", lambda e: e.tensor_tensor(out=tmpf[:, 64:128], in0=lamv[:, 128:192], in1=lamv[:, 192:256], op=ALU.mult),
             reads=["lamv"], writes=["lt1"])
        P.op("dve", lambda e: e.reduce_sum(out=small[:, 61:62], in_=tmpf[:, 64:128], axis=AX.X),
             reads=["lt1"], writes=["s2"])
        P.op("act", lambda e: e.activation(out=small[:, 62:64], in_=small[:, 60:62], func=AF.Exp),
             reads=["s1", "s2"], writes=["e12"])
        P.op("dve", lambda e: e.tensor_tensor(out=small[:, 64:65], in0=small[:, 63:64], in1=small[:, 62:63], op=ALU.subtract),
             reads=["e12"], writes=["nl0"])
        P.op("dve", lambda e: e.tensor_scalar(out=small[:, 64:65], in0=small[:, 64:65], scalar1=-LAMBDA_INIT, scalar2=None, op0=ALU.add),
             reads=["nl0"], writes=["neglam"])
        neglam = small[:, 64:65]
        P.op("dve", lambda e: e.tensor_scalar(out=gsub[:, :], in0=gsub[:, :], scalar1=1.0 - LAMBDA_INIT, scalar2=None, op0=ALU.mult),
             reads=["gsub"], writes=["gsub"])

        wbuf = [scr[:, 8192 + s * 4096: 8192 + (s + 1) * 4096].rearrange("p (c n) -> p c n", n=512) for s in range(2)] + \
               [scr[:, s * 4096:(s + 1) * 4096].rearrange("p (c n) -> p c n", n=512) for s in range(2)]

        def load_w(slot, g):
            src = wext[:, g * 512:(g + 1) * 512].rearrange("(c p) n -> p c n", p=128)
            P.dma("pool", f"w{slot}", wbuf[slot], src, writes=[f"w{slot}"])

        load_w(0, 0)
        load_w(1, 1)

        xs = [A5f[:, s * 1024:(s + 1) * 1024] for s in range(2)]
        ub = [A5flat[:, 4096 + s * 1024: 4096 + (s + 1) * 1024] for s in range(2)]
        sqj = A5flat[:, 6144:7168]
        def p0_front(b):
            rows = blk_rows(b)
            s = b % 2
            src = xr[b * 128:(b + 1) * 128, :] if b < 16 else metar
            P.dma("sp", f"xs{s}", xs[s][:rows, :], src, writes=[f"xs{s}"])
            P.op("act", lambda e: e.activation(
                out=sqj[:rows, :], in_=xs[s][:rows, :], func=AF.Square, accum_out=small[:rows, b:b + 1]),
                reads=[f"xs{s}"], writes=["sqj", f"ss{b}"])
            P.op("dve", lambda e: e.tensor_scalar(
                out=small[:rows, 17 + b:18 + b], in0=small[:rows, b:b + 1], scalar1=1.0 / D, scalar2=EPS,
                op0=ALU.mult, op1=ALU.add),
                reads=[f"ss{b}"], writes=[f"sr{b}"])
            P.op("pool", lambda e: e.tensor_tensor(
                out=small[:rows, 34 + b:35 + b], in0=small[:rows, 17 + b:18 + b], in1=neghalf[:rows, :], op=ALU.pow),
                reads=[f"sr{b}", "neghalf"], writes=[f"rstd{b}"])
            P.op("dve", lambda e: e.scalar_tensor_tensor(
                out=ub[s][:rows, :], in0=xs[s][:rows, :], scalar=small[:rows, 34 + b:35 + b], in1=gbc[:rows, :],
                op0=ALU.mult, op1=ALU.mult),
                reads=[f"xs{s}", f"rstd{b}", "gbc"], writes=[f"ub{s}"])

        def p0_tr(b):
            rows = blk_rows(b)
            s = b % 2
            pi = 4 + b % 2
            for c in range(8):
                P.op("pe", lambda e, c=c: e.transpose(
                    out=psb[pi][:, c * 128:c * 128 + rows], in_=ub[s][:rows, c * 128:(c + 1) * 128],
                    identity=ident_bf[:rows, :rows]),
                    reads=[f"ub{s}", "cbf"], writes=[f"ps{pi}"])

        def p0_back(b):
            rows = blk_rows(b)
            pi = 4 + b % 2
            srcv = psb[pi][:, :].rearrange("p (c t) -> p c t", t=128)[:, :, :rows]
            dstv = uT[:, :, b * 128:b * 128 + rows]
            if b % 2 == 0:
                P.op("act", lambda e: e.activation(out=dstv, in_=srcv, func=AF.Copy),
                     reads=[f"ps{pi}"], writes=[f"uT{b}"])
            else:
                P.op("dve", lambda e: e.tensor_copy(out=dstv, in_=srcv),
                     reads=[f"ps{pi}"], writes=[f"uT{b}"])


        TR = [(0, 512), (512, 512), (1024, 512), (1536, 512), (2048, 16)]
        pcount = [0]
        pspool = [[0, 1, 2, 3, 6, 7]]

        def nextps(pool):
            i = pool[pcount[0] % len(pool)]
            pcount[0] += 1
            return i

        def feat_unit(slot, evac, cc, t0, n):
            pi = nextps(pspool[0])
            ub_ = [f"uT{b}" for b in range(t0 // 128, (t0 + n + 127) // 128)]
            for dc in range(8):
                P.op("pe", lambda e, dc=dc: e.matmul(
                    out=ps[pi][:, :n], lhsT=wbuf[slot][:, dc, cc * 128:(cc + 1) * 128],
                    rhs=uT[:, dc, t0:t0 + n], start=(dc == 0), stop=(dc == 7)),
                    reads=[f"w{slot}"] + ub_, writes=[f"ps{pi}"])
            evac(cc, t0, n, pi)

        def tok_unit(slot, evac, b):
            rows = blk_rows(b)
            pi = nextps(pspool[0])
            for dc in range(8):
                P.op("pe", lambda e, dc=dc: e.matmul(
                    out=ps[pi][:rows, :512], lhsT=uT[:, dc, b * 128:b * 128 + rows],
                    rhs=wbuf[slot][:, dc, :], start=(dc == 0), stop=(dc == 7)),
                    reads=[f"w{slot}", f"uT{b}"], writes=[f"ps{pi}"])
            evac(b, rows, pi)

        def proj_feat(slot, evac, tr=None):
            for cc in range(4):
                for (t0, n) in (tr or TR):
                    feat_unit(slot, evac, cc, t0, n)

        def proj_tok(slot, evac, nblk):
            for b in range(nblk):
                tok_unit(slot, evac, b)

        flip = [0]

        def evac_copy_to(dst3):
            def f(cc, t0, n, pi):
                flip[0] ^= 1
                if flip[0]:
                    P.op("act", lambda e: e.activation(out=dst3[:, cc, t0:t0 + n], in_=ps[pi][:, :n], func=AF.Copy),
                         reads=[f"ps{pi}"], writes=[])
                else:
                    P.op("dve", lambda e: e.tensor_copy(out=dst3[:, cc, t0:t0 + n], in_=ps[pi][:, :n]),
                         reads=[f"ps{pi}"], writes=[])
            return f

        QP = [A1[:, h, 0:SEQ] for h in range(4)] + [mixT[:, 4 + h, :] for h in range(4)]
        for h in range(8):
            zr = slice(64, 128) if h % 2 == 0 else slice(0, 64)
            P.op("pool", lambda e, h=h, zr=zr: e.memset(QP[h][zr, :], 0.0), writes=[f"qpz{h}"])

        def evac_q(cc, t0, n, pi):
            if t0 >= SEQ:
                return
            P.op("act", lambda e: e.activation(out=QP[2 * cc][0:64, t0:t0 + n], in_=ps[pi][0:64, :n], func=AF.Copy),
                 reads=[f"ps{pi}"], writes=[])
            P.op("dve", lambda e: e.tensor_copy(out=QP[2 * cc + 1][64:128, t0:t0 + n], in_=ps[pi][64:128, :n]),
                 reads=[f"ps{pi}"], writes=[])
        def evac_v(b, rows, pi):
            P.op("act", lambda e: e.activation(out=A3[:rows, b, 0:512], in_=ps[pi][:rows, :512], func=AF.Copy),
                 reads=[f"ps{pi}"], writes=[f"v{b}"])

        def evac_g(b, rows, pi):
            P.op("act", lambda e: e.activation(out=A4[:rows, b, :], in_=ps[pi][:rows, :512], func=AF.Silu),
                 reads=[f"ps{pi}"], writes=[])

        load_w(2, 2)
        load_w(3, 3)
        evac_k = evac_copy_to(A2)
        fifo = []
        p0_front(0)
        p0_tr(0)
        for b in range(1, NB + 1):
            if b < NB:
                p0_front(b)
            if b >= 1:
                p0_back(b - 1)
                bb = b - 1
                if bb % 4 == 3 or bb == 16:
                    r = bb // 4
                    t0, n = TR[r]
                    for cc in range(4):
                        if r < 4:
                            fifo.append(lambda cc=cc, t0=t0, n=n: feat_unit(0, evac_q, cc, t0, n))
                        fifo.append(lambda cc=cc, t0=t0, n=n: feat_unit(1, evac_k, cc, t0, n))
                    for b2 in range(4 * r, min(4 * r + 4, NB)):
                        fifo.append(lambda b2=b2: tok_unit(2, evac_v, b2))
                        if b2 < 16:
                            fifo.append(lambda b2=b2: tok_unit(3, evac_g, b2))
            for _ in range(4):
                if fifo:
                    fifo.pop(0)()
            if b < NB:
                p0_tr(b)
        while fifo:
            fifo.pop(0)()
        P.barrier()
        P.op("pool", lambda e: e.memset(A5[:, 16, :], 0.0), writes=["dvmeta"])
        for b in range(NB):
            rows = blk_rows(b)
            pi = nextps([0, 1, 2, 3])
            last = (b == NB - 1)
            P.op("pe", lambda e, b=b, rows=rows, pi=pi, last=last: e.matmul(
                out=ps[pi][:rows, :512], lhsT=Dmat[:rows, :rows], rhs=A3[:rows, b, 0:512], start=True, stop=last),
                reads=[f"v{b}", "cbf"], writes=[f"ps{pi}"])
            if not last:
                P.op("pe", lambda e, b=b, pi=pi: e.matmul(
                    out=ps[pi][:128, :512], lhsT=dlast, rhs=A3[0:1, b + 1, 0:512], start=False, stop=True),
                    reads=[f"v{b + 1}"], writes=[f"ps{pi}"])
            P.op("dve", lambda e, b=b, rows=rows, pi=pi: e.tensor_copy(out=A5[:rows, b, :], in_=ps[pi][:rows, :512]),
                 reads=[f"ps{pi}"] + (["dvmeta"] if b == 16 else []), writes=[f"dv{b}"] + (["dvmeta"] if b == 16 else []))

        keepL = [scrf[:, i * 2064:(i + 1) * 2064] for i in range(2)]
        Pb = [scr[:, 8256 + i * 2176: 8256 + (i + 1) * 2176] for i in range(3)]
        tmpb = tmpf.bitcast(BF16)
        if DMAT_MOD:
            PTDs = [tmpb[:, 512 + i * 1152: 512 + (i + 1) * 1152].rearrange("p (b t) -> p b t", t=128) for i in range(2)]
            PTb = [scr[:, 14784 + i * 512: 14784 + (i + 1) * 512] for i in range(3)] + \
                  [tmpb[:, 2816 + i * 512: 2816 + (i + 1) * 512] for i in range(2)]
        else:
            PTb = [scr[:, 14784 + i * 512: 14784 + (i + 1) * 512] for i in range(3)] + \
                  [tmpb[:, 512 + i * 512: 512 + (i + 1) * 512] for i in range(7)]
        NPT = len(PTb)

        def is_dma_chunk(k, c):
            return bool(DMAT_MOD) and c >= 2
        for i in range(3):
            P.op("pool", lambda e, i=i: e.memset(Pb[i][:, :], 0.0), writes=[f"Pb{i}"])
        mixtok = [tmpf.bitcast(BF16)[:, 0:512]]
        cnt = {"z": 0, "k": 0, "p": 0, "t": 0, "pt": 0, "ev": 0}

        def rot(name, n):
            i = cnt[name] % n
            cnt[name] += 1
            return i

        def sb_chunks(j):
            blocks = list(range(j, 16))
            chunks = []
            while blocks:
                cb = blocks[:4]
                blocks = blocks[4:]
                chunks.append([(kb, 128) for kb in cb])
            if len(chunks[-1]) < 4:
                chunks[-1].append((16, 16))
            else:
                chunks.append([(16, 16)])
            return chunks

        items = [(j, h) for j in range(16) for h in range(8)]
        NI = len(items)
        ptslot = {}

        def st_z(k, c):
            j, h = items[k]
            ch = sb_chunks(j)[c]
            cc, po = h // 2, (h % 2) * 64
            n = sum(r for _, r in ch)
            t0 = ch[0][0] * 128
            off = t0 - 128 * j
            zi = rot("z", 2)
            ks = k % 2
            P.op("pe", lambda e: e.matmul(
                out=ps[zi][:, :n], lhsT=QP[h][:, j * 128:(j + 1) * 128],
                rhs=A2[:, cc, t0:t0 + n], start=True, stop=True),
                reads=[], writes=[f"ps{zi}"])
            P.op("act", lambda e: e.activation(
                out=keepL[ks][:, off:off + n], in_=ps[zi][:, :n], func=AF.Sigmoid, scale=-0.125),
                reads=[f"ps{zi}"], writes=[f"keep{ks}"])

        def st_scan(k):
            j, h = items[k]
            ntot = NTOK - 128 * j
            ks, pslot = k % 2, k % 3
            P.op("dve", lambda e: e.tensor_tensor_scan(
                out=Pb[pslot][:, :ntot], data0=keepL[ks][:, :ntot], data1=M1Z[:, :ntot], initial=1.0,
                op0=ALU.mult, op1=ALU.max),
                reads=[f"keep{ks}", "cf"], writes=[f"Pb{pslot}"])

        def st_t(k, c):
            j, h = items[k]
            ch = sb_chunks(j)[c]
            pslot = k % 3
            if is_dma_chunk(k, c):
                if c == 2:
                    nfar = 17 - j - 8
                    P.dma("sp", "dmat", PTDs[k % 2][:, 0:nfar, :], Pb[pslot][:, 1024:1024 + nfar * 128],
                          reads=[f"Pb{pslot}"], writes=[f"PTD{k % 2}", "dmat_serial"], transpose=True)
                return
            ti = 2 + rot("t", 2)
            off = ch[0][0] * 128 - 128 * j
            col = off
            for bi, (kb, r) in enumerate(ch):
                P.op("pe", lambda e, bi=bi, col=col: e.transpose(
                    out=psb[ti][:, bi * 128:(bi + 1) * 128], in_=Pb[pslot][:, col:col + 128],
                    identity=ident_bf),
                    reads=[f"Pb{pslot}", "cbf"], writes=[f"ps{ti}"])
                col += r
            pti = rot("pt", NPT)
            ptslot[(k, c)] = pti
            nb_ = len(ch)
            if ((not DMAT_MOD) and rot("ev", 4) == 3) or (DMAT_MOD and j >= 11 and c == 1):
                P.op("dve", lambda e: e.tensor_copy(out=PTb[pti][:, :nb_ * 128], in_=psb[ti][:, :nb_ * 128]),
                     reads=[f"ps{ti}"], writes=[f"PT{pti}"])
            else:
                P.op("act", lambda e: e.activation(
                    out=PTb[pti][:, :nb_ * 128], in_=psb[ti][:, :nb_ * 128], func=AF.Copy),
                    reads=[f"ps{ti}"], writes=[f"PT{pti}"])

        def st_av(k, c):
            j, h = items[k]
            ch = sb_chunks(j)[c]
            oi = 6 + (j % 2)
            hc = slice(h * 64, (h + 1) * 64)
            if is_dma_chunk(k, c):
                for bi, (kb, r) in enumerate(ch):
                    P.op("pe", lambda e, bi=bi, kb=kb: e.matmul(
                        out=ps[oi][:, hc], lhsT=PTDs[k % 2][:, kb - j - 8, :],
                        rhs=A5[:, kb, hc], start=(c == 0 and bi == 0), stop=False),
                        reads=[f"PTD{k % 2}", f"dv{kb}"], writes=[f"ps{oi}"])
                return
            pti = ptslot[(k, c)]
            for bi, (kb, r) in enumerate(ch):
                P.op("pe", lambda e, bi=bi, kb=kb: e.matmul(
                    out=ps[oi][:, hc], lhsT=PTb[pti][:, bi * 128:(bi + 1) * 128],
                    rhs=A5[:, kb, hc], start=(c == 0 and bi == 0), stop=False),
                    reads=[f"PT{pti}", f"dv{kb}"], writes=[f"ps{oi}"])

        def st_tail(k):
            j, h = items[k]
            oi = 6 + (j % 2)
            hc = slice(h * 64, (h + 1) * 64)
            P.op("pe", lambda e: e.matmul(
                out=ps[oi][:, hc], lhsT=negM2T, rhs=A5[:, j, hc], start=False, stop=False),
                reads=["cbf", f"dv{j}"], writes=[f"ps{oi}"])
            P.op("pe", lambda e: e.matmul(
                out=ps[oi][:, hc], lhsT=ident_bf, rhs=A3[:, j, hc], start=False, stop=True),
                reads=["cbf"], writes=[f"ps{oi}"])
            if h == 7:
                P.op("dve", lambda e: e.tensor_tensor(
                    out=mixtok[0][:, :], in0=ps[oi][:, :512], in1=A4[:, j, :], op=ALU.mult),
                    reads=[f"ps{oi}"], writes=["mixtok"])
                ti = 4 + (j % 2)
                for c in range(4):
                    P.op("pe", lambda e, c=c: e.transpose(
                        out=psb[ti][:, c * 128:(c + 1) * 128], in_=mixtok[0][:, c * 128:(c + 1) * 128], identity=ident_bf),
                        reads=["mixtok", "cbf"], writes=[f"ps{ti}"])
                P.op("act", lambda e: e.activation(
                    out=mixT[:, 0:4, j * 128:(j + 1) * 128],
                    in_=psb[ti][:, 0:512].rearrange("p (c t) -> p c t", t=128), func=AF.Copy),
                    reads=[f"ps{ti}"], writes=[])

        def nch(k):
            return len(sb_chunks(items[k][0])) if 0 <= k < NI else 0

        SK = 2
        for k in range(NI + SK + 1):
            if 0 <= k - 1 < NI:
                st_scan(k - 1)
            na, nt_, nv_ = nch(k), nch(k - SK), nch(k - SK - 1)
            for c in range(na):
                st_z(k, c)
            for c in range(max(nt_, nv_)):
                if c < nt_:
                    st_t(k - SK, c)
                if c < nv_:
                    st_av(k - SK - 1, c)
            if 0 <= k - SK - 1 < NI:
                st_tail(k - SK - 1)
            if k == NI:
                for slot, g in ((2, 6), (3, 7)):
                    src = wext[:, g * 512:(g + 1) * 512].rearrange("(c p) n -> p c n", p=128)
                    P.dma("pool", f"w{slot}", wbuf[slot], src, writes=[f"w{slot}", "keep0", "keep1", "dmat_serial"])
        P.barrier()
        load_w(0, 4)
        load_w(1, 5)

        pspool[0] = [0, 1, 2, 3, 4, 5, 6, 7]
        ropec = A5f[:, 0:NTOK]
        ropes = A5f[:, NTOK:2 * NTOK]
        P.dma("sp", "c_rc", ropec, ropec_d, writes=["ropec"])
        P.dma("sp", "c_rs", ropes, ropes_d, writes=["ropes"])
        vaug4 = A3[:, :, :].rearrange("p b (h e) -> p b h e", e=130)
        P.op("pool", lambda e: e.memset(vaug4[:, :, :, 128:129], 1.0), writes=["vones"])
        P.op("pool", lambda e: e.memset(vaug4[:, :, :, 129:130], 0.0), writes=["vzero"])

        QPD = [A1[:, h, 0:SEQ] for h in range(4)] + [mixT[:, 4 + h, :] for h in range(4)]
        for h in range(8):
            zr = slice(64, 128) if h % 2 == 0 else slice(0, 64)
            P.op("pool", lambda e, h=h, zr=zr: e.memset(QPD[h][zr, :], 0.0), writes=[f"qpdz{h}"])

        qhb = [scr[:, i * 512:(i + 1) * 512] for i in range(4)]
        qlb = [scr[:, 2048 + i * 512: 2048 + (i + 1) * 512] for i in range(4)]

        def proj_rope(dst3, slot, padded=False):
            units = [(cc, t0, n) for cc in range(4) for (t0, n) in (TR[:4] if padded else TR)]
            pend = None

            def finish(u):
                cc, t0, n, pa, qi = u
                pb = nextps([0, 1, 2, 3, 4, 5, 6, 7])
                P.op("pe", lambda e: e.matmul(out=ps[pb][:, :n], lhsT=permT_bf, rhs=qhb[qi][:, :n], start=True, stop=False),
                     reads=[f"qs{qi}", "cbf"], writes=[f"ps{pb}"])
                P.op("pe", lambda e: e.matmul(out=ps[pb][:, :n], lhsT=permT_bf, rhs=qlb[qi][:, :n], start=False, stop=True),
                     reads=[f"ql{qi}", "cbf"], writes=[f"ps{pb}"])
                ta = rot("k", 2)
                t1 = tmpf[:, ta * 1024: ta * 1024 + n]
                t2 = tmpf[:, ta * 1024 + 512: ta * 1024 + 512 + n]
                P.op("dve", lambda e: e.tensor_tensor(out=t1, in0=ps[pa][:, :n], in1=ropec[:, t0:t0 + n], op=ALU.mult),
                     reads=[f"ps{pa}", "ropec"], writes=[f"t1{ta}"])
                P.op("dve", lambda e: e.tensor_tensor(out=t2, in0=ps[pb][:, :n], in1=ropes[:, t0:t0 + n], op=ALU.mult),
                     reads=[f"ps{pb}", "ropes"], writes=[f"t2{ta}"])
                if padded:
                    P.op("pool", lambda e: e.tensor_tensor(
                        out=QPD[2 * cc][0:64, t0:t0 + n], in0=t1[0:64, :], in1=t2[0:64, :], op=ALU.add),
                        reads=[f"t1{ta}", f"t2{ta}", f"qpdz{2 * cc}"], writes=[f"qpa{ta}"])
                    P.op("pool", lambda e: e.tensor_tensor(
                        out=QPD[2 * cc + 1][64:128, t0:t0 + n], in0=t1[64:128, :], in1=t2[64:128, :], op=ALU.add),
                        reads=[f"t1{ta}", f"t2{ta}", f"qpdz{2 * cc + 1}"], writes=[f"qpb{ta}"])
                else:
                    P.op("pool", lambda e: e.tensor_tensor(
                        out=dst3[:, cc, t0:t0 + n], in0=t1, in1=t2, op=ALU.add),
                        reads=[f"t1{ta}", f"t2{ta}"], writes=[f"qpa{ta}"])

            for (cc, t0, n) in units:
                pa = nextps([0, 1, 2, 3, 4, 5, 6, 7])
                for dc in range(8):
                    P.op("pe", lambda e, dc=dc, pa=pa, cc=cc, t0=t0, n=n: e.matmul(
                        out=ps[pa][:, :n], lhsT=wbuf[slot][:, dc, cc * 128:(cc + 1) * 128],
                        rhs=uT[:, dc, t0:t0 + n], start=(dc == 0), stop=(dc == 7)),
                        reads=[f"w{slot}"], writes=[f"ps{pa}"])
                qi = rot("p", 4)
                P.op("act", lambda e, pa=pa, qi=qi, n=n: e.activation(out=qhb[qi][:, :n], in_=ps[pa][:, :n], func=AF.Copy),
                     reads=[f"ps{pa}"], writes=[f"qs{qi}"])
                P.op("dve", lambda e, pa=pa, qi=qi, n=n: e.tensor_tensor(
                    out=qlb[qi][:, :n], in0=ps[pa][:, :n], in1=qhb[qi][:, :n], op=ALU.subtract),
                    reads=[f"ps{pa}", f"qs{qi}"], writes=[f"ql{qi}"])
                if pend is not None:
                    finish(pend)
                pend = (cc, t0, n, pa, qi)
            finish(pend)

        def evac_vd(b, rows, pi):
            P.op("act", lambda e: e.activation(
                out=vaug4[:rows, b, :, 0:128], in_=ps[pi][:rows, :512].rearrange("p (h e) -> p h e", e=128), func=AF.Copy),
                reads=[f"ps{pi}"], writes=[])
        if DFP_LEVEL >= 1:
            proj_tok(2, evac_vd, NB)
        if DFP_LEVEL >= 2:
            proj_tok(3, evac_g, 16)
        if DFP_LEVEL >= 3:
            proj_rope(None, 0, padded=True)
        if DFP_LEVEL >= 4:
            proj_rope(A2, 1)
        P.barrier()

        wo = scr[:, 8192:16384].rearrange("p (c n) -> p c n", n=1024)
        P.dma("pool", "w0", wo, wout.rearrange("(c p) n -> p c n", p=128), writes=["wo"])
        gfin = A5f[:, 0:1024]
        P.dma("sp", "c_rc", gfin, gfin_d, writes=["gfin"])
        ETb = [scr[:, i * 512:(i + 1) * 512] for i in range(8)]
        of32 = [tmpf[:, i * 128:(i + 1) * 128] for i in range(2)]
        yf32 = [tmpf[:, 256 + i * 128: 256 + (i + 1) * 128] for i in range(2)]
        sqj2 = tmpf[:, 0:128]
        mixtok2s = [scr[:, 4096:4608], scr[:, 4608:5120]]
        sc = {"i": 0}
        dsteps = []
        for j in range(16):
            for hp in range(2):
                st_l = [[kb] for kb in range(j, NB)]
                for si, kbs in enumerate(st_l):
                    dsteps.append({"j": j, "hp": hp, "kbs": kbs, "last": si == len(st_l) - 1,
                                   "par": ((2 * j + hp) % 2) if DF_ODB else 1})

        def d_zexp(stp):
            j, hp, kbs = stp["j"], stp["hp"], stp["kbs"]
            rows = blk_rows(kbs[0])
            zb = [rot("z", DF_ZP) for _ in kbs]
            eb = [rot("k", 8) for _ in kbs]
            stp["eb"] = eb
            for bi, kb in enumerate(kbs):
                for p2 in range(2):
                    hc0 = 4 * hp + 2 * p2
                    cc = hc0 // 2
                    if hc0 < 4:
                        rhs2 = A1[:, hc0:hc0 + 2, j * 128:(j + 1) * 128]
                    else:
                        rhs2 = mixT[:, hc0:hc0 + 2, j * 128:(j + 1) * 128]
                    P.op("pe", lambda e, zi=zb[bi], p2=p2, cc=cc, kb=kb, rhs2=rhs2: e.matmul(
                        out=ps[zi][:rows, p2 * 256:(p2 + 1) * 256], lhsT=A2[:, cc, kb * 128:kb * 128 + rows],
                        rhs=rhs2, start=True, stop=True),
                        reads=[f"qpd{j}"], writes=[f"ps{zb[bi]}"])
                zi, ei = zb[bi], eb[bi]
                P.op("act", lambda e, zi=zi, ei=ei: e.activation(
                    out=ETb[ei][:rows, :], in_=ps[zi][:rows, :], func=AF.Exp, scale=0.125),
                    reads=[f"ps{zi}"], writes=[f"ET{ei}"])
                if kb == j:
                    rect = ETb[ei][0:64, :].rearrange("p (i t) -> p i t", t=128)[:, :, 64:128]
                    P.op("pool", lambda e, rect=rect: e.memset(rect, 0.0),
                         reads=[], writes=[f"ET{ei}"])

        def d_av(stp):
            j, hp, kbs, eb = stp["j"], stp["hp"], stp["kbs"], stp["eb"]
            rows = blk_rows(kbs[0])
            for bi, kb in enumerate(kbs):
                for i in range(4):
                    head = (4 * hp + i) // 2
                    ei = eb[bi]
                    col = i * 128
                    ob, oc = 4 + 2 * stp["par"] + i // 2, (i % 2) * 256
                    P.op("pe", lambda e, i=i, ei=ei, col=col, kb=kb, head=head, ob=ob, oc=oc: e.matmul(
                        out=ps[ob][:, oc:oc + 129], lhsT=ETb[ei][:rows, col:col + 128],
                        rhs=A3[:rows, kb, head * 130:head * 130 + 129],
                        start=(kb == j and i % 2 == 0), stop=(kb == NB - 1), skip_group_check=True),
                        reads=[f"ET{ei}", "vones"], writes=[f"ps{ob}"])
            if stp["last"]:
                d_epilogue(j, hp, stp["par"])

        def d_epilogue(j, hp, par):
            k = sc["i"] % 4
            sc["i"] += 1
            q = k % 2
            cb = 80 + k * 16
            banks = [4 + 2 * par, 5 + 2 * par]
            tb = 512 + q * 768
            t1b = [tmpf[:, tb + hl * 128: tb + (hl + 1) * 128] for hl in range(2)]
            ofb = [tmpf[:, tb + 256 + hl * 128: tb + 256 + (hl + 1) * 128] for hl in range(2)]
            yfb = [tmpf[:, tb + 512 + hl * 128: tb + 512 + (hl + 1) * 128] for hl in range(2)]
            for hl in range(2):
                pb_ = banks[hl]
                P.op("dve", lambda e, pb_=pb_, hl=hl: e.reciprocal(
                    out=small[:, cb + 2 * hl:cb + 2 * hl + 2], in_=ps[pb_][:, 128:512:256]),
                    reads=[f"ps{pb_}"], writes=[f"rz{k}_{hl}"])
            P.op("dve", lambda e: e.tensor_scalar(
                out=small[:, cb + 4:cb + 6], in0=small[:, cb + 1:cb + 4:2], scalar1=neglam, scalar2=None, op0=ALU.mult),
                reads=[f"rz{k}_0", f"rz{k}_1", "neglam"], writes=[f"c1{k}"])
            act_t1 = j >= EPI_ACT_FROM_TILE
            for hl in range(2):
                pb_ = banks[hl]
                if act_t1:
                    P.op("act", lambda e, pb_=pb_, hl=hl: e.activation(
                        out=t1b[hl], in_=ps[pb_][:, 256:384], func=AF.Copy, scale=small[:, cb + 4 + hl:cb + 5 + hl]),
                        reads=[f"ps{pb_}", f"c1{k}"], writes=[f"t1e{q}_{hl}"])
                else:
                    P.op("dve", lambda e, pb_=pb_, hl=hl: e.tensor_scalar(
                        out=t1b[hl], in0=ps[pb_][:, 256:384], scalar1=small[:, cb + 4 + hl:cb + 5 + hl], scalar2=None, op0=ALU.mult),
                        reads=[f"ps{pb_}", f"c1{k}"], writes=[f"t1e{q}_{hl}"])
            for hl in range(2):
                pb_ = banks[hl]
                P.op("dve", lambda e, pb_=pb_, hl=hl: e.scalar_tensor_tensor(
                    out=ofb[hl], in0=ps[pb_][:, 0:128], scalar=small[:, cb + 2 * hl:cb + 2 * hl + 1], in1=t1b[hl],
                    op0=ALU.mult, op1=ALU.add),
                    reads=[f"ps{pb_}", f"rz{k}_{hl}", f"t1e{q}_{hl}"], writes=[f"of{q}_{hl}"])
            for hl in range(2):
                P.op("dve", lambda e, hl=hl: e.scalar_tensor_tensor(
                    out=sqj2, in0=ofb[hl], scalar=1.0, in1=ofb[hl], op0=ALU.mult, op1=ALU.mult,
                    accum_out=small[:, cb + 6 + hl:cb + 7 + hl]),
                    reads=[f"of{q}_{hl}"], writes=["sqj2", f"ss2{k}_{hl}"])
            P.op("dve", lambda e: e.tensor_scalar(
                out=small[:, cb + 8:cb + 10], in0=small[:, cb + 6:cb + 8], scalar1=1.0 / 128, scalar2=EPS,
                op0=ALU.mult, op1=ALU.add),
                reads=[f"ss2{k}_0", f"ss2{k}_1"], writes=[f"sq2{k}"])
            P.op("pool", lambda e: e.tensor_tensor(
                out=small[:, cb + 10:cb + 12], in0=small[:, cb + 8:cb + 10], in1=neghalf2, op=ALU.pow),
                reads=[f"sq2{k}", "neghalf"], writes=[f"rs2{k}"])
            def part_b():
                for hl in range(2):
                    head = 2 * hp + hl
                    P.op("dve", lambda e, hl=hl: e.scalar_tensor_tensor(
                        out=yfb[hl], in0=ofb[hl], scalar=small[:, cb + 10 + hl:cb + 11 + hl], in1=gsub[:, :],
                        op0=ALU.mult, op1=ALU.mult),
                        reads=[f"of{q}_{hl}", f"rs2{k}", "gsub"], writes=[f"yf{q}_{hl}"])
                    P.op("pool", lambda e, hl=hl, head=head: e.tensor_tensor(
                        out=mixtok2s[j % 2][:, head * 128:(head + 1) * 128], in0=yfb[hl], in1=A4[:, j, head * 128:(head + 1) * 128], op=ALU.mult),
                        reads=[f"yf{q}_{hl}"], writes=[f"mt2_{j % 2}_{head}"])
                if hp == 1:
                    pending_mix.append([j, 5])
            while pend_b:
                pend_b.pop(0)()
            pend_b.append(part_b)

        pend_b = []

        def d_mix(j):
            ti = rot("z", DF_ZP)
            mt = mixtok2s[j % 2]
            for c in range(4):
                P.op("pe", lambda e, ti=ti, c=c: e.transpose(
                    out=psb[ti][:, c * 128:(c + 1) * 128], in_=mt[:, c * 128:(c + 1) * 128], identity=ident_bf),
                    reads=[f"mt2_{j % 2}_{c}", "cbf"], writes=[f"ps{ti}"])
            P.op("dve", lambda e, ti=ti, j=j: e.tensor_copy(
                out=mixT[:, 4:8, j * 128:(j + 1) * 128],
                in_=psb[ti][:, 0:512].rearrange("p (c t) -> p c t", t=128)),
                reads=[f"ps{ti}"], writes=[f"qpd{j}"])

        pending_mix = []
        DSK = DF_DSK
        for idx in range(len(dsteps) + DSK):
            if idx < len(dsteps):
                d_zexp(dsteps[idx])
            for pm in list(pending_mix):
                pm[1] -= 1
                if pm[1] <= 0:
                    pending_mix.remove(pm)
                    d_mix(pm[0])
            if idx >= DSK:
                d_av(dsteps[idx - DSK])
        while pend_b:
            pend_b.pop(0)()
        for pm in pending_mix:
            d_mix(pm[0])
        P.barrier()

        xs2 = [scrf[:, s * 1024:(s + 1) * 1024] for s in range(4)]
        yo = [tmpf[:, 0:1024], tmpf[:, 1024:2048]]
        sqj3 = A4[:, 0:2, :].rearrange("p a n -> p (a n)")

        def p5_front(j):
            s = j % 4
            P.dma("sp", f"xs{s}", xs2[s], xr[j * 128:(j + 1) * 128, :], writes=[f"xs2{s}"])
            pa, pb = 2 * (j % 4), 2 * (j % 4) + 1
            for (pi, half) in ((pa, 0), (pb, 1)):
                for c in range(8):
                    P.op("pe", lambda e, pi=pi, half=half, c=c: e.matmul(
                        out=ps[pi][:, :512], lhsT=mixT[:, c, j * 128:(j + 1) * 128],
                        rhs=wo[:, c, half * 512:(half + 1) * 512], start=(c == 0), stop=(c == 7)),
                        reads=["wo"], writes=[f"ps{pi}"])
            for (pi, half) in ((pa, 0), (pb, 1)):
                P.op("dve", lambda e, pi=pi, half=half: e.tensor_tensor(
                    out=xs2[s][:, half * 512:(half + 1) * 512], in0=ps[pi][:, :512],
                    in1=xs2[s][:, half * 512:(half + 1) * 512], op=ALU.add),
                    reads=[f"ps{pi}", f"xs2{s}"], writes=[f"xs2{s}"])
            cb = 200 + (j % 4) * 4
            P.op("act", lambda e: e.activation(
                out=sqj3, in_=xs2[s], func=AF.Square, accum_out=small[:, cb:cb + 1]),
                reads=[f"xs2{s}"], writes=["sqj3", f"ss3{j % 4}"])

        def p5_back(j):
            s = j % 4
            so = j % 2
            cb = 200 + (j % 4) * 4
            P.op("dve", lambda e: e.tensor_scalar(
                out=small[:, cb + 1:cb + 2], in0=small[:, cb:cb + 1], scalar1=1.0 / D, scalar2=EPS,
                op0=ALU.mult, op1=ALU.add),
                reads=[f"ss3{j % 4}"], writes=[f"sq3{j % 4}"])
            P.op("pool", lambda e: e.tensor_tensor(
                out=small[:, cb + 2:cb + 3], in0=small[:, cb + 1:cb + 2], in1=neghalf, op=ALU.pow),
                reads=[f"sq3{j % 4}", "neghalf"], writes=[f"rs3{j % 4}"])
            P.op("dve", lambda e: e.scalar_tensor_tensor(
                out=yo[so], in0=xs2[s], scalar=small[:, cb + 2:cb + 3], in1=gfin, op0=ALU.mult, op1=ALU.mult),
                reads=[f"xs2{s}", f"rs3{j % 4}", "gfin"], writes=[f"yo{so}"])
            P.dma("pool", f"o{so}", out_d[j * 128:(j + 1) * 128, :], yo[so], reads=[f"yo{so}"], writes=[f"out{j}"])

        for j in range(17):
            if j < 16:
                p5_front(j)
            if j >= 1:
                p5_back(j - 1)
        P.barrier()

        with nc.Block() as block:
            @block.tensor
            def _(e):
                for f in P.streams["pe"]:
                    f(e)

            @block.scalar
            def _(e):
                for f in P.streams["act"]:
                    f(e)

            @block.vector
            def _(e):
                for f in P.streams["dve"]:
                    f(e)

            @block.gpsimd
            def _(e):
                for f in P.streams["pool"]:
                    f(e)

            @block.sync
            def _(e):
                for f in P.streams["sp"]:
                    f(e)
    return nc


def _consts():
    bf = ml_dtypes.bfloat16
    k = np.arange(128)[:, None]
    m = np.arange(128)[None, :]
    ident = (k == m).astype(np.float32)
    negM2T = -(k < m).astype(np.float32)
    Dm = (k == m + 1).astype(np.float32) - (k == m).astype(np.float32)
    dlast = np.zeros((128, 128), np.float32)
    dlast[0, 127] = 1.0
    swp = (np.arange(128) // 64) * 64 + ((np.arange(128) % 64) + 32) % 64
    permT = np.zeros((128, 128), np.float32)
    permT[swp, np.arange(128)] = 1.0
    cbf = np.concatenate([ident, negM2T, Dm, dlast, permT], axis=1).astype(bf)
    M1 = (m <= k).astype(np.float32)
    cf = np.concatenate([permT, M1, np.zeros((128, 1936), np.float32)], axis=1).astype(np.float32)
    inv = (1.0 / (np.float32(10000.0) ** (np.arange(0, 64, 2, dtype=np.float32) / np.float32(64)))).astype(np.float32)
    pos = (NTOK - 1 - np.arange(NTOK)).astype(np.float32)
    ang = (pos[None, :] * inv[:, None]).astype(np.float32)
    cos = np.cos(ang).astype(np.float32)
    sin = np.sin(ang).astype(np.float32)
    p = np.arange(128)
    ropec = cos[p % 32, :]
    sign = np.where((p % 64) < 32, -1.0, 1.0).astype(np.float32)[:, None]
    ropes = (sin[p % 32, :] * sign).astype(np.float32)
    return cbf, cf, np.ascontiguousarray(ropec), np.ascontiguousarray(ropes)


_NC_CACHE = {}


def kernel(x, meta_tokens, norm_gain, w_in, w_out, lambda_q1, lambda_k1, lambda_q2, lambda_k2,
           subln_gain, final_norm_gain):
    x = np.asarray(x, np.float32)
    B = x.shape[0]
    w = np.asarray(w_in, np.float32)[0]
    wext = np.ascontiguousarray(w)
    wout = np.ascontiguousarray(np.asarray(w_out, np.float32)[0])
    rep = lambda v, n: np.ascontiguousarray(np.broadcast_to(np.asarray(v, np.float32).reshape(1, n), (128, n)))
    gbc = rep(norm_gain[0], D)
    gfin = rep(final_norm_gain, D)
    gsub = rep(subln_gain[0], 128)
    lamv = np.ascontiguousarray(np.concatenate(
        [rep(lambda_q1[0], 64), rep(lambda_k1[0], 64), rep(lambda_q2[0], 64), rep(lambda_k2[0], 64)], axis=1))
    cbf, cf, ropec, ropes = _consts()
    metar = np.ascontiguousarray(np.asarray(meta_tokens, np.float32)[::-1])
    if "nc" not in _NC_CACHE:
        _NC_CACHE["nc"] = build_nc(DEBUG)
    nc = _NC_CACHE["nc"]
    in_maps = []
    for b in range(B):
        in_maps.append({
            "xr": np.ascontiguousarray(x[b, ::-1, :]), "metar": metar, "wext": wext, "wout": wout,
            "gbc": gbc, "gfin": gfin, "gsub": gsub, "lamv": lamv, "ropec": ropec, "ropes": ropes,
            "cbf": cbf, "cf": cf,
        })
    res = run_bass_kernel_spmd(nc, in_maps, core_ids=list(range(B)))
    outs = [np.asarray(r["out"], np.float32)[::-1] for r in res.results]
    return np.ascontiguousarray(np.stack(outs, axis=0))
```

```python
import contextlib
import numpy as np
import ml_dtypes
import concourse.bass as bass
import concourse.mybir as mybir
from concourse.bass_utils import run_bass_kernel_spmd

F32 = mybir.dt.float32
BF16 = mybir.dt.bfloat16
AF = mybir.ActivationFunctionType
ALU = mybir.AluOpType
AX = mybir.AxisListType

SEQ = 2048
NMETA = 16
NTOK = SEQ + NMETA
D = 1024
NB = 17
EPS = 1e-6
LAMBDA_INIT = 0.8 - 0.6 * float(np.exp(-0.3 * 0))

DEBUG = False
STOP_AFTER = 99
DF_META = True
DFP_LEVEL = 9
DF_ODB = True
DF_DSK = 4
DF_ZP = 4
DF_LEVEL = 9
DMAT_MOD = 2
EPI_ACT_FROM_TILE = 16


def blk_rows(b):
    return 128 if b < 16 else 16


class Prog:
    ENG = ("pe", "act", "dve", "pool", "sp")

    def __init__(self, nc, stack):
        self.nc = nc
        self.stack = stack
        self.streams = {e: [] for e in self.ENG}
        self.sems = {}
        self.semval = {}
        for e in self.ENG:
            self.sems[e] = stack.enter_context(nc.semaphore("s_" + e))
            self.semval[e] = 0
        self.known = {e: {} for e in self.ENG}
        self.snap = {}
        self.res = {}
        self.ninstr = 0
        self.enabled = True
        self.nbar = 0

    def dma_sem(self, key):
        if key not in self.sems:
            self.sems[key] = self.stack.enter_context(self.nc.semaphore("d_" + key))
            self.semval[key] = 0
        return key

    def _wait(self, eng, ev):
        key, val = ev
        if self.known[eng].get(key, 0) >= val:
            return
        self.known[eng][key] = val
        inherited = self.snap.get((key, val))
        if inherited:
            kn = self.known[eng]
            for k2, v2 in inherited.items():
                if k2 != eng and kn.get(k2, 0) < v2:
                    kn[k2] = v2
        sem = self.sems[key]
        self.streams[eng].append(lambda e, sem=sem, val=val: e.wait_ge(sem, val))

    def _deps(self, eng, reads, writes):
        evs = []
        for r in reads:
            st = self.res.get(r)
            if st and st["w"]:
                evs.append((st["w"], "raw"))
        for w in writes:
            st = self.res.get(w)
            if st:
                if st["w"]:
                    evs.append((st["w"], "waw"))
                for k, v in st["r"].items():
                    evs.append(((k, v), "war"))
        for ev, kind in evs:
            key = ev[0]
            if key == eng:
                if eng in ("pe", "sp") or kind == "war":
                    continue
            self._wait(eng, ev)

    def _commit(self, ev, reads, writes):
        for r in reads:
            st = self.res.setdefault(r, {"w": None, "r": {}})
            k, v = ev
            if st["r"].get(k, 0) < v:
                st["r"][k] = v
        for w in writes:
            self.res[w] = {"w": ev, "r": {}}

    def op(self, eng, fn, reads=(), writes=()):
        if not self.enabled:
            return
        self._deps(eng, reads, writes)
        self.semval[eng] += 1
        ev = (eng, self.semval[eng])
        self.snap[ev] = dict(self.known[eng])
        sem = self.sems[eng]
        self.streams[eng].append(lambda e, fn=fn, sem=sem: fn(e).then_inc(sem, 1))
        self._commit(ev, reads, writes)
        self.ninstr += 1

    def dma(self, eng, slot, out, in_, reads=(), writes=(), transpose=False):
        if not self.enabled:
            return
        key = self.dma_sem(slot)
        self._deps(eng, reads, writes)
        self.semval[key] += 16
        ev = (key, self.semval[key])
        sem = self.sems[key]
        if transpose:
            self.streams[eng].append(
                lambda e, out=out, in_=in_, sem=sem: e.dma_start_transpose(out=out, in_=in_).then_inc(sem, 16))
        else:
            self.streams[eng].append(
                lambda e, out=out, in_=in_, sem=sem: e.dma_start(out=out, in_=in_).then_inc(sem, 16))
        self._commit(ev, reads, writes)
        self.ninstr += 1

    def barrier(self):
        if not self.enabled:
            return
        self.nbar += 1
        if self.nbar > STOP_AFTER:
            self.enabled = False
        for e in self.ENG:
            for k, v in self.semval.items():
                if k != e and v > 0:
                    self._wait(e, (k, v))
        self.res = {}


def build_nc(dbg=False):
    nc = bass.Bass("TRN2", target_bir_lowering=False)

    def din(name, shape, dt=F32):
        return nc.dram_tensor(name, list(shape), dt, kind="ExternalInput").ap()

    xr = din("xr", [SEQ, D])
    metar = din("metar", [NMETA, D])
    wext = din("wext", [D, 4096])
    wout = din("wout", [D, D])
    gbc_d = din("gbc", [128, D])
    gfin_d = din("gfin", [128, D])
    gsub_d = din("gsub", [128, 128])
    lamv_d = din("lamv", [128, 256])
    ropec_d = din("ropec", [128, NTOK])
    ropes_d = din("ropes", [128, NTOK])
    cbf_d = din("cbf", [128, 640], BF16)
    cf_d = din("cf", [128, 2192])
    out_d = nc.dram_tensor("out", [SEQ, D], F32, kind="ExternalOutput").ap()
    if dbg:
        dbg_d = nc.dram_tensor("dbg", [128, 8 * NTOK], F32, kind="ExternalOutput").ap()

    with contextlib.ExitStack() as st:
        def sb(name, shape, dt):
            return st.enter_context(nc.sbuf_tensor(name, list(shape), dt))

        uT = sb("uT", [128, 8, NTOK], BF16)
        A1 = sb("A1", [128, 4, NTOK], BF16)
        A2 = sb("A2", [128, 4, NTOK], BF16)
        A3 = sb("A3", [128, NB, 520], BF16)
        A4 = sb("A4", [128, 16, 512], BF16)
        A5flat = sb("A5", [128, NB * 512], BF16)
        A5 = A5flat[:, :].rearrange("p (b n) -> p b n", n=512)
        A5f = A5flat.bitcast(F32)
        mixT = sb("mixT", [128, 8, SEQ], BF16)
        scr = sb("scr", [128, 16384], BF16)
        scrf = scr.bitcast(F32)
        tmpf = sb("tmpf", [128, 2048], F32)
        gbc = sb("gbcs", [128, D], F32)
        cbf = sb("cbfs", [128, 640], BF16)
        cf = sb("cfs", [128, 2192], F32)
        gsub = sb("gsubs", [128, 128], F32)
        lamv = sb("lamvs", [128, 256], F32)
        small = sb("small", [128, 256], F32)
        ps = [st.enter_context(nc.psum_tensor(f"ps{i}", [128, 512], F32)) for i in range(8)]
        psb = [p.bitcast(BF16) for p in ps]

        P = Prog(nc, st)

        ident_bf = cbf[:, 0:128]
        negM2T = cbf[:, 128:256]
        Dmat = cbf[:, 256:384]
        dlast = cbf[0:1, 384:512]
        permT_bf = cbf[:, 512:640]
        permT = cf[:, 0:128]
        M1Z = cf[:, 128:2192]

        P.dma("sp", "c_gbc", gbc[:, :], gbc_d, writes=["gbc"])
        P.dma("sp", "c_cbf", cbf[:, :], cbf_d, writes=["cbf"])
        P.dma("sp", "c_cf", cf[:, :], cf_d, writes=["cf"])
        P.dma("sp", "c_gsub", gsub[:, :], gsub_d, writes=["gsub"])
        P.dma("sp", "c_lamv", lamv[:, :], lamv_d, writes=["lamv"])

        P.op("pool", lambda e: e.memset(small[:, 65:66], EPS), writes=["epsc"])
        epsc = small[:, 65:66]
        P.op("pool", lambda e: e.memset(small[:, 66:68], -0.5), writes=["neghalf"])
        neghalf = small[:, 66:67]
        neghalf2 = small[:, 66:68]
        P.op("dve", lambda e: e.tensor_tensor(out=tmpf[:, 0:64], in0=lamv[:, 0:64], in1=lamv[:, 64:128], op=ALU.mult),
             reads=["lamv"], writes=["lt0"])
        P.op("dve", lambda e: e.reduce_sum(out=small[:, 60:61], in_=tmpf[:, 0:64], axis=AX.X),
             reads=["lt0"], writes=["s1"])
        P.op("dve", lambda e: e.tensor_tensor(out=tmpf[:, 64:128], in0=lamv[:, 128:192], in1=lamv[:, 192:256], op=ALU.mult),
             reads=["lamv"], writes=["lt1"])
        P.op("dve", lambda e: e.reduce_sum(out=small[:, 61:62], in_=tmpf[:, 64:128], axis=AX.X),
             reads=["lt1"], writes=["s2"])
        P.op("act", lambda e: e.activation(out=small[:, 62:64], in_=small[:, 60:62], func=AF.Exp),
             reads=["s1", "s2"], writes=["e12"])
        P.op("dve", lambda e: e.tensor_tensor(out=small[:, 64:65], in0=small[:, 63:64], in1=small[:, 62:63], op=ALU.subtract),
             reads=["e12"], writes=["nl0"])
        P.op("dve", lambda e: e.tensor_scalar(out=small[:, 64:65], in0=small[:, 64:65], scalar1=-LAMBDA_INIT, scalar2=None, op0=ALU.add),
             reads=["nl0"], writes=["neglam"])
        neglam = small[:, 64:65]
        P.op("dve", lambda e: e.tensor_scalar(out=gsub[:, :], in0=gsub[:, :], scalar1=1.0 - LAMBDA_INIT, scalar2=None, op0=ALU.mult),
             reads=["gsub"], writes=["gsub"])

        wbuf = [scr[:, 8192 + s * 4096: 8192 + (s + 1) * 4096].rearrange("p (c n) -> p c n", n=512) for s in range(2)] + \
               [scr[:, s * 4096:(s + 1) * 4096].rearrange("p (c n) -> p c n", n=512) for s in range(2)]

        def load_w(slot, g):
            src = wext[:, g * 512:(g + 1) * 512].rearrange("(c p) n -> p c n", p=128)
            P.dma("pool", f"w{slot}", wbuf[slot], src, writes=[f"w{slot}"])

        load_w(0, 0)
        load_w(1, 1)

        xs = [A5f[:, s * 1024:(s + 1) * 1024] for s in range(2)]
        ub = [A5flat[:, 4096 + s * 1024: 4096 + (s + 1) * 1024] for s in range(2)]
        sqj = A5flat[:, 6144:7168]
        def p0_front(b):
            rows = blk_rows(b)
            s = b % 2
            src = xr[b * 128:(b + 1) * 128, :] if b < 16 else metar
            P.dma("sp", f"xs{s}", xs[s][:rows, :], src, writes=[f"xs{s}"])
            P.op("act", lambda e: e.activation(
                out=sqj[:rows, :], in_=xs[s][:rows, :], func=AF.Square, accum_out=small[:rows, b:b + 1]),
                reads=[f"xs{s}"], writes=["sqj", f"ss{b}"])
            P.op("dve", lambda e: e.tensor_scalar(
                out=small[:rows, 17 + b:18 + b], in0=small[:rows, b:b + 1], scalar1=1.0 / D, scalar2=EPS,
                op0=ALU.mult, op1=ALU.add),
                reads=[f"ss{b}"], writes=[f"sr{b}"])
            P.op("pool", lambda e: e.tensor_tensor(
                out=small[:rows, 34 + b:35 + b], in0=small[:rows, 17 + b:18 + b], in1=neghalf[:rows, :], op=ALU.pow),
                reads=[f"sr{b}", "neghalf"], writes=[f"rstd{b}"])
            P.op("dve", lambda e: e.scalar_tensor_tensor(
                out=ub[s][:rows, :], in0=xs[s][:rows, :], scalar=small[:rows, 34 + b:35 + b], in1=gbc[:rows, :],
                op0=ALU.mult, op1=ALU.mult),
                reads=[f"xs{s}", f"rstd{b}", "gbc"], writes=[f"ub{s}"])

        def p0_tr(b):
            rows = blk_rows(b)
            s = b % 2
            pi = 4 + b % 2
            for c in range(8):
                P.op("pe", lambda e, c=c: e.transpose(
                    out=psb[pi][:, c * 128:c * 128 + rows], in_=ub[s][:rows, c * 128:(c + 1) * 128],
                    identity=ident_bf[:rows, :rows]),
                    reads=[f"ub{s}", "cbf"], writes=[f"ps{pi}"])

        def p0_back(b):
            rows = blk_rows(b)
            pi = 4 + b % 2
            srcv = psb[pi][:, :].rearrange("p (c t) -> p c t", t=128)[:, :, :rows]
            dstv = uT[:, :, b * 128:b * 128 + rows]
            if b % 2 == 0:
                P.op("act", lambda e: e.activation(out=dstv, in_=srcv, func=AF.Copy),
                     reads=[f"ps{pi}"], writes=[f"uT{b}"])
            else:
                P.op("dve", lambda e: e.tensor_copy(out=dstv, in_=srcv),
                     reads=[f"ps{pi}"], writes=[f"uT{b}"])


        TR = [(0, 512), (512, 512), (1024, 512), (1536, 512), (2048, 16)]
        pcount = [0]
        pspool = [[0, 1, 2, 3, 6, 7]]

        def nextps(pool):
            i = pool[pcount[0] % len(pool)]
            pcount[0] += 1
            return i

        def feat_unit(slot, evac, cc, t0, n):
            pi = nextps(pspool[0])
            ub_ = [f"uT{b}" for b in range(t0 // 128, (t0 + n + 127) // 128)]
            for dc in range(8):
                P.op("pe", lambda e, dc=dc: e.matmul(
                    out=ps[pi][:, :n], lhsT=wbuf[slot][:, dc, cc * 128:(cc + 1) * 128],
                    rhs=uT[:, dc, t0:t0 + n], start=(dc == 0), stop=(dc == 7)),
                    reads=[f"w{slot}"] + ub_, writes=[f"ps{pi}"])
            evac(cc, t0, n, pi)

        def tok_unit(slot, evac, b):
            rows = blk_rows(b)
            pi = nextps(pspool[0])
            for dc in range(8):
                P.op("pe", lambda e, dc=dc: e.matmul(
                    out=ps[pi][:rows, :512], lhsT=uT[:, dc, b * 128:b * 128 + rows],
                    rhs=wbuf[slot][:, dc, :], start=(dc == 0), stop=(dc == 7)),
                    reads=[f"w{slot}", f"uT{b}"], writes=[f"ps{pi}"])
            evac(b, rows, pi)

        def proj_feat(slot, evac, tr=None):
            for cc in range(4):
                for (t0, n) in (tr or TR):
                    feat_unit(slot, evac, cc, t0, n)

        def proj_tok(slot, evac, nblk):
            for b in range(nblk):
                tok_unit(slot, evac, b)

        flip = [0]

        def evac_copy_to(dst3):
            def f(cc, t0, n, pi):
                flip[0] ^= 1
                if flip[0]:
                    P.op("act", lambda e: e.activation(out=dst3[:, cc, t0:t0 + n], in_=ps[pi][:, :n], func=AF.Copy),
                         reads=[f"ps{pi}"], writes=[])
                else:
                    P.op("dve", lambda e: e.tensor_copy(out=dst3[:, cc, t0:t0 + n], in_=ps[pi][:, :n]),
                         reads=[f"ps{pi}"], writes=[])
            return f

        QP = [A1[:, h, 0:SEQ] for h in range(4)] + [mixT[:, 4 + h, :] for h in range(4)]
        for h in range(8):
            zr = slice(64, 128) if h % 2 == 0 else slice(0, 64)
            P.op("pool", lambda e, h=h, zr=zr: e.memset(QP[h][zr, :], 0.0), writes=[f"qpz{h}"])

        def evac_q(cc, t0, n, pi):
            if t0 >= SEQ:
                return
            P.op("act", lambda e: e.activation(out=QP[2 * cc][0:64, t0:t0 + n], in_=ps[pi][0:64, :n], func=AF.Copy),
                 reads=[f"ps{pi}"], writes=[])
            P.op("dve", lambda e: e.tensor_copy(out=QP[2 * cc + 1][64:128, t0:t0 + n], in_=ps[pi][64:128, :n]),
                 reads=[f"ps{pi}"], writes=[])
        def evac_v(b, rows, pi):
            P.op("act", lambda e: e.activation(out=A3[:rows, b, 0:512], in_=ps[pi][:rows, :512], func=AF.Copy),
                 reads=[f"ps{pi}"], writes=[f"v{b}"])

        def evac_g(b, rows, pi):
            P.op("act", lambda e: e.activation(out=A4[:rows, b, :], in_=ps[pi][:rows, :512], func=AF.Silu),
                 reads=[f"ps{pi}"], writes=[])

        load_w(2, 2)
        load_w(3, 3)
        evac_k = evac_copy_to(A2)
        fifo = []
        p0_front(0)
        p0_tr(0)
        for b in range(1, NB + 1):
            if b < NB:
                p0_front(b)
            if b >= 1:
                p0_back(b - 1)
                bb = b - 1
                if bb % 4 == 3 or bb == 16:
                    r = bb // 4
                    t0, n = TR[r]
                    for cc in range(4):
                        if r < 4:
                            fifo.append(lambda cc=cc, t0=t0, n=n: feat_unit(0, evac_q, cc, t0, n))
                        fifo.append(lambda cc=cc, t0=t0, n=n: feat_unit(1, evac_k, cc, t0, n))
                    for b2 in range(4 * r, min(4 * r + 4, NB)):
                        fifo.append(lambda b2=b2: tok_unit(2, evac_v, b2))
                        if b2 < 16:
                            fifo.append(lambda b2=b2: tok_unit(3, evac_g, b2))
            for _ in range(4):
                if fifo:
                    fifo.pop(0)()
            if b < NB:
                p0_tr(b)
        while fifo:
            fifo.pop(0)()
        P.barrier()
        P.op("pool", lambda e: e.memset(A5[:, 16, :], 0.0), writes=["dvmeta"])
        for b in range(NB):
            rows = blk_rows(b)
            pi = nextps([0, 1, 2, 3])
            last = (b == NB - 1)
            P.op("pe", lambda e, b=b, rows=rows, pi=pi, last=last: e.matmul(
                out=ps[pi][:rows, :512], lhsT=Dmat[:rows, :rows], rhs=A3[:rows, b, 0:512], start=True, stop=last),
                reads=[f"v{b}", "cbf"], writes=[f"ps{pi}"])
            if not last:
                P.op("pe", lambda e, b=b, pi=pi: e.matmul(
                    out=ps[pi][:128, :512], lhsT=dlast, rhs=A3[0:1, b + 1, 0:512], start=False, stop=True),
                    reads=[f"v{b + 1}"], writes=[f"ps{pi}"])
            P.op("dve", lambda e, b=b, rows=rows, pi=pi: e.tensor_copy(out=A5[:rows, b, :], in_=ps[pi][:rows, :512]),
                 reads=[f"ps{pi}"] + (["dvmeta"] if b == 16 else []), writes=[f"dv{b}"] + (["dvmeta"] if b == 16 else []))

        keepL = [scrf[:, i * 2064:(i + 1) * 2064] for i in range(2)]
        Pb = [scr[:, 8256 + i * 2176: 8256 + (i + 1) * 2176] for i in range(3)]
        tmpb = tmpf.bitcast(BF16)
        if DMAT_MOD:
            PTDs = [tmpb[:, 512 + i * 1152: 512 + (i + 1) * 1152].rearrange("p (b t) -> p b t", t=128) for i in range(2)]
            PTb = [scr[:, 14784 + i * 512: 14784 + (i + 1) * 512] for i in range(3)] + \
                  [tmpb[:, 2816 + i * 512: 2816 + (i + 1) * 512] for i in range(2)]
        else:
            PTb = [scr[:, 14784 + i * 512: 14784 + (i + 1) * 512] for i in range(3)] + \
                  [tmpb[:, 512 + i * 512: 512 + (i + 1) * 512] for i in range(7)]
        NPT = len(PTb)

        def is_dma_chunk(k, c):
            return bool(DMAT_MOD) and c >= 2
        for i in range(3):
            P.op("pool", lambda e, i=i: e.memset(Pb[i][:, :], 0.0), writes=[f"Pb{i}"])
        mixtok = [tmpf.bitcast(BF16)[:, 0:512]]
        cnt = {"z": 0, "k": 0, "p": 0, "t": 0, "pt": 0, "ev": 0}

        def rot(name, n):
            i = cnt[name] % n
            cnt[name] += 1
            return i

        def sb_chunks(j):
            blocks = list(range(j, 16))
            chunks = []
            while blocks:
                cb = blocks[:4]
                blocks = blocks[4:]
                chunks.append([(kb, 128) for kb in cb])
            if len(chunks[-1]) < 4:
                chunks[-1].append((16, 16))
            else:
                chunks.append([(16, 16)])
            return chunks

        items = [(j, h) for j in range(16) for h in range(8)]
        NI = len(items)
        ptslot = {}

        def st_z(k, c):
            j, h = items[k]
            ch = sb_chunks(j)[c]
            cc, po = h // 2, (h % 2) * 64
            n = sum(r for _, r in ch)
            t0 = ch[0][0] * 128
            off = t0 - 128 * j
            zi = rot("z", 2)
            ks = k % 2
            P.op("pe", lambda e: e.matmul(
                out=ps[zi][:, :n], lhsT=QP[h][:, j * 128:(j + 1) * 128],
                rhs=A2[:, cc, t0:t0 + n], start=True, stop=True),
                reads=[], writes=[f"ps{zi}"])
            P.op("act", lambda e: e.activation(
                out=keepL[ks][:, off:off + n], in_=ps[zi][:, :n], func=AF.Sigmoid, scale=-0.125),
                reads=[f"ps{zi}"], writes=[f"keep{ks}"])

        def st_scan(k):
            j, h = items[k]
            ntot = NTOK - 128 * j
            ks, pslot = k % 2, k % 3
            P.op("dve", lambda e: e.tensor_tensor_scan(
                out=Pb[pslot][:, :ntot], data0=keepL[ks][:, :ntot], data1=M1Z[:, :ntot], initial=1.0,
                op0=ALU.mult, op1=ALU.max),
                reads=[f"keep{ks}", "cf"], writes=[f"Pb{pslot}"])

        def st_t(k, c):
            j, h = items[k]
            ch = sb_chunks(j)[c]
            pslot = k % 3
            if is_dma_chunk(k, c):
                if c == 2:
                    nfar = 17 - j - 8
                    P.dma("sp", "dmat", PTDs[k % 2][:, 0:nfar, :], Pb[pslot][:, 1024:1024 + nfar * 128],
                          reads=[f"Pb{pslot}"], writes=[f"PTD{k % 2}", "dmat_serial"], transpose=True)
                return
            ti = 2 + rot("t", 2)
            off = ch[0][0] * 128 - 128 * j
            col = off
            for bi, (kb, r) in enumerate(ch):
                P.op("pe", lambda e, bi=bi, col=col: e.transpose(
                    out=psb[ti][:, bi * 128:(bi + 1) * 128], in_=Pb[pslot][:, col:col + 128],
                    identity=ident_bf),
                    reads=[f"Pb{pslot}", "cbf"], writes=[f"ps{ti}"])
                col += r
            pti = rot("pt", NPT)
            ptslot[(k, c)] = pti
            nb_ = len(ch)
            if ((not DMAT_MOD) and rot("ev", 4) == 3) or (DMAT_MOD and j >= 11 and c == 1):
                P.op("dve", lambda e: e.tensor_copy(out=PTb[pti][:, :nb_ * 128], in_=psb[ti][:, :nb_ * 128]),
                     reads=[f"ps{ti}"], writes=[f"PT{pti}"])
            else:
                P.op("act", lambda e: e.activation(
                    out=PTb[pti][:, :nb_ * 128], in_=psb[ti][:, :nb_ * 128], func=AF.Copy),
                    reads=[f"ps{ti}"], writes=[f"PT{pti}"])

        def st_av(k, c):
            j, h = items[k]
            ch = sb_chunks(j)[c]
            oi = 6 + (j % 2)
            hc = slice(h * 64, (h + 1) * 64)
            if is_dma_chunk(k, c):
                for bi, (kb, r) in enumerate(ch):
                    P.op("pe", lambda e, bi=bi, kb=kb: e.matmul(
                        out=ps[oi][:, hc], lhsT=PTDs[k % 2][:, kb - j - 8, :],
                        rhs=A5[:, kb, hc], start=(c == 0 and bi == 0), stop=False),
                        reads=[f"PTD{k % 2}", f"dv{kb}"], writes=[f"ps{oi}"])
                return
            pti = ptslot[(k, c)]
            for bi, (kb, r) in enumerate(ch):
                P.op("pe", lambda e, bi=bi, kb=kb: e.matmul(
                    out=ps[oi][:, hc], lhsT=PTb[pti][:, bi * 128:(bi + 1) * 128],
                    rhs=A5[:, kb, hc], start=(c == 0 and bi == 0), stop=False),
                    reads=[f"PT{pti}", f"dv{kb}"], writes=[f"ps{oi}"])

        def st_tail(k):
            j, h = items[k]
            oi = 6 + (j % 2)
            hc = slice(h * 64, (h + 1) * 64)
            P.op("pe", lambda e: e.matmul(
                out=ps[oi][:, hc], lhsT=negM2T, rhs=A5[:, j, hc], start=False, stop=False),
                reads=["cbf", f"dv{j}"], writes=[f"ps{oi}"])
            P.op("pe", lambda e: e.matmul(
                out=ps[oi][:, hc], lhsT=ident_bf, rhs=A3[:, j, hc], start=False, stop=True),
                reads=["cbf"], writes=[f"ps{oi}"])
            if h == 7:
                P.op("dve", lambda e: e.tensor_tensor(
                    out=mixtok[0][:, :], in0=ps[oi][:, :512], in1=A4[:, j, :], op=ALU.mult),
                    reads=[f"ps{oi}"], writes=["mixtok"])
                ti = 4 + (j % 2)

                def sb_mix(j=j, ti=ti):
                    for c in range(4):
                        P.op("pe", lambda e, c=c: e.transpose(
                            out=psb[ti][:, c * 128:(c + 1) * 128], in_=mixtok[0][:, c * 128:(c + 1) * 128], identity=ident_bf),
                            reads=["mixtok", "cbf"], writes=[f"ps{ti}"])
                    P.op("act", lambda e: e.activation(
                        out=mixT[:, 0:4, j * 128:(j + 1) * 128],
                        in_=psb[ti][:, 0:512].rearrange("p (c t) -> p c t", t=128), func=AF.Copy),
                        reads=[f"ps{ti}"], writes=[])
                sb_pending.append(sb_mix)
            elif sb_pending:
                sb_pending.pop(0)()

        sb_pending = []

        def nch(k):
            return len(sb_chunks(items[k][0])) if 0 <= k < NI else 0

        SK = 2
        for k in range(NI + SK + 1):
            if 0 <= k - 1 < NI:
                st_scan(k - 1)
            na, nt_, nv_ = nch(k), nch(k - SK), nch(k - SK - 1)
            for c in range(na):
                st_z(k, c)
            for c in range(max(nt_, nv_)):
                if c < nt_:
                    st_t(k - SK, c)
                if c < nv_:
                    st_av(k - SK - 1, c)
            if 0 <= k - SK - 1 < NI:
                st_tail(k - SK - 1)
            if k == NI:
                for slot, g in ((2, 6), (3, 7)):
                    src = wext[:, g * 512:(g + 1) * 512].rearrange("(c p) n -> p c n", p=128)
                    P.dma("pool", f"w{slot}", wbuf[slot], src, writes=[f"w{slot}", "keep0", "keep1", "dmat_serial"])
        while sb_pending:
            sb_pending.pop(0)()
        P.barrier()
        load_w(0, 4)
        load_w(1, 5)

        pspool[0] = [0, 1, 2, 3, 4, 5, 6, 7]
        ropec = A5f[:, 0:NTOK]
        ropes = A5f[:, NTOK:2 * NTOK]
        P.dma("sp", "c_rc", ropec, ropec_d, writes=["ropec"])
        P.dma("sp", "c_rs", ropes, ropes_d, writes=["ropes"])
        vaug4 = A3[:, :, :].rearrange("p b (h e) -> p b h e", e=130)
        P.op("pool", lambda e: e.memset(vaug4[:, :, :, 128:129], 1.0), writes=["vones"])
        P.op("pool", lambda e: e.memset(vaug4[:, :, :, 129:130], 0.0), writes=["vzero"])

        QPD = [A1[:, h, 0:SEQ] for h in range(4)] + [mixT[:, 4 + h, :] for h in range(4)]
        for h in range(8):
            zr = slice(64, 128) if h % 2 == 0 else slice(0, 64)
            P.op("pool", lambda e, h=h, zr=zr: e.memset(QPD[h][zr, :], 0.0), writes=[f"qpdz{h}"])

        qhb = [scr[:, i * 512:(i + 1) * 512] for i in range(4)]
        qlb = [scr[:, 2048 + i * 512: 2048 + (i + 1) * 512] for i in range(4)]

        def proj_rope(dst3, slot, padded=False):
            units = [(cc, t0, n) for cc in range(4) for (t0, n) in (TR[:4] if padded else TR)]
            pend = None

            def finish(u):
                cc, t0, n, pa, qi = u
                pb = nextps([0, 1, 2, 3, 4, 5, 6, 7])
                P.op("pe", lambda e: e.matmul(out=ps[pb][:, :n], lhsT=permT_bf, rhs=qhb[qi][:, :n], start=True, stop=False),
                     reads=[f"qs{qi}", "cbf"], writes=[f"ps{pb}"])
                P.op("pe", lambda e: e.matmul(out=ps[pb][:, :n], lhsT=permT_bf, rhs=qlb[qi][:, :n], start=False, stop=True),
                     reads=[f"ql{qi}", "cbf"], writes=[f"ps{pb}"])
                ta = rot("k", 2)
                t1 = tmpf[:, ta * 1024: ta * 1024 + n]
                t2 = tmpf[:, ta * 1024 + 512: ta * 1024 + 512 + n]
                P.op("dve", lambda e: e.tensor_tensor(out=t1, in0=ps[pa][:, :n], in1=ropec[:, t0:t0 + n], op=ALU.mult),
                     reads=[f"ps{pa}", "ropec"], writes=[f"t1{ta}"])
                P.op("dve", lambda e: e.tensor_tensor(out=t2, in0=ps[pb][:, :n], in1=ropes[:, t0:t0 + n], op=ALU.mult),
                     reads=[f"ps{pb}", "ropes"], writes=[f"t2{ta}"])
                if padded:
                    P.op("pool", lambda e: e.tensor_tensor(
                        out=QPD[2 * cc][0:64, t0:t0 + n], in0=t1[0:64, :], in1=t2[0:64, :], op=ALU.add),
                        reads=[f"t1{ta}", f"t2{ta}", f"qpdz{2 * cc}"], writes=[f"qpa{ta}"])
                    P.op("pool", lambda e: e.tensor_tensor(
                        out=QPD[2 * cc + 1][64:128, t0:t0 + n], in0=t1[64:128, :], in1=t2[64:128, :], op=ALU.add),
                        reads=[f"t1{ta}", f"t2{ta}", f"qpdz{2 * cc + 1}"], writes=[f"qpb{ta}"])
                else:
                    P.op("pool", lambda e: e.tensor_tensor(
                        out=dst3[:, cc, t0:t0 + n], in0=t1, in1=t2, op=ALU.add),
                        reads=[f"t1{ta}", f"t2{ta}"], writes=[f"qpa{ta}"])

            for (cc, t0, n) in units:
                pa = nextps([0, 1, 2, 3, 4, 5, 6, 7])
                for dc in range(8):
                    P.op("pe", lambda e, dc=dc, pa=pa, cc=cc, t0=t0, n=n: e.matmul(
                        out=ps[pa][:, :n], lhsT=wbuf[slot][:, dc, cc * 128:(cc + 1) * 128],
                        rhs=uT[:, dc, t0:t0 + n], start=(dc == 0), stop=(dc == 7)),
                        reads=[f"w{slot}"], writes=[f"ps{pa}"])
                qi = rot("p", 4)
                P.op("act", lambda e, pa=pa, qi=qi, n=n: e.activation(out=qhb[qi][:, :n], in_=ps[pa][:, :n], func=AF.Copy),
                     reads=[f"ps{pa}"], writes=[f"qs{qi}"])
                P.op("dve", lambda e, pa=pa, qi=qi, n=n: e.tensor_tensor(
                    out=qlb[qi][:, :n], in0=ps[pa][:, :n], in1=qhb[qi][:, :n], op=ALU.subtract),
                    reads=[f"ps{pa}", f"qs{qi}"], writes=[f"ql{qi}"])
                if pend is not None:
                    finish(pend)
                pend = (cc, t0, n, pa, qi)
            finish(pend)

        def evac_vd(b, rows, pi):
            P.op("act", lambda e: e.activation(
                out=vaug4[:rows, b, :, 0:128], in_=ps[pi][:rows, :512].rearrange("p (h e) -> p h e", e=128), func=AF.Copy),
                reads=[f"ps{pi}"], writes=[])
        if DFP_LEVEL >= 1:
            proj_tok(2, evac_vd, NB)
        if DFP_LEVEL >= 2:
            proj_tok(3, evac_g, 16)
        if DFP_LEVEL >= 3:
            proj_rope(None, 0, padded=True)
        if DFP_LEVEL >= 4:
            proj_rope(A2, 1)
        P.barrier()

        wo = scr[:, 8192:16384].rearrange("p (c n) -> p c n", n=1024)
        P.dma("pool", "w0", wo, wout.rearrange("(c p) n -> p c n", p=128), writes=["wo"])
        gfin = A5f[:, 0:1024]
        P.dma("sp", "c_rc", gfin, gfin_d, writes=["gfin"])
        ETb = [scr[:, i * 512:(i + 1) * 512] for i in range(8)]
        of32 = [tmpf[:, i * 128:(i + 1) * 128] for i in range(2)]
        yf32 = [tmpf[:, 256 + i * 128: 256 + (i + 1) * 128] for i in range(2)]
        sqj2 = tmpf[:, 0:128]
        mixtok2s = [scr[:, 4096:4608], scr[:, 4608:5120]]
        sc = {"i": 0}
        dsteps = []
        for j in range(16):
            for hp in range(2):
                st_l = [[kb] for kb in range(j, NB)]
                for si, kbs in enumerate(st_l):
                    dsteps.append({"j": j, "hp": hp, "kbs": kbs, "last": si == len(st_l) - 1,
                                   "par": ((2 * j + hp) % 2) if DF_ODB else 1})

        def d_zexp(stp):
            j, hp, kbs = stp["j"], stp["hp"], stp["kbs"]
            rows = blk_rows(kbs[0])
            zb = [rot("z", DF_ZP) for _ in kbs]
            eb = [rot("k", 8) for _ in kbs]
            stp["eb"] = eb
            for bi, kb in enumerate(kbs):
                for p2 in range(2):
                    hc0 = 4 * hp + 2 * p2
                    cc = hc0 // 2
                    if hc0 < 4:
                        rhs2 = A1[:, hc0:hc0 + 2, j * 128:(j + 1) * 128]
                    else:
                        rhs2 = mixT[:, hc0:hc0 + 2, j * 128:(j + 1) * 128]
                    P.op("pe", lambda e, zi=zb[bi], p2=p2, cc=cc, kb=kb, rhs2=rhs2: e.matmul(
                        out=ps[zi][:rows, p2 * 256:(p2 + 1) * 256], lhsT=A2[:, cc, kb * 128:kb * 128 + rows],
                        rhs=rhs2, start=True, stop=True),
                        reads=[f"qpd{j}"], writes=[f"ps{zb[bi]}"])
                zi, ei = zb[bi], eb[bi]
                P.op("act", lambda e, zi=zi, ei=ei: e.activation(
                    out=ETb[ei][:rows, :], in_=ps[zi][:rows, :], func=AF.Exp, scale=0.125),
                    reads=[f"ps{zi}"], writes=[f"ET{ei}"])
                if kb == j:
                    rect = ETb[ei][0:64, :].rearrange("p (i t) -> p i t", t=128)[:, :, 64:128]
                    P.op("pool", lambda e, rect=rect: e.memset(rect, 0.0),
                         reads=[], writes=[f"ET{ei}"])

        def d_av(stp):
            j, hp, kbs, eb = stp["j"], stp["hp"], stp["kbs"], stp["eb"]
            rows = blk_rows(kbs[0])
            for bi, kb in enumerate(kbs):
                for i in range(4):
                    head = (4 * hp + i) // 2
                    ei = eb[bi]
                    col = i * 128
                    ob, oc = 4 + 2 * stp["par"] + i // 2, (i % 2) * 256
                    P.op("pe", lambda e, i=i, ei=ei, col=col, kb=kb, head=head, ob=ob, oc=oc: e.matmul(
                        out=ps[ob][:, oc:oc + 129], lhsT=ETb[ei][:rows, col:col + 128],
                        rhs=A3[:rows, kb, head * 130:head * 130 + 129],
                        start=(kb == j and i % 2 == 0), stop=(kb == NB - 1), skip_group_check=True),
                        reads=[f"ET{ei}", "vones"], writes=[f"ps{ob}"])
            if stp["last"]:
                d_epilogue(j, hp, stp["par"])

        def d_epilogue(j, hp, par):
            k = sc["i"] % 4
            sc["i"] += 1
            q = k % 2
            cb = 80 + k * 16
            banks = [4 + 2 * par, 5 + 2 * par]
            tb = 512 + q * 768
            t1b = [tmpf[:, tb + hl * 128: tb + (hl + 1) * 128] for hl in range(2)]
            ofb = [tmpf[:, tb + 256 + hl * 128: tb + 256 + (hl + 1) * 128] for hl in range(2)]
            yfb = [tmpf[:, tb + 512 + hl * 128: tb + 512 + (hl + 1) * 128] for hl in range(2)]
            for hl in range(2):
                pb_ = banks[hl]
                P.op("dve", lambda e, pb_=pb_, hl=hl: e.reciprocal(
                    out=small[:, cb + 2 * hl:cb + 2 * hl + 2], in_=ps[pb_][:, 128:512:256]),
                    reads=[f"ps{pb_}"], writes=[f"rz{k}_{hl}"])
            P.op("dve", lambda e: e.tensor_scalar(
                out=small[:, cb + 4:cb + 6], in0=small[:, cb + 1:cb + 4:2], scalar1=neglam, scalar2=None, op0=ALU.mult),
                reads=[f"rz{k}_0", f"rz{k}_1", "neglam"], writes=[f"c1{k}"])
            act_t1 = j >= EPI_ACT_FROM_TILE
            for hl in range(2):
                pb_ = banks[hl]
                if act_t1:
                    P.op("act", lambda e, pb_=pb_, hl=hl: e.activation(
                        out=t1b[hl], in_=ps[pb_][:, 256:384], func=AF.Copy, scale=small[:, cb + 4 + hl:cb + 5 + hl]),
                        reads=[f"ps{pb_}", f"c1{k}"], writes=[f"t1e{q}_{hl}"])
                else:
                    P.op("dve", lambda e, pb_=pb_, hl=hl: e.tensor_scalar(
                        out=t1b[hl], in0=ps[pb_][:, 256:384], scalar1=small[:, cb + 4 + hl:cb + 5 + hl], scalar2=None, op0=ALU.mult),
                        reads=[f"ps{pb_}", f"c1{k}"], writes=[f"t1e{q}_{hl}"])
            for hl in range(2):
                pb_ = banks[hl]
                P.op("dve", lambda e, pb_=pb_, hl=hl: e.scalar_tensor_tensor(
                    out=ofb[hl], in0=ps[pb_][:, 0:128], scalar=small[:, cb + 2 * hl:cb + 2 * hl + 1], in1=t1b[hl],
                    op0=ALU.mult, op1=ALU.add),
                    reads=[f"ps{pb_}", f"rz{k}_{hl}", f"t1e{q}_{hl}"], writes=[f"of{q}_{hl}"])
            for hl in range(2):
                P.op("dve", lambda e, hl=hl: e.scalar_tensor_tensor(
                    out=sqj2, in0=ofb[hl], scalar=1.0, in1=ofb[hl], op0=ALU.mult, op1=ALU.mult,
                    accum_out=small[:, cb + 6 + hl:cb + 7 + hl]),
                    reads=[f"of{q}_{hl}"], writes=["sqj2", f"ss2{k}_{hl}"])
            P.op("dve", lambda e: e.tensor_scalar(
                out=small[:, cb + 8:cb + 10], in0=small[:, cb + 6:cb + 8], scalar1=1.0 / 128, scalar2=EPS,
                op0=ALU.mult, op1=ALU.add),
                reads=[f"ss2{k}_0", f"ss2{k}_1"], writes=[f"sq2{k}"])
            P.op("pool", lambda e: e.tensor_tensor(
                out=small[:, cb + 10:cb + 12], in0=small[:, cb + 8:cb + 10], in1=neghalf2, op=ALU.pow),
                reads=[f"sq2{k}", "neghalf"], writes=[f"rs2{k}"])
            def part_b():
                for hl in range(2):
                    head = 2 * hp + hl
                    P.op("dve", lambda e, hl=hl: e.scalar_tensor_tensor(
                        out=yfb[hl], in0=ofb[hl], scalar=small[:, cb + 10 + hl:cb + 11 + hl], in1=gsub[:, :],
                        op0=ALU.mult, op1=ALU.mult),
                        reads=[f"of{q}_{hl}", f"rs2{k}", "gsub"], writes=[f"yf{q}_{hl}"])
                    P.op("pool", lambda e, hl=hl, head=head: e.tensor_tensor(
                        out=mixtok2s[j % 2][:, head * 128:(head + 1) * 128], in0=yfb[hl], in1=A4[:, j, head * 128:(head + 1) * 128], op=ALU.mult),
                        reads=[f"yf{q}_{hl}"], writes=[f"mt2_{j % 2}_{head}"])
                if hp == 1:
                    pending_mix.append([j, 6])
            while pend_b:
                pend_b.pop(0)()
            pend_b.append(part_b)

        pend_b = []

        def d_mix(j):
            ti = rot("z", DF_ZP)
            mt = mixtok2s[j % 2]
            for c in range(4):
                P.op("pe", lambda e, ti=ti, c=c: e.transpose(
                    out=psb[ti][:, c * 128:(c + 1) * 128], in_=mt[:, c * 128:(c + 1) * 128], identity=ident_bf),
                    reads=[f"mt2_{j % 2}_{c}", "cbf"], writes=[f"ps{ti}"])
            P.op("dve", lambda e, ti=ti, j=j: e.tensor_copy(
                out=mixT[:, 4:8, j * 128:(j + 1) * 128],
                in_=psb[ti][:, 0:512].rearrange("p (c t) -> p c t", t=128)),
                reads=[f"ps{ti}"], writes=[f"qpd{j}"])

        pending_mix = []
        DSK = DF_DSK
        for idx in range(len(dsteps) + DSK):
            if idx < len(dsteps):
                d_zexp(dsteps[idx])
            for pm in list(pending_mix):
                pm[1] -= 1
                if pm[1] <= 0:
                    pending_mix.remove(pm)
                    d_mix(pm[0])
            if idx >= DSK:
                d_av(dsteps[idx - DSK])
        while pend_b:
            pend_b.pop(0)()
        for pm in pending_mix:
            d_mix(pm[0])
        P.barrier()

        xs2 = [scrf[:, s * 1024:(s + 1) * 1024] for s in range(4)]
        yo = [tmpf[:, 0:1024], tmpf[:, 1024:2048]]
        sqj3 = A4[:, 0:2, :].rearrange("p a n -> p (a n)")

        def p5_front(j):
            s = j % 4
            P.dma("sp", f"xs{s}", xs2[s], xr[j * 128:(j + 1) * 128, :], writes=[f"xs2{s}"])
            pa, pb = 2 * (j % 4), 2 * (j % 4) + 1
            for (pi, half) in ((pa, 0), (pb, 1)):
                for c in range(8):
                    P.op("pe", lambda e, pi=pi, half=half, c=c: e.matmul(
                        out=ps[pi][:, :512], lhsT=mixT[:, c, j * 128:(j + 1) * 128],
                        rhs=wo[:, c, half * 512:(half + 1) * 512], start=(c == 0), stop=(c == 7)),
                        reads=["wo"], writes=[f"ps{pi}"])
            for (pi, half) in ((pa, 0), (pb, 1)):
                P.op("dve", lambda e, pi=pi, half=half: e.tensor_tensor(
                    out=xs2[s][:, half * 512:(half + 1) * 512], in0=ps[pi][:, :512],
                    in1=xs2[s][:, half * 512:(half + 1) * 512], op=ALU.add),
                    reads=[f"ps{pi}", f"xs2{s}"], writes=[f"xs2{s}"])
            cb = 200 + (j % 4) * 4
            P.op("act", lambda e: e.activation(
                out=sqj3, in_=xs2[s], func=AF.Square, accum_out=small[:, cb:cb + 1]),
                reads=[f"xs2{s}"], writes=["sqj3", f"ss3{j % 4}"])

        def p5_back(j):
            s = j % 4
            so = j % 2
            cb = 200 + (j % 4) * 4
            P.op("dve", lambda e: e.tensor_scalar(
                out=small[:, cb + 1:cb + 2], in0=small[:, cb:cb + 1], scalar1=1.0 / D, scalar2=EPS,
                op0=ALU.mult, op1=ALU.add),
                reads=[f"ss3{j % 4}"], writes=[f"sq3{j % 4}"])
            P.op("pool", lambda e: e.tensor_tensor(
                out=small[:, cb + 2:cb + 3], in0=small[:, cb + 1:cb + 2], in1=neghalf, op=ALU.pow),
                reads=[f"sq3{j % 4}", "neghalf"], writes=[f"rs3{j % 4}"])
            P.op("dve", lambda e: e.scalar_tensor_tensor(
                out=yo[so], in0=xs2[s], scalar=small[:, cb + 2:cb + 3], in1=gfin, op0=ALU.mult, op1=ALU.mult),
                reads=[f"xs2{s}", f"rs3{j % 4}", "gfin"], writes=[f"yo{so}"])
            P.dma("pool", f"o{so}", out_d[j * 128:(j + 1) * 128, :], yo[so], reads=[f"yo{so}"], writes=[f"out{j}"])

        for j in range(17):
            if j < 16:
                p5_front(j)
            if j >= 1:
                p5_back(j - 1)
        P.barrier()

        with nc.Block() as block:
            @block.tensor
            def _(e):
                for f in P.streams["pe"]:
                    f(e)

            @block.scalar
            def _(e):
                for f in P.streams["act"]:
                    f(e)

            @block.vector
            def _(e):
                for f in P.streams["dve"]:
                    f(e)

            @block.gpsimd
            def _(e):
                for f in P.streams["pool"]:
                    f(e)

            @block.sync
            def _(e):
                for f in P.streams["sp"]:
                    f(e)
    return nc


def _consts():
    bf = ml_dtypes.bfloat16
    k = np.arange(128)[:, None]
    m = np.arange(128)[None, :]
    ident = (k == m).astype(np.float32)
    negM2T = -(k < m).astype(np.float32)
    Dm = (k == m + 1).astype(np.float32) - (k == m).astype(np.float32)
    dlast = np.zeros((128, 128), np.float32)
    dlast[0, 127] = 1.0
    swp = (np.arange(128) // 64) * 64 + ((np.arange(128) % 64) + 32) % 64
    permT = np.zeros((128, 128), np.float32)
    permT[swp, np.arange(128)] = 1.0
    cbf = np.concatenate([ident, negM2T, Dm, dlast, permT], axis=1).astype(bf)
    M1 = (m <= k).astype(np.float32)
    cf = np.concatenate([permT, M1, np.zeros((128, 1936), np.float32)], axis=1).astype(np.float32)
    inv = (1.0 / (np.float32(10000.0) ** (np.arange(0, 64, 2, dtype=np.float32) / np.float32(64)))).astype(np.float32)
    pos = (NTOK - 1 - np.arange(NTOK)).astype(np.float32)
    ang = (pos[None, :] * inv[:, None]).astype(np.float32)
    cos = np.cos(ang).astype(np.float32)
    sin = np.sin(ang).astype(np.float32)
    p = np.arange(128)
    ropec = cos[p % 32, :]
    sign = np.where((p % 64) < 32, -1.0, 1.0).astype(np.float32)[:, None]
    ropes = (sin[p % 32, :] * sign).astype(np.float32)
    return cbf, cf, np.ascontiguousarray(ropec), np.ascontiguousarray(ropes)


_NC_CACHE = {}


def kernel(x, meta_tokens, norm_gain, w_in, w_out, lambda_q1, lambda_k1, lambda_q2, lambda_k2,
           subln_gain, final_norm_gain):
    x = np.asarray(x, np.float32)
    B = x.shape[0]
    w = np.asarray(w_in, np.float32)[0]
    wext = np.ascontiguousarray(w)
    wout = np.ascontiguousarray(np.asarray(w_out, np.float32)[0])
    rep = lambda v, n: np.ascontiguousarray(np.broadcast_to(np.asarray(v, np.float32).reshape(1, n), (128, n)))
    gbc = rep(norm_gain[0], D)
    gfin = rep(final_norm_gain, D)
    gsub = rep(subln_gain[0], 128)
    lamv = np.ascontiguousarray(np.concatenate(
        [rep(lambda_q1[0], 64), rep(lambda_k1[0], 64), rep(lambda_q2[0], 64), rep(lambda_k2[0], 64)], axis=1))
    cbf, cf, ropec, ropes = _consts()
    metar = np.ascontiguousarray(np.asarray(meta_tokens, np.float32)[::-1])
    if "nc" not in _NC_CACHE:
        _NC_CACHE["nc"] = build_nc(DEBUG)
    nc = _NC_CACHE["nc"]
    in_maps = []
    for b in range(B):
        in_maps.append({
            "xr": np.ascontiguousarray(x[b, ::-1, :]), "metar": metar, "wext": wext, "wout": wout,
            "gbc": gbc, "gfin": gfin, "gsub": gsub, "lamv": lamv, "ropec": ropec, "ropes": ropes,
            "cbf": cbf, "cf": cf,
        })
    res = run_bass_kernel_spmd(nc, in_maps, core_ids=list(range(B)))
    outs = [np.asarray(r["out"], np.float32)[::-1] for r in res.results]
    return np.ascontiguousarray(np.stack(outs, axis=0))
```

```python
import contextlib
import numpy as np
import ml_dtypes
import concourse.bass as bass
import concourse.mybir as mybir
from concourse.bass_utils import run_bass_kernel_spmd

F32 = mybir.dt.float32
BF16 = mybir.dt.bfloat16
AF = mybir.ActivationFunctionType
ALU = mybir.AluOpType
AX = mybir.AxisListType

SEQ = 2048
NMETA = 16
NTOK = SEQ + NMETA
D = 1024
NB = 17
EPS = 1e-6
LAMBDA_INIT = 0.8 - 0.6 * float(np.exp(-0.3 * 0))

DEBUG = False
STOP_AFTER = 99
DF_META = True
DFP_LEVEL = 9
DF_ODB = True
DF_DSK = 4
DF_ZP = 4
DF_LEVEL = 9
DMAT_MOD = 2
EPI_ACT_FROM_TILE = 16


def blk_rows(b):
    return 128 if b < 16 else 16


class Prog:
    ENG = ("pe", "act", "dve", "pool", "sp")

    def __init__(self, nc, stack):
        self.nc = nc
        self.stack = stack
        self.streams = {e: [] for e in self.ENG}
        self.sems = {}
        self.semval = {}
        for e in self.ENG:
            self.sems[e] = stack.enter_context(nc.semaphore("s_" + e))
            self.semval[e] = 0
        self.known = {e: {} for e in self.ENG}
        self.snap = {}
        self.res = {}
        self.ninstr = 0
        self.enabled = True
        self.nbar = 0

    def dma_sem(self, key):
        if key not in self.sems:
            self.sems[key] = self.stack.enter_context(self.nc.semaphore("d_" + key))
            self.semval[key] = 0
        return key

    def _wait(self, eng, ev):
        key, val = ev
        if self.known[eng].get(key, 0) >= val:
            return
        self.known[eng][key] = val
        inherited = self.snap.get((key, val))
        if inherited:
            kn = self.known[eng]
            for k2, v2 in inherited.items():
                if k2 != eng and kn.get(k2, 0) < v2:
                    kn[k2] = v2
        sem = self.sems[key]
        self.streams[eng].append(lambda e, sem=sem, val=val: e.wait_ge(sem, val))

    def _deps(self, eng, reads, writes):
        evs = []
        for r in reads:
            st = self.res.get(r)
            if st and st["w"]:
                evs.append((st["w"], "raw"))
        for w in writes:
            st = self.res.get(w)
            if st:
                if st["w"]:
                    evs.append((st["w"], "waw"))
                for k, v in st["r"].items():
                    evs.append(((k, v), "war"))
        for ev, kind in evs:
            key = ev[0]
            if key == eng:
                if eng in ("pe", "sp") or kind == "war":
                    continue
            self._wait(eng, ev)

    def _commit(self, ev, reads, writes):
        for r in reads:
            st = self.res.setdefault(r, {"w": None, "r": {}})
            k, v = ev
            if st["r"].get(k, 0) < v:
                st["r"][k] = v
        for w in writes:
            self.res[w] = {"w": ev, "r": {}}

    def op(self, eng, fn, reads=(), writes=()):
        if not self.enabled:
            return
        self._deps(eng, reads, writes)
        self.semval[eng] += 1
        ev = (eng, self.semval[eng])
        self.snap[ev] = dict(self.known[eng])
        sem = self.sems[eng]
        self.streams[eng].append(lambda e, fn=fn, sem=sem: fn(e).then_inc(sem, 1))
        self._commit(ev, reads, writes)
        self.ninstr += 1

    def dma(self, eng, slot, out, in_, reads=(), writes=(), transpose=False):
        if not self.enabled:
            return
        key = self.dma_sem(slot)
        self._deps(eng, reads, writes)
        self.semval[key] += 16
        ev = (key, self.semval[key])
        sem = self.sems[key]
        if transpose:
            self.streams[eng].append(
                lambda e, out=out, in_=in_, sem=sem: e.dma_start_transpose(out=out, in_=in_).then_inc(sem, 16))
        else:
            self.streams[eng].append(
                lambda e, out=out, in_=in_, sem=sem: e.dma_start(out=out, in_=in_).then_inc(sem, 16))
        self._commit(ev, reads, writes)
        self.ninstr += 1

    def barrier(self):
        if not self.enabled:
            return
        self.nbar += 1
        if self.nbar > STOP_AFTER:
            self.enabled = False
        for e in self.ENG:
            for k, v in self.semval.items():
                if k != e and v > 0:
                    self._wait(e, (k, v))
        self.res = {}


def build_nc(dbg=False):
    nc = bass.Bass("TRN2", target_bir_lowering=False)

    def din(name, shape, dt=F32):
        return nc.dram_tensor(name, list(shape), dt, kind="ExternalInput").ap()

    xr = din("xr", [SEQ, D])
    metar = din("metar", [NMETA, D])
    wext = din("wext", [D, 4096])
    wout = din("wout", [D, D])
    gbc_d = din("gbc", [128, D])
    gfin_d = din("gfin", [128, D])
    gsub_d = din("gsub", [128, 128])
    lamv_d = din("lamv", [128, 256])
    ropec_d = din("ropec", [128, NTOK])
    ropes_d = din("ropes", [128, NTOK])
    cbf_d = din("cbf", [128, 640], BF16)
    cf_d = din("cf", [128, 2192])
    out_d = nc.dram_tensor("out", [SEQ, D], F32, kind="ExternalOutput").ap()
    if dbg:
        dbg_d = nc.dram_tensor("dbg", [128, 8 * NTOK], F32, kind="ExternalOutput").ap()

    with contextlib.ExitStack() as st:
        def sb(name, shape, dt):
            return st.enter_context(nc.sbuf_tensor(name, list(shape), dt))

        uT = sb("uT", [128, 8, NTOK], BF16)
        A1 = sb("A1", [128, 4, NTOK], BF16)
        A2 = sb("A2", [128, 4, NTOK], BF16)
        A3 = sb("A3", [128, NB, 520], BF16)
        A4 = sb("A4", [128, 16, 512], BF16)
        A5flat = sb("A5", [128, NB * 512], BF16)
        A5 = A5flat[:, :].rearrange("p (b n) -> p b n", n=512)
        A5f = A5flat.bitcast(F32)
        mixT = sb("mixT", [128, 8, SEQ], BF16)
        scr = sb("scr", [128, 16384], BF16)
        scrf = scr.bitcast(F32)
        tmpf = sb("tmpf", [128, 2048], F32)
        gbc = sb("gbcs", [128, D], F32)
        cbf = sb("cbfs", [128, 640], BF16)
        cf = sb("cfs", [128, 2192], F32)
        gsub = sb("gsubs", [128, 128], F32)
        lamv = sb("lamvs", [128, 256], F32)
        small = sb("small", [128, 256], F32)
        ps = [st.enter_context(nc.psum_tensor(f"ps{i}", [128, 512], F32)) for i in range(8)]
        psb = [p.bitcast(BF16) for p in ps]

        P = Prog(nc, st)

        ident_bf = cbf[:, 0:128]
        negM2T = cbf[:, 128:256]
        Dmat = cbf[:, 256:384]
        dlast = cbf[0:1, 384:512]
        permT_bf = cbf[:, 512:640]
        permT = cf[:, 0:128]
        M1Z = cf[:, 128:2192]

        P.dma("sp", "c_gbc", gbc[:, :], gbc_d, writes=["gbc"])
        P.dma("sp", "c_cbf", cbf[:, :], cbf_d, writes=["cbf"])
        P.dma("sp", "c_cf", cf[:, :], cf_d, writes=["cf"])
        P.dma("sp", "c_gsub", gsub[:, :], gsub_d, writes=["gsub"])
        P.dma("sp", "c_lamv", lamv[:, :], lamv_d, writes=["lamv"])

        P.op("pool", lambda e: e.memset(small[:, 65:66], EPS), writes=["epsc"])
        epsc = small[:, 65:66]
        P.op("pool", lambda e: e.memset(small[:, 66:68], -0.5), writes=["neghalf"])
        neghalf = small[:, 66:67]
        neghalf2 = small[:, 66:68]
        P.op("dve", lambda e: e.tensor_tensor(out=tmpf[:, 0:64], in0=lamv[:, 0:64], in1=lamv[:, 64:128], op=ALU.mult),
             reads=["lamv"], writes=["lt0"])
        P.op("dve", lambda e: e.reduce_sum(out=small[:, 60:61], in_=tmpf[:, 0:64], axis=AX.X),
             reads=["lt0"], writes=["s1"])
        P.op("dve", lambda e: e.tensor_tensor(out=tmpf[:, 64:128], in0=lamv[:, 128:192], in1=lamv[:, 192:256], op=ALU.mult),
             reads=["lamv"], writes=["lt1"])
        P.op("dve", lambda e: e.reduce_sum(out=small[:, 61:62], in_=tmpf[:, 64:128], axis=AX.X),
             reads=["lt1"], writes=["s2"])
        P.op("act", lambda e: e.activation(out=small[:, 62:64], in_=small[:, 60:62], func=AF.Exp),
             reads=["s1", "s2"], writes=["e12"])
        P.op("dve", lambda e: e.tensor_tensor(out=small[:, 64:65], in0=small[:, 63:64], in1=small[:, 62:63], op=ALU.subtract),
             reads=["e12"], writes=["nl0"])
        P.op("dve", lambda e: e.tensor_scalar(out=small[:, 64:65], in0=small[:, 64:65], scalar1=-LAMBDA_INIT, scalar2=None, op0=ALU.add),
             reads=["nl0"], writes=["neglam"])
        neglam = small[:, 64:65]
        P.op("dve", lambda e: e.tensor_scalar(out=gsub[:, :], in0=gsub[:, :], scalar1=1.0 - LAMBDA_INIT, scalar2=None, op0=ALU.mult),
             reads=["gsub"], writes=["gsub"])

        wbuf = [scr[:, 8192 + s * 4096: 8192 + (s + 1) * 4096].rearrange("p (c n) -> p c n", n=512) for s in range(2)] + \
               [scr[:, s * 4096:(s + 1) * 4096].rearrange("p (c n) -> p c n", n=512) for s in range(2)]

        def load_w(slot, g):
            src = wext[:, g * 512:(g + 1) * 512].rearrange("(c p) n -> p c n", p=128)
            P.dma("pool", f"w{slot}", wbuf[slot], src, writes=[f"w{slot}"])

        load_w(0, 0)
        load_w(1, 1)

        xs = [A5f[:, s * 1024:(s + 1) * 1024] for s in range(2)]
        ub = [A5flat[:, 4096 + s * 1024: 4096 + (s + 1) * 1024] for s in range(2)]
        sqj = A5flat[:, 6144:7168]
        def p0_front(b):
            rows = blk_rows(b)
            s = b % 2
            src = xr[b * 128:(b + 1) * 128, :] if b < 16 else metar
            P.dma("sp", f"xs{s}", xs[s][:rows, :], src, writes=[f"xs{s}"])
            P.op("act", lambda e: e.activation(
                out=sqj[:rows, :], in_=xs[s][:rows, :], func=AF.Square, accum_out=small[:rows, b:b + 1]),
                reads=[f"xs{s}"], writes=["sqj", f"ss{b}"])
            P.op("dve", lambda e: e.tensor_scalar(
                out=small[:rows, 17 + b:18 + b], in0=small[:rows, b:b + 1], scalar1=1.0 / D, scalar2=EPS,
                op0=ALU.mult, op1=ALU.add),
                reads=[f"ss{b}"], writes=[f"sr{b}"])
            P.op("pool", lambda e: e.tensor_tensor(
                out=small[:rows, 34 + b:35 + b], in0=small[:rows, 17 + b:18 + b], in1=neghalf[:rows, :], op=ALU.pow),
                reads=[f"sr{b}", "neghalf"], writes=[f"rstd{b}"])
            P.op("dve", lambda e: e.scalar_tensor_tensor(
                out=ub[s][:rows, :], in0=xs[s][:rows, :], scalar=small[:rows, 34 + b:35 + b], in1=gbc[:rows, :],
                op0=ALU.mult, op1=ALU.mult),
                reads=[f"xs{s}", f"rstd{b}", "gbc"], writes=[f"ub{s}"])

        def p0_tr(b):
            rows = blk_rows(b)
            s = b % 2
            pi = 4 + b % 2
            for c in range(8):
                P.op("pe", lambda e, c=c: e.transpose(
                    out=psb[pi][:, c * 128:c * 128 + rows], in_=ub[s][:rows, c * 128:(c + 1) * 128],
                    identity=ident_bf[:rows, :rows]),
                    reads=[f"ub{s}", "cbf"], writes=[f"ps{pi}"])

        def p0_back(b):
            rows = blk_rows(b)
            pi = 4 + b % 2
            srcv = psb[pi][:, :].rearrange("p (c t) -> p c t", t=128)[:, :, :rows]
            dstv = uT[:, :, b * 128:b * 128 + rows]
            if b % 2 == 0:
                P.op("act", lambda e: e.activation(out=dstv, in_=srcv, func=AF.Copy),
                     reads=[f"ps{pi}"], writes=[f"uT{b}"])
            else:
                P.op("dve", lambda e: e.tensor_copy(out=dstv, in_=srcv),
                     reads=[f"ps{pi}"], writes=[f"uT{b}"])


        TR = [(0, 512), (512, 512), (1024, 512), (1536, 512), (2048, 16)]
        pcount = [0]
        pspool = [[0, 1, 2, 3, 6, 7]]

        def nextps(pool):
            i = pool[pcount[0] % len(pool)]
            pcount[0] += 1
            return i

        def feat_unit(slot, evac, cc, t0, n):
            pi = nextps(pspool[0])
            ub_ = [f"uT{b}" for b in range(t0 // 128, (t0 + n + 127) // 128)]
            for dc in range(8):
                P.op("pe", lambda e, dc=dc: e.matmul(
                    out=ps[pi][:, :n], lhsT=wbuf[slot][:, dc, cc * 128:(cc + 1) * 128],
                    rhs=uT[:, dc, t0:t0 + n], start=(dc == 0), stop=(dc == 7)),
                    reads=[f"w{slot}"] + ub_, writes=[f"ps{pi}"])
            evac(cc, t0, n, pi)

        def tok_unit(slot, evac, b):
            rows = blk_rows(b)
            pi = nextps(pspool[0])
            for dc in range(8):
                P.op("pe", lambda e, dc=dc: e.matmul(
                    out=ps[pi][:rows, :512], lhsT=uT[:, dc, b * 128:b * 128 + rows],
                    rhs=wbuf[slot][:, dc, :], start=(dc == 0), stop=(dc == 7)),
                    reads=[f"w{slot}", f"uT{b}"], writes=[f"ps{pi}"])
            evac(b, rows, pi)

        def proj_feat(slot, evac, tr=None):
            for cc in range(4):
                for (t0, n) in (tr or TR):
                    feat_unit(slot, evac, cc, t0, n)

        def proj_tok(slot, evac, nblk):
            for b in range(nblk):
                tok_unit(slot, evac, b)

        flip = [0]

        def evac_copy_to(dst3):
            def f(cc, t0, n, pi):
                flip[0] ^= 1
                if flip[0]:
                    P.op("act", lambda e: e.activation(out=dst3[:, cc, t0:t0 + n], in_=ps[pi][:, :n], func=AF.Copy),
                         reads=[f"ps{pi}"], writes=[])
                else:
                    P.op("dve", lambda e: e.tensor_copy(out=dst3[:, cc, t0:t0 + n], in_=ps[pi][:, :n]),
                         reads=[f"ps{pi}"], writes=[])
            return f

        QP = [A1[:, h, 0:SEQ] for h in range(4)] + [mixT[:, 4 + h, :] for h in range(4)]
        for h in range(8):
            zr = slice(64, 128) if h % 2 == 0 else slice(0, 64)
            P.op("pool", lambda e, h=h, zr=zr: e.memset(QP[h][zr, :], 0.0), writes=[f"qpz{h}"])

        def evac_q(cc, t0, n, pi):
            if t0 >= SEQ:
                return
            P.op("act", lambda e: e.activation(out=QP[2 * cc][0:64, t0:t0 + n], in_=ps[pi][0:64, :n], func=AF.Copy),
                 reads=[f"ps{pi}"], writes=[])
            P.op("dve", lambda e: e.tensor_copy(out=QP[2 * cc + 1][64:128, t0:t0 + n], in_=ps[pi][64:128, :n]),
                 reads=[f"ps{pi}"], writes=[])
        def evac_v(b, rows, pi):
            P.op("act", lambda e: e.activation(out=A3[:rows, b, 0:512], in_=ps[pi][:rows, :512], func=AF.Copy),
                 reads=[f"ps{pi}"], writes=[f"v{b}"])

        def evac_g(b, rows, pi):
            P.op("act", lambda e: e.activation(out=A4[:rows, b, :], in_=ps[pi][:rows, :512], func=AF.Silu),
                 reads=[f"ps{pi}"], writes=[])

        load_w(2, 2)
        load_w(3, 3)
        evac_k = evac_copy_to(A2)
        fifo = []
        p0_front(0)
        p0_tr(0)
        for b in range(1, NB + 1):
            if b < NB:
                p0_front(b)
            if b >= 1:
                p0_back(b - 1)
                bb = b - 1
                if bb % 4 == 3 or bb == 16:
                    r = bb // 4
                    t0, n = TR[r]
                    for cc in range(4):
                        if r < 4:
                            fifo.append(lambda cc=cc, t0=t0, n=n: feat_unit(0, evac_q, cc, t0, n))
                        fifo.append(lambda cc=cc, t0=t0, n=n: feat_unit(1, evac_k, cc, t0, n))
                    for b2 in range(4 * r, min(4 * r + 4, NB)):
                        fifo.append(lambda b2=b2: tok_unit(2, evac_v, b2))
                        if b2 < 16:
                            fifo.append(lambda b2=b2: tok_unit(3, evac_g, b2))
            for _ in range(4):
                if fifo:
                    fifo.pop(0)()
            if b < NB:
                p0_tr(b)
        while fifo:
            fifo.pop(0)()
        P.barrier()
        P.op("pool", lambda e: e.memset(A5[:, 16, :], 0.0), writes=["dvmeta"])
        for b in range(NB):
            rows = blk_rows(b)
            pi = nextps([0, 1, 2, 3])
            last = (b == NB - 1)
            P.op("pe", lambda e, b=b, rows=rows, pi=pi, last=last: e.matmul(
                out=ps[pi][:rows, :512], lhsT=Dmat[:rows, :rows], rhs=A3[:rows, b, 0:512], start=True, stop=last),
                reads=[f"v{b}", "cbf"], writes=[f"ps{pi}"])
            if not last:
                P.op("pe", lambda e, b=b, pi=pi: e.matmul(
                    out=ps[pi][:128, :512], lhsT=dlast, rhs=A3[0:1, b + 1, 0:512], start=False, stop=True),
                    reads=[f"v{b + 1}"], writes=[f"ps{pi}"])
            P.op("dve", lambda e, b=b, rows=rows, pi=pi: e.tensor_copy(out=A5[:rows, b, :], in_=ps[pi][:rows, :512]),
                 reads=[f"ps{pi}"] + (["dvmeta"] if b == 16 else []), writes=[f"dv{b}"] + (["dvmeta"] if b == 16 else []))

        keepL = [scrf[:, i * 2064:(i + 1) * 2064] for i in range(2)]
        Pb = [scr[:, 8256 + i * 2176: 8256 + (i + 1) * 2176] for i in range(3)]
        tmpb = tmpf.bitcast(BF16)
        if DMAT_MOD:
            PTDs = [tmpb[:, 512 + i * 1152: 512 + (i + 1) * 1152].rearrange("p (b t) -> p b t", t=128) for i in range(2)]
            PTb = [scr[:, 14784 + i * 512: 14784 + (i + 1) * 512] for i in range(3)] + \
                  [tmpb[:, 2816 + i * 512: 2816 + (i + 1) * 512] for i in range(2)]
        else:
            PTb = [scr[:, 14784 + i * 512: 14784 + (i + 1) * 512] for i in range(3)] + \
                  [tmpb[:, 512 + i * 512: 512 + (i + 1) * 512] for i in range(7)]
        NPT = len(PTb)

        def is_dma_chunk(k, c):
            return bool(DMAT_MOD) and c >= 2
        for i in range(3):
            P.op("pool", lambda e, i=i: e.memset(Pb[i][:, :], 0.0), writes=[f"Pb{i}"])
        mixtok = [tmpf.bitcast(BF16)[:, 0:512]]
        cnt = {"z": 0, "k": 0, "p": 0, "t": 0, "pt": 0, "ev": 0}

        def rot(name, n):
            i = cnt[name] % n
            cnt[name] += 1
            return i

        def sb_chunks(j):
            blocks = list(range(j, 16))
            chunks = []
            while blocks:
                cb = blocks[:4]
                blocks = blocks[4:]
                chunks.append([(kb, 128) for kb in cb])
            if len(chunks[-1]) < 4:
                chunks[-1].append((16, 16))
            else:
                chunks.append([(16, 16)])
            return chunks

        items = [(j, h) for j in range(16) for h in range(8)]
        NI = len(items)
        ptslot = {}

        def st_z(k, c):
            j, h = items[k]
            ch = sb_chunks(j)[c]
            cc, po = h // 2, (h % 2) * 64
            n = sum(r for _, r in ch)
            t0 = ch[0][0] * 128
            off = t0 - 128 * j
            zi = rot("z", 2)
            ks = k % 2
            P.op("pe", lambda e: e.matmul(
                out=ps[zi][:, :n], lhsT=QP[h][:, j * 128:(j + 1) * 128],
                rhs=A2[:, cc, t0:t0 + n], start=True, stop=True),
                reads=[], writes=[f"ps{zi}"])
            P.op("act", lambda e: e.activation(
                out=keepL[ks][:, off:off + n], in_=ps[zi][:, :n], func=AF.Sigmoid, scale=-0.125),
                reads=[f"ps{zi}"], writes=[f"keep{ks}"])

        def st_scan(k):
            j, h = items[k]
            ntot = NTOK - 128 * j
            ks, pslot = k % 2, k % 3
            P.op("dve", lambda e: e.tensor_tensor_scan(
                out=Pb[pslot][:, :ntot], data0=keepL[ks][:, :ntot], data1=M1Z[:, :ntot], initial=1.0,
                op0=ALU.mult, op1=ALU.max),
                reads=[f"keep{ks}", "cf"], writes=[f"Pb{pslot}"])

        def st_t(k, c):
            j, h = items[k]
            ch = sb_chunks(j)[c]
            pslot = k % 3
            if is_dma_chunk(k, c):
                if c == 2:
                    nfar = 17 - j - 8
                    P.dma("sp", "dmat", PTDs[k % 2][:, 0:nfar, :], Pb[pslot][:, 1024:1024 + nfar * 128],
                          reads=[f"Pb{pslot}"], writes=[f"PTD{k % 2}", "dmat_serial"], transpose=True)
                return
            ti = 2 + rot("t", 2)
            off = ch[0][0] * 128 - 128 * j
            col = off
            for bi, (kb, r) in enumerate(ch):
                P.op("pe", lambda e, bi=bi, col=col: e.transpose(
                    out=psb[ti][:, bi * 128:(bi + 1) * 128], in_=Pb[pslot][:, col:col + 128],
                    identity=ident_bf),
                    reads=[f"Pb{pslot}", "cbf"], writes=[f"ps{ti}"])
                col += r
            pti = rot("pt", NPT)
            ptslot[(k, c)] = pti
            nb_ = len(ch)
            if ((not DMAT_MOD) and rot("ev", 4) == 3) or (DMAT_MOD and j >= 11 and c == 1):
                P.op("dve", lambda e: e.tensor_copy(out=PTb[pti][:, :nb_ * 128], in_=psb[ti][:, :nb_ * 128]),
                     reads=[f"ps{ti}"], writes=[f"PT{pti}"])
            else:
                P.op("act", lambda e: e.activation(
                    out=PTb[pti][:, :nb_ * 128], in_=psb[ti][:, :nb_ * 128], func=AF.Copy),
                    reads=[f"ps{ti}"], writes=[f"PT{pti}"])

        def st_av(k, c):
            j, h = items[k]
            ch = sb_chunks(j)[c]
            oi = 6 + (j % 2)
            hc = slice(h * 64, (h + 1) * 64)
            if is_dma_chunk(k, c):
                for bi, (kb, r) in enumerate(ch):
                    P.op("pe", lambda e, bi=bi, kb=kb: e.matmul(
                        out=ps[oi][:, hc], lhsT=PTDs[k % 2][:, kb - j - 8, :],
                        rhs=A5[:, kb, hc], start=(c == 0 and bi == 0), stop=False),
                        reads=[f"PTD{k % 2}", f"dv{kb}"], writes=[f"ps{oi}"])
                return
            pti = ptslot[(k, c)]
            for bi, (kb, r) in enumerate(ch):
                P.op("pe", lambda e, bi=bi, kb=kb: e.matmul(
                    out=ps[oi][:, hc], lhsT=PTb[pti][:, bi * 128:(bi + 1) * 128],
                    rhs=A5[:, kb, hc], start=(c == 0 and bi == 0), stop=False),
                    reads=[f"PT{pti}", f"dv{kb}"], writes=[f"ps{oi}"])

        def st_tail(k):
            j, h = items[k]
            oi = 6 + (j % 2)
            hc = slice(h * 64, (h + 1) * 64)
            P.op("pe", lambda e: e.matmul(
                out=ps[oi][:, hc], lhsT=negM2T, rhs=A5[:, j, hc], start=False, stop=False),
                reads=["cbf", f"dv{j}"], writes=[f"ps{oi}"])
            P.op("pe", lambda e: e.matmul(
                out=ps[oi][:, hc], lhsT=ident_bf, rhs=A3[:, j, hc], start=False, stop=True),
                reads=["cbf"], writes=[f"ps{oi}"])
            if h == 7:
                P.op("dve", lambda e: e.tensor_tensor(
                    out=mixtok[0][:, :], in0=ps[oi][:, :512], in1=A4[:, j, :], op=ALU.mult),
                    reads=[f"ps{oi}"], writes=["mixtok"])
                ti = 4 + (j % 2)

                def sb_mix(j=j, ti=ti):
                    for c in range(4):
                        P.op("pe", lambda e, c=c: e.transpose(
                            out=psb[ti][:, c * 128:(c + 1) * 128], in_=mixtok[0][:, c * 128:(c + 1) * 128], identity=ident_bf),
                            reads=["mixtok", "cbf"], writes=[f"ps{ti}"])
                    P.op("act", lambda e: e.activation(
                        out=mixT[:, 0:4, j * 128:(j + 1) * 128],
                        in_=psb[ti][:, 0:512].rearrange("p (c t) -> p c t", t=128), func=AF.Copy),
                        reads=[f"ps{ti}"], writes=[])
                sb_pending.append(sb_mix)
            elif sb_pending:
                sb_pending.pop(0)()

        sb_pending = []

        def nch(k):
            return len(sb_chunks(items[k][0])) if 0 <= k < NI else 0

        SK = 2
        for k in range(NI + SK + 1):
            if 0 <= k - 1 < NI:
                st_scan(k - 1)
            na, nt_, nv_ = nch(k), nch(k - SK), nch(k - SK - 1)
            for c in range(na):
                st_z(k, c)
            for c in range(max(nt_, nv_)):
                if c < nt_:
                    st_t(k - SK, c)
                if c < nv_:
                    st_av(k - SK - 1, c)
            if 0 <= k - SK - 1 < NI:
                st_tail(k - SK - 1)
            if k == NI:
                for slot, g in ((2, 6), (3, 7)):
                    src = wext[:, g * 512:(g + 1) * 512].rearrange("(c p) n -> p c n", p=128)
                    P.dma("pool", f"w{slot}", wbuf[slot], src, writes=[f"w{slot}", "keep0", "keep1", "dmat_serial"])
        while sb_pending:
            sb_pending.pop(0)()
        P.barrier()
        load_w(0, 4)
        load_w(1, 5)

        pspool[0] = [0, 1, 2, 3, 4, 5, 6, 7]
        ropec = A5f[:, 0:NTOK]
        ropes = A5f[:, NTOK:2 * NTOK]
        P.dma("sp", "c_rc", ropec, ropec_d, writes=["ropec"])
        P.dma("sp", "c_rs", ropes, ropes_d, writes=["ropes"])
        vaug4 = A3[:, :, :].rearrange("p b (h e) -> p b h e", e=130)
        P.op("pool", lambda e: e.memset(vaug4[:, :, :, 128:129], 1.0), writes=["vones"])
        P.op("pool", lambda e: e.memset(vaug4[:, :, :, 129:130], 0.0), writes=["vzero"])

        QPD = [A1[:, h, 0:SEQ] for h in range(4)] + [mixT[:, 4 + h, :] for h in range(4)]
        for h in range(8):
            zr = slice(64, 128) if h % 2 == 0 else slice(0, 64)
            P.op("pool", lambda e, h=h, zr=zr: e.memset(QPD[h][zr, :], 0.0), writes=[f"qpdz{h}"])

        qhb = [scr[:, i * 512:(i + 1) * 512] for i in range(4)]
        qlb = [scr[:, 2048 + i * 512: 2048 + (i + 1) * 512] for i in range(4)]

        def proj_rope(dst3, slot, padded=False):
            units = [(cc, t0, n) for cc in range(4) for (t0, n) in (TR[:4] if padded else TR)]
            pend = None

            def finish(u):
                cc, t0, n, pa, qi = u
                pb = nextps([0, 1, 2, 3, 4, 5, 6, 7])
                P.op("pe", lambda e: e.matmul(out=ps[pb][:, :n], lhsT=permT_bf, rhs=qhb[qi][:, :n], start=True, stop=False),
                     reads=[f"qs{qi}", "cbf"], writes=[f"ps{pb}"])
                P.op("pe", lambda e: e.matmul(out=ps[pb][:, :n], lhsT=permT_bf, rhs=qlb[qi][:, :n], start=False, stop=True),
                     reads=[f"ql{qi}", "cbf"], writes=[f"ps{pb}"])
                ta = rot("k", 2)
                t1 = tmpf[:, ta * 1024: ta * 1024 + n]
                t2 = tmpf[:, ta * 1024 + 512: ta * 1024 + 512 + n]
                P.op("dve", lambda e: e.tensor_tensor(out=t1, in0=ps[pa][:, :n], in1=ropec[:, t0:t0 + n], op=ALU.mult),
                     reads=[f"ps{pa}", "ropec"], writes=[f"t1{ta}"])
                P.op("dve", lambda e: e.tensor_tensor(out=t2, in0=ps[pb][:, :n], in1=ropes[:, t0:t0 + n], op=ALU.mult),
                     reads=[f"ps{pb}", "ropes"], writes=[f"t2{ta}"])
                if padded:
                    P.op("pool", lambda e: e.tensor_tensor(
                        out=QPD[2 * cc][0:64, t0:t0 + n], in0=t1[0:64, :], in1=t2[0:64, :], op=ALU.add),
                        reads=[f"t1{ta}", f"t2{ta}", f"qpdz{2 * cc}"], writes=[f"qpa{ta}"])
                    P.op("pool", lambda e: e.tensor_tensor(
                        out=QPD[2 * cc + 1][64:128, t0:t0 + n], in0=t1[64:128, :], in1=t2[64:128, :], op=ALU.add),
                        reads=[f"t1{ta}", f"t2{ta}", f"qpdz{2 * cc + 1}"], writes=[f"qpb{ta}"])
                else:
                    P.op("pool", lambda e: e.tensor_tensor(
                        out=dst3[:, cc, t0:t0 + n], in0=t1, in1=t2, op=ALU.add),
                        reads=[f"t1{ta}", f"t2{ta}"], writes=[f"qpa{ta}"])

            for (cc, t0, n) in units:
                pa = nextps([0, 1, 2, 3, 4, 5, 6, 7])
                for dc in range(8):
                    P.op("pe", lambda e, dc=dc, pa=pa, cc=cc, t0=t0, n=n: e.matmul(
                        out=ps[pa][:, :n], lhsT=wbuf[slot][:, dc, cc * 128:(cc + 1) * 128],
                        rhs=uT[:, dc, t0:t0 + n], start=(dc == 0), stop=(dc == 7)),
                        reads=[f"w{slot}"], writes=[f"ps{pa}"])
                qi = rot("p", 4)
                P.op("act", lambda e, pa=pa, qi=qi, n=n: e.activation(out=qhb[qi][:, :n], in_=ps[pa][:, :n], func=AF.Copy),
                     reads=[f"ps{pa}"], writes=[f"qs{qi}"])
                P.op("dve", lambda e, pa=pa, qi=qi, n=n: e.tensor_tensor(
                    out=qlb[qi][:, :n], in0=ps[pa][:, :n], in1=qhb[qi][:, :n], op=ALU.subtract),
                    reads=[f"ps{pa}", f"qs{qi}"], writes=[f"ql{qi}"])
                if pend is not None:
                    finish(pend)
                pend = (cc, t0, n, pa, qi)
            finish(pend)

        def evac_vd(b, rows, pi):
            P.op("act", lambda e: e.activation(
                out=vaug4[:rows, b, :, 0:128], in_=ps[pi][:rows, :512].rearrange("p (h e) -> p h e", e=128), func=AF.Copy),
                reads=[f"ps{pi}"], writes=[])
        if DFP_LEVEL >= 1:
            proj_tok(2, evac_vd, NB)
        if DFP_LEVEL >= 2:
            proj_tok(3, evac_g, 16)
        if DFP_LEVEL >= 3:
            proj_rope(None, 0, padded=True)
        if DFP_LEVEL >= 4:
            proj_rope(A2, 1)
        P.barrier()

        wo = scr[:, 8192:16384].rearrange("p (c n) -> p c n", n=1024)
        P.dma("pool", "w0", wo, wout.rearrange("(c p) n -> p c n", p=128), writes=["wo"])
        gfin = A5f[:, 0:1024]
        P.dma("sp", "c_rc", gfin, gfin_d, writes=["gfin"])
        ETb = [scr[:, i * 512:(i + 1) * 512] for i in range(8)]
        of32 = [tmpf[:, i * 128:(i + 1) * 128] for i in range(2)]
        yf32 = [tmpf[:, 256 + i * 128: 256 + (i + 1) * 128] for i in range(2)]
        sqj2 = tmpf[:, 0:128]
        mixtok2s = [scr[:, 4096:4608], scr[:, 4608:5120], scr[:, 5120:5632]]
        sc = {"i": 0}
        dsteps = []
        for j in range(16):
            for hp in range(2):
                st_l = [[kb] for kb in range(j, NB)]
                for si, kbs in enumerate(st_l):
                    dsteps.append({"j": j, "hp": hp, "kbs": kbs, "last": si == len(st_l) - 1,
                                   "par": ((2 * j + hp) % 2) if DF_ODB else 1})

        def d_zexp(stp):
            j, hp, kbs = stp["j"], stp["hp"], stp["kbs"]
            rows = blk_rows(kbs[0])
            zb = [rot("z", DF_ZP) for _ in kbs]
            eb = [rot("k", 8) for _ in kbs]
            stp["eb"] = eb
            for bi, kb in enumerate(kbs):
                for p2 in range(2):
                    hc0 = 4 * hp + 2 * p2
                    cc = hc0 // 2
                    if hc0 < 4:
                        rhs2 = A1[:, hc0:hc0 + 2, j * 128:(j + 1) * 128]
                    else:
                        rhs2 = mixT[:, hc0:hc0 + 2, j * 128:(j + 1) * 128]
                    P.op("pe", lambda e, zi=zb[bi], p2=p2, cc=cc, kb=kb, rhs2=rhs2: e.matmul(
                        out=ps[zi][:rows, p2 * 256:(p2 + 1) * 256], lhsT=A2[:, cc, kb * 128:kb * 128 + rows],
                        rhs=rhs2, start=True, stop=True),
                        reads=[f"qpd{j}"], writes=[f"ps{zb[bi]}"])
                zi, ei = zb[bi], eb[bi]
                P.op("act", lambda e, zi=zi, ei=ei: e.activation(
                    out=ETb[ei][:rows, :], in_=ps[zi][:rows, :], func=AF.Exp, scale=0.125),
                    reads=[f"ps{zi}"], writes=[f"ET{ei}"])
                if kb == j:
                    rect = ETb[ei][0:64, :].rearrange("p (i t) -> p i t", t=128)[:, :, 64:128]
                    P.op("pool", lambda e, rect=rect: e.memset(rect, 0.0),
                         reads=[], writes=[f"ET{ei}"])

        def d_av(stp):
            j, hp, kbs, eb = stp["j"], stp["hp"], stp["kbs"], stp["eb"]
            rows = blk_rows(kbs[0])
            for bi, kb in enumerate(kbs):
                for i in range(4):
                    head = (4 * hp + i) // 2
                    ei = eb[bi]
                    col = i * 128
                    ob, oc = 4 + 2 * stp["par"] + i // 2, (i % 2) * 256
                    P.op("pe", lambda e, i=i, ei=ei, col=col, kb=kb, head=head, ob=ob, oc=oc: e.matmul(
                        out=ps[ob][:, oc:oc + 129], lhsT=ETb[ei][:rows, col:col + 128],
                        rhs=A3[:rows, kb, head * 130:head * 130 + 129],
                        start=(kb == j and i % 2 == 0), stop=(kb == NB - 1), skip_group_check=True),
                        reads=[f"ET{ei}", "vones"], writes=[f"ps{ob}"])
            if stp["last"]:
                d_epilogue(j, hp, stp["par"])

        def d_epilogue(j, hp, par):
            k = sc["i"] % 4
            sc["i"] += 1
            q = k % 2
            cb = 80 + k * 16
            banks = [4 + 2 * par, 5 + 2 * par]
            tb = 512 + q * 768
            t1b = [tmpf[:, tb + hl * 128: tb + (hl + 1) * 128] for hl in range(2)]
            ofb = [tmpf[:, tb + 256 + hl * 128: tb + 256 + (hl + 1) * 128] for hl in range(2)]
            yfb = [tmpf[:, tb + 512 + hl * 128: tb + 512 + (hl + 1) * 128] for hl in range(2)]
            for hl in range(2):
                pb_ = banks[hl]
                P.op("dve", lambda e, pb_=pb_, hl=hl: e.reciprocal(
                    out=small[:, cb + 2 * hl:cb + 2 * hl + 2], in_=ps[pb_][:, 128:512:256]),
                    reads=[f"ps{pb_}"], writes=[f"rz{k}_{hl}"])
            P.op("dve", lambda e: e.tensor_scalar(
                out=small[:, cb + 4:cb + 6], in0=small[:, cb + 1:cb + 4:2], scalar1=neglam, scalar2=None, op0=ALU.mult),
                reads=[f"rz{k}_0", f"rz{k}_1", "neglam"], writes=[f"c1{k}"])
            act_t1 = j >= EPI_ACT_FROM_TILE
            for hl in range(2):
                pb_ = banks[hl]
                if act_t1:
                    P.op("act", lambda e, pb_=pb_, hl=hl: e.activation(
                        out=t1b[hl], in_=ps[pb_][:, 256:384], func=AF.Copy, scale=small[:, cb + 4 + hl:cb + 5 + hl]),
                        reads=[f"ps{pb_}", f"c1{k}"], writes=[f"t1e{q}_{hl}"])
                else:
                    P.op("dve", lambda e, pb_=pb_, hl=hl: e.tensor_scalar(
                        out=t1b[hl], in0=ps[pb_][:, 256:384], scalar1=small[:, cb + 4 + hl:cb + 5 + hl], scalar2=None, op0=ALU.mult),
                        reads=[f"ps{pb_}", f"c1{k}"], writes=[f"t1e{q}_{hl}"])
            for hl in range(2):
                pb_ = banks[hl]
                P.op("dve", lambda e, pb_=pb_, hl=hl: e.scalar_tensor_tensor(
                    out=ofb[hl], in0=ps[pb_][:, 0:128], scalar=small[:, cb + 2 * hl:cb + 2 * hl + 1], in1=t1b[hl],
                    op0=ALU.mult, op1=ALU.add),
                    reads=[f"ps{pb_}", f"rz{k}_{hl}", f"t1e{q}_{hl}"], writes=[f"of{q}_{hl}"])
            for hl in range(2):
                P.op("dve", lambda e, hl=hl: e.scalar_tensor_tensor(
                    out=sqj2, in0=ofb[hl], scalar=1.0, in1=ofb[hl], op0=ALU.mult, op1=ALU.mult,
                    accum_out=small[:, cb + 6 + hl:cb + 7 + hl]),
                    reads=[f"of{q}_{hl}"], writes=["sqj2", f"ss2{k}_{hl}"])
            P.op("dve", lambda e: e.tensor_scalar(
                out=small[:, cb + 8:cb + 10], in0=small[:, cb + 6:cb + 8], scalar1=1.0 / 128, scalar2=EPS,
                op0=ALU.mult, op1=ALU.add),
                reads=[f"ss2{k}_0", f"ss2{k}_1"], writes=[f"sq2{k}"])
            P.op("pool", lambda e: e.tensor_tensor(
                out=small[:, cb + 10:cb + 12], in0=small[:, cb + 8:cb + 10], in1=neghalf2, op=ALU.pow),
                reads=[f"sq2{k}", "neghalf"], writes=[f"rs2{k}"])
            def part_b():
                for hl in range(2):
                    head = 2 * hp + hl
                    P.op("dve", lambda e, hl=hl: e.scalar_tensor_tensor(
                        out=yfb[hl], in0=ofb[hl], scalar=small[:, cb + 10 + hl:cb + 11 + hl], in1=gsub[:, :],
                        op0=ALU.mult, op1=ALU.mult),
                        reads=[f"of{q}_{hl}", f"rs2{k}", "gsub"], writes=[f"yf{q}_{hl}"])
                    P.op("pool", lambda e, hl=hl, head=head: e.tensor_tensor(
                        out=mixtok2s[j % 3][:, head * 128:(head + 1) * 128], in0=yfb[hl], in1=A4[:, j, head * 128:(head + 1) * 128], op=ALU.mult),
                        reads=[f"yf{q}_{hl}"], writes=[f"mt2_{j % 3}_{head}"])
                if hp == 1:
                    pending_mix.append([j, 10])
            while pend_b:
                pend_b.pop(0)()
            pend_b.append(part_b)

        pend_b = []

        def d_mix(j):
            ti = rot("z", DF_ZP)
            mt = mixtok2s[j % 3]
            for c in range(4):
                P.op("pe", lambda e, ti=ti, c=c: e.transpose(
                    out=psb[ti][:, c * 128:(c + 1) * 128], in_=mt[:, c * 128:(c + 1) * 128], identity=ident_bf),
                    reads=[f"mt2_{j % 3}_{c}", "cbf"], writes=[f"ps{ti}"])
            P.op("dve", lambda e, ti=ti, j=j: e.tensor_copy(
                out=mixT[:, 4:8, j * 128:(j + 1) * 128],
                in_=psb[ti][:, 0:512].rearrange("p (c t) -> p c t", t=128)),
                reads=[f"ps{ti}"], writes=[f"qpd{j}"])

        pending_mix = []
        DSK = DF_DSK
        for idx in range(len(dsteps) + DSK):
            if idx < len(dsteps):
                d_zexp(dsteps[idx])
            for pm in list(pending_mix):
                pm[1] -= 1
                if pm[1] <= 0:
                    pending_mix.remove(pm)
                    d_mix(pm[0])
            if idx >= DSK:
                d_av(dsteps[idx - DSK])
        while pend_b:
            pend_b.pop(0)()
        for pm in pending_mix:
            d_mix(pm[0])
        P.barrier()

        xs2 = [scrf[:, s * 1024:(s + 1) * 1024] for s in range(4)]
        yo = [tmpf[:, 0:1024], tmpf[:, 1024:2048]]
        sqj3 = A4[:, 0:2, :].rearrange("p a n -> p (a n)")

        def p5_front(j):
            s = j % 4
            P.dma("sp", f"xs{s}", xs2[s], xr[j * 128:(j + 1) * 128, :], writes=[f"xs2{s}"])
            pa, pb = 2 * (j % 4), 2 * (j % 4) + 1
            for (pi, half) in ((pa, 0), (pb, 1)):
                for c in range(8):
                    P.op("pe", lambda e, pi=pi, half=half, c=c: e.matmul(
                        out=ps[pi][:, :512], lhsT=mixT[:, c, j * 128:(j + 1) * 128],
                        rhs=wo[:, c, half * 512:(half + 1) * 512], start=(c == 0), stop=(c == 7)),
                        reads=["wo"], writes=[f"ps{pi}"])
            for (pi, half) in ((pa, 0), (pb, 1)):
                P.op("dve", lambda e, pi=pi, half=half: e.tensor_tensor(
                    out=xs2[s][:, half * 512:(half + 1) * 512], in0=ps[pi][:, :512],
                    in1=xs2[s][:, half * 512:(half + 1) * 512], op=ALU.add),
                    reads=[f"ps{pi}", f"xs2{s}"], writes=[f"xs2{s}"])
            cb = 200 + (j % 4) * 4
            P.op("act", lambda e: e.activation(
                out=sqj3, in_=xs2[s], func=AF.Square, accum_out=small[:, cb:cb + 1]),
                reads=[f"xs2{s}"], writes=["sqj3", f"ss3{j % 4}"])

        def p5_back(j):
            s = j % 4
            so = j % 2
            cb = 200 + (j % 4) * 4
            P.op("dve", lambda e: e.tensor_scalar(
                out=small[:, cb + 1:cb + 2], in0=small[:, cb:cb + 1], scalar1=1.0 / D, scalar2=EPS,
                op0=ALU.mult, op1=ALU.add),
                reads=[f"ss3{j % 4}"], writes=[f"sq3{j % 4}"])
            P.op("pool", lambda e: e.tensor_tensor(
                out=small[:, cb + 2:cb + 3], in0=small[:, cb + 1:cb + 2], in1=neghalf, op=ALU.pow),
                reads=[f"sq3{j % 4}", "neghalf"], writes=[f"rs3{j % 4}"])
            P.op("dve", lambda e: e.scalar_tensor_tensor(
                out=yo[so], in0=xs2[s], scalar=small[:, cb + 2:cb + 3], in1=gfin, op0=ALU.mult, op1=ALU.mult),
                reads=[f"xs2{s}", f"rs3{j % 4}", "gfin"], writes=[f"yo{so}"])
            P.dma("pool", f"o{so}", out_d[j * 128:(j + 1) * 128, :], yo[so], reads=[f"yo{so}"], writes=[f"out{j}"])

        for j in range(17):
            if j < 16:
                p5_front(j)
            if j >= 1:
                p5_back(j - 1)
        P.barrier()

        with nc.Block() as block:
            @block.tensor
            def _(e):
                for f in P.streams["pe"]:
                    f(e)

            @block.scalar
            def _(e):
                for f in P.streams["act"]:
                    f(e)

            @block.vector
            def _(e):
                for f in P.streams["dve"]:
                    f(e)

            @block.gpsimd
            def _(e):
                for f in P.streams["pool"]:
                    f(e)

            @block.sync
            def _(e):
                for f in P.streams["sp"]:
                    f(e)
    return nc


def _consts():
    bf = ml_dtypes.bfloat16
    k = np.arange(128)[:, None]
    m = np.arange(128)[None, :]
    ident = (k == m).astype(np.float32)
    negM2T = -(k < m).astype(np.float32)
    Dm = (k == m + 1).astype(np.float32) - (k == m).astype(np.float32)
    dlast = np.zeros((128, 128), np.float32)
    dlast[0, 127] = 1.0
    swp = (np.arange(128) // 64) * 64 + ((np.arange(128) % 64) + 32) % 64
    permT = np.zeros((128, 128), np.float32)
    permT[swp, np.arange(128)] = 1.0
    cbf = np.concatenate([ident, negM2T, Dm, dlast, permT], axis=1).astype(bf)
    M1 = (m <= k).astype(np.float32)
    cf = np.concatenate([permT, M1, np.zeros((128, 1936), np.float32)], axis=1).astype(np.float32)
    inv = (1.0 / (np.float32(10000.0) ** (np.arange(0, 64, 2, dtype=np.float32) / np.float32(64)))).astype(np.float32)
    pos = (NTOK - 1 - np.arange(NTOK)).astype(np.float32)
    ang = (pos[None, :] * inv[:, None]).astype(np.float32)
    cos = np.cos(ang).astype(np.float32)
    sin = np.sin(ang).astype(np.float32)
    p = np.arange(128)
    ropec = cos[p % 32, :]
    sign = np.where((p % 64) < 32, -1.0, 1.0).astype(np.float32)[:, None]
    ropes = (sin[p % 32, :] * sign).astype(np.float32)
    return cbf, cf, np.ascontiguousarray(ropec), np.ascontiguousarray(ropes)


_NC_CACHE = {}


def kernel(x, meta_tokens, norm_gain, w_in, w_out, lambda_q1, lambda_k1, lambda_q2, lambda_k2,
           subln_gain, final_norm_gain):
    x = np.asarray(x, np.float32)
    B = x.shape[0]
    w = np.asarray(w_in, np.float32)[0]
    wext = np.ascontiguousarray(w)
    wout = np.ascontiguousarray(np.asarray(w_out, np.float32)[0])
    rep = lambda v, n: np.ascontiguousarray(np.broadcast_to(np.asarray(v, np.float32).reshape(1, n), (128, n)))
    gbc = rep(norm_gain[0], D)
    gfin = rep(final_norm_gain, D)
    gsub = rep(subln_gain[0], 128)
    lamv = np.ascontiguousarray(np.concatenate(
        [rep(lambda_q1[0], 64), rep(lambda_k1[0], 64), rep(lambda_q2[0], 64), rep(lambda_k2[0], 64)], axis=1))
    cbf, cf, ropec, ropes = _consts()
    metar = np.ascontiguousarray(np.asarray(meta_tokens, np.float32)[::-1])
    if "nc" not in _NC_CACHE:
        _NC_CACHE["nc"] = build_nc(DEBUG)
    nc = _NC_CACHE["nc"]
    in_maps = []
    for b in range(B):
        in_maps.append({
            "xr": np.ascontiguousarray(x[b, ::-1, :]), "metar": metar, "wext": wext, "wout": wout,
            "gbc": gbc, "gfin": gfin, "gsub": gsub, "lamv": lamv, "ropec": ropec, "ropes": ropes,
            "cbf": cbf, "cf": cf,
        })
    res = run_bass_kernel_spmd(nc, in_maps, core_ids=list(range(B)))
    outs = [np.asarray(r["out"], np.float32)[::-1] for r in res.results]
    return np.ascontiguousarray(np.stack(outs, axis=0))
```

```python
import contextlib
import numpy as np
import ml_dtypes
import concourse.bass as bass
import concourse.mybir as mybir
from concourse.bass_utils import run_bass_kernel_spmd

F32 = mybir.dt.float32
BF16 = mybir.dt.bfloat16
AF = mybir.ActivationFunctionType
ALU = mybir.AluOpType
AX = mybir.AxisListType

SEQ = 2048
NMETA = 16
NTOK = SEQ + NMETA
D = 1024
NB = 17
EPS = 1e-6
LAMBDA_INIT = 0.8 - 0.6 * float(np.exp(-0.3 * 0))

DEBUG = False
STOP_AFTER = 99
DF_META = True
DFP_LEVEL = 9
DF_ODB = True
DF_DSK = 4
DF_ZP = 4
DF_LEVEL = 9
DMAT_MOD = 2
EPI_ACT_FROM_TILE = 16


def blk_rows(b):
    return 128 if b < 16 else 16


class Prog:
    ENG = ("pe", "act", "dve", "pool", "sp")

    def __init__(self, nc, stack):
        self.nc = nc
        self.stack = stack
        self.streams = {e: [] for e in self.ENG}
        self.sems = {}
        self.semval = {}
        for e in self.ENG:
            self.sems[e] = stack.enter_context(nc.semaphore("s_" + e))
            self.semval[e] = 0
        self.known = {e: {} for e in self.ENG}
        self.snap = {}
        self.res = {}
        self.ninstr = 0
        self.enabled = True
        self.nbar = 0

    def dma_sem(self, key):
        if key not in self.sems:
            self.sems[key] = self.stack.enter_context(self.nc.semaphore("d_" + key))
            self.semval[key] = 0
        return key

    def _wait(self, eng, ev):
        key, val = ev
        if self.known[eng].get(key, 0) >= val:
            return
        self.known[eng][key] = val
        inherited = self.snap.get((key, val))
        if inherited:
            kn = self.known[eng]
            for k2, v2 in inherited.items():
                if k2 != eng and kn.get(k2, 0) < v2:
                    kn[k2] = v2
        sem = self.sems[key]
        self.streams[eng].append(lambda e, sem=sem, val=val: e.wait_ge(sem, val))

    def _deps(self, eng, reads, writes):
        evs = []
        for r in reads:
            st = self.res.get(r)
            if st and st["w"]:
                evs.append((st["w"], "raw"))
        for w in writes:
            st = self.res.get(w)
            if st:
                if st["w"]:
                    evs.append((st["w"], "waw"))
                for k, v in st["r"].items():
                    evs.append(((k, v), "war"))
        for ev, kind in evs:
            key = ev[0]
            if key == eng:
                if eng in ("pe", "sp") or kind == "war":
                    continue
            self._wait(eng, ev)

    def _commit(self, ev, reads, writes):
        for r in reads:
            st = self.res.setdefault(r, {"w": None, "r": {}})
            k, v = ev
            if st["r"].get(k, 0) < v:
                st["r"][k] = v
        for w in writes:
            self.res[w] = {"w": ev, "r": {}}

    def op(self, eng, fn, reads=(), writes=()):
        if not self.enabled:
            return
        self._deps(eng, reads, writes)
        self.semval[eng] += 1
        ev = (eng, self.semval[eng])
        self.snap[ev] = dict(self.known[eng])
        sem = self.sems[eng]
        self.streams[eng].append(lambda e, fn=fn, sem=sem: fn(e).then_inc(sem, 1))
        self._commit(ev, reads, writes)
        self.ninstr += 1

    def dma(self, eng, slot, out, in_, reads=(), writes=(), transpose=False):
        if not self.enabled:
            return
        key = self.dma_sem(slot)
        self._deps(eng, reads, writes)
        self.semval[key] += 16
        ev = (key, self.semval[key])
        sem = self.sems[key]
        if transpose:
            self.streams[eng].append(
                lambda e, out=out, in_=in_, sem=sem: e.dma_start_transpose(out=out, in_=in_).then_inc(sem, 16))
        else:
            self.streams[eng].append(
                lambda e, out=out, in_=in_, sem=sem: e.dma_start(out=out, in_=in_).then_inc(sem, 16))
        self._commit(ev, reads, writes)
        self.ninstr += 1

    def barrier(self):
        if not self.enabled:
            return
        self.nbar += 1
        if self.nbar > STOP_AFTER:
            self.enabled = False
        for e in self.ENG:
            for k, v in self.semval.items():
                if k != e and v > 0:
                    self._wait(e, (k, v))
        self.res = {}


def build_nc(dbg=False):
    nc = bass.Bass("TRN2", target_bir_lowering=False)

    def din(name, shape, dt=F32):
        return nc.dram_tensor(name, list(shape), dt, kind="ExternalInput").ap()

    xr = din("xr", [SEQ, D])
    metar = din("metar", [NMETA, D])
    wext = din("wext", [D, 4096])
    wout = din("wout", [D, D])
    gbc_d = din("gbc", [128, D])
    gfin_d = din("gfin", [128, D])
    gsub_d = din("gsub", [128, 128])
    lamv_d = din("lamv", [128, 256])
    ropec_d = din("ropec", [128, NTOK])
    ropes_d = din("ropes", [128, NTOK])
    cbf_d = din("cbf", [128, 640], BF16)
    cf_d = din("cf", [128, 2192])
    out_d = nc.dram_tensor("out", [SEQ, D], F32, kind="ExternalOutput").ap()
    if dbg:
        dbg_d = nc.dram_tensor("dbg", [128, 8 * NTOK], F32, kind="ExternalOutput").ap()

    with contextlib.ExitStack() as st:
        def sb(name, shape, dt):
            return st.enter_context(nc.sbuf_tensor(name, list(shape), dt))

        uT = sb("uT", [128, 8, NTOK], BF16)
        A1 = sb("A1", [128, 4, NTOK], BF16)
        A2 = sb("A2", [128, 4, NTOK], BF16)
        A3 = sb("A3", [128, NB, 520], BF16)
        A4 = sb("A4", [128, 16, 512], BF16)
        A5flat = sb("A5", [128, NB * 512], BF16)
        A5 = A5flat[:, :].rearrange("p (b n) -> p b n", n=512)
        A5f = A5flat.bitcast(F32)
        mixT = sb("mixT", [128, 8, SEQ], BF16)
        scr = sb("scr", [128, 16384], BF16)
        scrf = scr.bitcast(F32)
        tmpf = sb("tmpf", [128, 2048], F32)
        gbc = sb("gbcs", [128, D], F32)
        cbf = sb("cbfs", [128, 640], BF16)
        cf = sb("cfs", [128, 2192], F32)
        gsub = sb("gsubs", [128, 128], F32)
        lamv = sb("lamvs", [128, 256], F32)
        small = sb("small", [128, 256], F32)
        ps = [st.enter_context(nc.psum_tensor(f"ps{i}", [128, 512], F32)) for i in range(8)]
        psb = [p.bitcast(BF16) for p in ps]

        P = Prog(nc, st)

        ident_bf = cbf[:, 0:128]
        negM2T = cbf[:, 128:256]
        Dmat = cbf[:, 256:384]
        dlast = cbf[0:1, 384:512]
        permT_bf = cbf[:, 512:640]
        permT = cf[:, 0:128]
        M1Z = cf[:, 128:2192]

        P.dma("sp", "c_gbc", gbc[:, :], gbc_d, writes=["gbc"])
        P.dma("sp", "c_cbf", cbf[:, :], cbf_d, writes=["cbf"])
        P.dma("sp", "c_cf", cf[:, :], cf_d, writes=["cf"])
        P.dma("sp", "c_gsub", gsub[:, :], gsub_d, writes=["gsub"])
        P.dma("sp", "c_lamv", lamv[:, :], lamv_d, writes=["lamv"])

        P.op("pool", lambda e: e.memset(small[:, 65:66], EPS), writes=["epsc"])
        epsc = small[:, 65:66]
        P.op("pool", lambda e: e.memset(small[:, 66:68], -0.5), writes=["neghalf"])
        neghalf = small[:, 66:67]
        neghalf2 = small[:, 66:68]
        P.op("dve", lambda e: e.tensor_tensor(out=tmpf[:, 0:64], in0=lamv[:, 0:64], in1=lamv[:, 64:128], op=ALU.mult),
             reads=["lamv"], writes=["lt0"])
        P.op("dve", lambda e: e.reduce_sum(out=small[:, 60:61], in_=tmpf[:, 0:64], axis=AX.X),
             reads=["lt0"], writes=["s1"])
        P.op("dve", lambda e: e.tensor_tensor(out=tmpf[:, 64:128], in0=lamv[:, 128:192], in1=lamv[:, 192:256], op=ALU.mult),
             reads=["lamv"], writes=["lt1"])
        P.op("dve", lambda e: e.reduce_sum(out=small[:, 61:62], in_=tmpf[:, 64:128], axis=AX.X),
             reads=["lt1"], writes=["s2"])
        P.op("act", lambda e: e.activation(out=small[:, 62:64], in_=small[:, 60:62], func=AF.Exp),
             reads=["s1", "s2"], writes=["e12"])
        P.op("dve", lambda e: e.tensor_tensor(out=small[:, 64:65], in0=small[:, 63:64], in1=small[:, 62:63], op=ALU.subtract),
             reads=["e12"], writes=["nl0"])
        P.op("dve", lambda e: e.tensor_scalar(out=small[:, 64:65], in0=small[:, 64:65], scalar1=-LAMBDA_INIT, scalar2=None, op0=ALU.add),
             reads=["nl0"], writes=["neglam"])
        neglam = small[:, 64:65]
        P.op("dve", lambda e: e.tensor_scalar(out=gsub[:, :], in0=gsub[:, :], scalar1=1.0 - LAMBDA_INIT, scalar2=None, op0=ALU.mult),
             reads=["gsub"], writes=["gsub"])

        wbuf = [scr[:, 8192 + s * 4096: 8192 + (s + 1) * 4096].rearrange("p (c n) -> p c n", n=512) for s in range(2)] + \
               [scr[:, s * 4096:(s + 1) * 4096].rearrange("p (c n) -> p c n", n=512) for s in range(2)]

        def load_w(slot, g):
            src = wext[:, g * 512:(g + 1) * 512].rearrange("(c p) n -> p c n", p=128)
            P.dma("pool", f"w{slot}", wbuf[slot], src, writes=[f"w{slot}"])

        load_w(0, 0)
        load_w(1, 1)

        xs = [A5f[:, s * 1024:(s + 1) * 1024] for s in range(2)]
        ub = [A5flat[:, 4096 + s * 1024: 4096 + (s + 1) * 1024] for s in range(2)]
        sqj = A5flat[:, 6144:7168]
        def p0_front(b):
            rows = blk_rows(b)
            s = b % 2
            src = xr[b * 128:(b + 1) * 128, :] if b < 16 else metar
            P.dma("sp", f"xs{s}", xs[s][:rows, :], src, writes=[f"xs{s}"])
            P.op("act", lambda e: e.activation(
                out=sqj[:rows, :], in_=xs[s][:rows, :], func=AF.Square, accum_out=small[:rows, b:b + 1]),
                reads=[f"xs{s}"], writes=["sqj", f"ss{b}"])
            P.op("dve", lambda e: e.tensor_scalar(
                out=small[:rows, 17 + b:18 + b], in0=small[:rows, b:b + 1], scalar1=1.0 / D, scalar2=EPS,
                op0=ALU.mult, op1=ALU.add),
                reads=[f"ss{b}"], writes=[f"sr{b}"])
            P.op("pool", lambda e: e.tensor_tensor(
                out=small[:rows, 34 + b:35 + b], in0=small[:rows, 17 + b:18 + b], in1=neghalf[:rows, :], op=ALU.pow),
                reads=[f"sr{b}", "neghalf"], writes=[f"rstd{b}"])
            P.op("dve", lambda e: e.scalar_tensor_tensor(
                out=ub[s][:rows, :], in0=xs[s][:rows, :], scalar=small[:rows, 34 + b:35 + b], in1=gbc[:rows, :],
                op0=ALU.mult, op1=ALU.mult),
                reads=[f"xs{s}", f"rstd{b}", "gbc"], writes=[f"ub{s}"])

        def p0_tr(b):
            rows = blk_rows(b)
            s = b % 2
            pi = 4 + b % 2
            for c in range(8):
                P.op("pe", lambda e, c=c: e.transpose(
                    out=psb[pi][:, c * 128:c * 128 + rows], in_=ub[s][:rows, c * 128:(c + 1) * 128],
                    identity=ident_bf[:rows, :rows]),
                    reads=[f"ub{s}", "cbf"], writes=[f"ps{pi}"])

        def p0_back(b):
            rows = blk_rows(b)
            pi = 4 + b % 2
            srcv = psb[pi][:, :].rearrange("p (c t) -> p c t", t=128)[:, :, :rows]
            dstv = uT[:, :, b * 128:b * 128 + rows]
            if b % 2 == 0:
                P.op("act", lambda e: e.activation(out=dstv, in_=srcv, func=AF.Copy),
                     reads=[f"ps{pi}"], writes=[f"uT{b}"])
            else:
                P.op("dve", lambda e: e.tensor_copy(out=dstv, in_=srcv),
                     reads=[f"ps{pi}"], writes=[f"uT{b}"])


        TR = [(0, 512), (512, 512), (1024, 512), (1536, 512), (2048, 16)]
        pcount = [0]
        pspool = [[0, 1, 2, 3, 6, 7]]

        def nextps(pool):
            i = pool[pcount[0] % len(pool)]
            pcount[0] += 1
            return i

        def feat_unit(slot, evac, cc, t0, n):
            pi = nextps(pspool[0])
            ub_ = [f"uT{b}" for b in range(t0 // 128, (t0 + n + 127) // 128)]
            for dc in range(8):
                P.op("pe", lambda e, dc=dc: e.matmul(
                    out=ps[pi][:, :n], lhsT=wbuf[slot][:, dc, cc * 128:(cc + 1) * 128],
                    rhs=uT[:, dc, t0:t0 + n], start=(dc == 0), stop=(dc == 7)),
                    reads=[f"w{slot}"] + ub_, writes=[f"ps{pi}"])
            evac(cc, t0, n, pi)

        def tok_unit(slot, evac, b):
            rows = blk_rows(b)
            pi = nextps(pspool[0])
            for dc in range(8):
                P.op("pe", lambda e, dc=dc: e.matmul(
                    out=ps[pi][:rows, :512], lhsT=uT[:, dc, b * 128:b * 128 + rows],
                    rhs=wbuf[slot][:, dc, :], start=(dc == 0), stop=(dc == 7)),
                    reads=[f"w{slot}", f"uT{b}"], writes=[f"ps{pi}"])
            evac(b, rows, pi)

        def proj_feat(slot, evac, tr=None):
            for cc in range(4):
                for (t0, n) in (tr or TR):
                    feat_unit(slot, evac, cc, t0, n)

        def proj_tok(slot, evac, nblk):
            for b in range(nblk):
                tok_unit(slot, evac, b)

        flip = [0]

        def evac_copy_to(dst3):
            def f(cc, t0, n, pi):
                flip[0] ^= 1
                if flip[0]:
                    P.op("act", lambda e: e.activation(out=dst3[:, cc, t0:t0 + n], in_=ps[pi][:, :n], func=AF.Copy),
                         reads=[f"ps{pi}"], writes=[])
                else:
                    P.op("dve", lambda e: e.tensor_copy(out=dst3[:, cc, t0:t0 + n], in_=ps[pi][:, :n]),
                         reads=[f"ps{pi}"], writes=[])
            return f

        QP = [A1[:, h, 0:SEQ] for h in range(4)] + [mixT[:, 4 + h, :] for h in range(4)]
        for h in range(8):
            zr = slice(64, 128) if h % 2 == 0 else slice(0, 64)
            P.op("pool", lambda e, h=h, zr=zr: e.memset(QP[h][zr, :], 0.0), writes=[f"qpz{h}"])

        def evac_q(cc, t0, n, pi):
            if t0 >= SEQ:
                return
            P.op("act", lambda e: e.activation(out=QP[2 * cc][0:64, t0:t0 + n], in_=ps[pi][0:64, :n], func=AF.Copy),
                 reads=[f"ps{pi}"], writes=[])
            P.op("dve", lambda e: e.tensor_copy(out=QP[2 * cc + 1][64:128, t0:t0 + n], in_=ps[pi][64:128, :n]),
                 reads=[f"ps{pi}"], writes=[])
        def evac_v(b, rows, pi):
            P.op("act", lambda e: e.activation(out=A3[:rows, b, 0:512], in_=ps[pi][:rows, :512], func=AF.Copy),
                 reads=[f"ps{pi}"], writes=[f"v{b}"])

        def evac_g(b, rows, pi):
            P.op("act", lambda e: e.activation(out=A4[:rows, b, :], in_=ps[pi][:rows, :512], func=AF.Silu),
                 reads=[f"ps{pi}"], writes=[])

        load_w(2, 2)
        load_w(3, 3)
        evac_k = evac_copy_to(A2)
        fifo = []
        p0_front(0)
        p0_tr(0)
        for b in range(1, NB + 1):
            if b < NB:
                p0_front(b)
            if b >= 1:
                p0_back(b - 1)
                bb = b - 1
                if bb % 4 == 3 or bb == 16:
                    r = bb // 4
                    t0, n = TR[r]
                    for cc in range(4):
                        if r < 4:
                            fifo.append(lambda cc=cc, t0=t0, n=n: feat_unit(0, evac_q, cc, t0, n))
                        fifo.append(lambda cc=cc, t0=t0, n=n: feat_unit(1, evac_k, cc, t0, n))
                    for b2 in range(4 * r, min(4 * r + 4, NB)):
                        fifo.append(lambda b2=b2: tok_unit(2, evac_v, b2))
                        if b2 < 16:
                            fifo.append(lambda b2=b2: tok_unit(3, evac_g, b2))
            for _ in range(4):
                if fifo:
                    fifo.pop(0)()
            if b < NB:
                p0_tr(b)
        while fifo:
            fifo.pop(0)()
        P.barrier()
        P.op("pool", lambda e: e.memset(A5[:, 16, :], 0.0), writes=["dvmeta"])
        for b in range(NB):
            rows = blk_rows(b)
            pi = nextps([0, 1, 2, 3])
            last = (b == NB - 1)
            P.op("pe", lambda e, b=b, rows=rows, pi=pi, last=last: e.matmul(
                out=ps[pi][:rows, :512], lhsT=Dmat[:rows, :rows], rhs=A3[:rows, b, 0:512], start=True, stop=last),
                reads=[f"v{b}", "cbf"], writes=[f"ps{pi}"])
            if not last:
                P.op("pe", lambda e, b=b, pi=pi: e.matmul(
                    out=ps[pi][:128, :512], lhsT=dlast, rhs=A3[0:1, b + 1, 0:512], start=False, stop=True),
                    reads=[f"v{b + 1}"], writes=[f"ps{pi}"])
            P.op("dve", lambda e, b=b, rows=rows, pi=pi: e.tensor_copy(out=A5[:rows, b, :], in_=ps[pi][:rows, :512]),
                 reads=[f"ps{pi}"] + (["dvmeta"] if b == 16 else []), writes=[f"dv{b}"] + (["dvmeta"] if b == 16 else []))

        keepL = [scrf[:, i * 2064:(i + 1) * 2064] for i in range(2)]
        Pb = [scr[:, 8256 + i * 2176: 8256 + (i + 1) * 2176] for i in range(3)]
        tmpb = tmpf.bitcast(BF16)
        if DMAT_MOD:
            PTDs = [tmpb[:, 512 + i * 1152: 512 + (i + 1) * 1152].rearrange("p (b t) -> p b t", t=128) for i in range(2)]
            PTb = [scr[:, 14784 + i * 512: 14784 + (i + 1) * 512] for i in range(3)] + \
                  [tmpb[:, 2816 + i * 512: 2816 + (i + 1) * 512] for i in range(2)]
        else:
            PTb = [scr[:, 14784 + i * 512: 14784 + (i + 1) * 512] for i in range(3)] + \
                  [tmpb[:, 512 + i * 512: 512 + (i + 1) * 512] for i in range(7)]
        NPT = len(PTb)

        def is_dma_chunk(k, c):
            return bool(DMAT_MOD) and c >= 2
        for i in range(3):
            P.op("pool", lambda e, i=i: e.memset(Pb[i][:, :], 0.0), writes=[f"Pb{i}"])
        mixtok = [tmpf.bitcast(BF16)[:, 0:512]]
        cnt = {"z": 0, "k": 0, "p": 0, "t": 0, "pt": 0, "ev": 0}

        def rot(name, n):
            i = cnt[name] % n
            cnt[name] += 1
            return i

        def sb_chunks(j):
            blocks = list(range(j, 16))
            chunks = []
            while blocks:
                cb = blocks[:4]
                blocks = blocks[4:]
                chunks.append([(kb, 128) for kb in cb])
            if len(chunks[-1]) < 4:
                chunks[-1].append((16, 16))
            else:
                chunks.append([(16, 16)])
            return chunks

        items = [(j, h) for j in range(16) for h in range(8)]
        NI = len(items)
        ptslot = {}

        def st_z(k, c):
            j, h = items[k]
            ch = sb_chunks(j)[c]
            cc, po = h // 2, (h % 2) * 64
            n = sum(r for _, r in ch)
            t0 = ch[0][0] * 128
            off = t0 - 128 * j
            zi = rot("z", 2)
            ks = k % 2
            P.op("pe", lambda e: e.matmul(
                out=ps[zi][:, :n], lhsT=QP[h][:, j * 128:(j + 1) * 128],
                rhs=A2[:, cc, t0:t0 + n], start=True, stop=True),
                reads=[], writes=[f"ps{zi}"])
            P.op("act", lambda e: e.activation(
                out=keepL[ks][:, off:off + n], in_=ps[zi][:, :n], func=AF.Sigmoid, scale=-0.125),
                reads=[f"ps{zi}"], writes=[f"keep{ks}"])

        def st_scan(k):
            j, h = items[k]
            ntot = NTOK - 128 * j
            ks, pslot = k % 2, k % 3
            P.op("dve", lambda e: e.tensor_tensor_scan(
                out=Pb[pslot][:, :ntot], data0=keepL[ks][:, :ntot], data1=M1Z[:, :ntot], initial=1.0,
                op0=ALU.mult, op1=ALU.max),
                reads=[f"keep{ks}", "cf"], writes=[f"Pb{pslot}"])

        def st_t(k, c):
            j, h = items[k]
            ch = sb_chunks(j)[c]
            pslot = k % 3
            if is_dma_chunk(k, c):
                if c == 2:
                    nfar = 17 - j - 8
                    P.dma("sp", "dmat", PTDs[k % 2][:, 0:nfar, :], Pb[pslot][:, 1024:1024 + nfar * 128],
                          reads=[f"Pb{pslot}"], writes=[f"PTD{k % 2}", "dmat_serial"], transpose=True)
                return
            ti = 2 + rot("t", 2)
            off = ch[0][0] * 128 - 128 * j
            col = off
            for bi, (kb, r) in enumerate(ch):
                P.op("pe", lambda e, bi=bi, col=col: e.transpose(
                    out=psb[ti][:, bi * 128:(bi + 1) * 128], in_=Pb[pslot][:, col:col + 128],
                    identity=ident_bf),
                    reads=[f"Pb{pslot}", "cbf"], writes=[f"ps{ti}"])
                col += r
            pti = rot("pt", NPT)
            ptslot[(k, c)] = pti
            nb_ = len(ch)
            if ((not DMAT_MOD) and rot("ev", 4) == 3) or (DMAT_MOD and j >= 11 and c == 1):
                P.op("dve", lambda e: e.tensor_copy(out=PTb[pti][:, :nb_ * 128], in_=psb[ti][:, :nb_ * 128]),
                     reads=[f"ps{ti}"], writes=[f"PT{pti}"])
            else:
                P.op("act", lambda e: e.activation(
                    out=PTb[pti][:, :nb_ * 128], in_=psb[ti][:, :nb_ * 128], func=AF.Copy),
                    reads=[f"ps{ti}"], writes=[f"PT{pti}"])

        def st_av(k, c):
            j, h = items[k]
            ch = sb_chunks(j)[c]
            oi = 6 + (j % 2)
            hc = slice(h * 64, (h + 1) * 64)
            if is_dma_chunk(k, c):
                for bi, (kb, r) in enumerate(ch):
                    P.op("pe", lambda e, bi=bi, kb=kb: e.matmul(
                        out=ps[oi][:, hc], lhsT=PTDs[k % 2][:, kb - j - 8, :],
                        rhs=A5[:, kb, hc], start=(c == 0 and bi == 0), stop=False),
                        reads=[f"PTD{k % 2}", f"dv{kb}"], writes=[f"ps{oi}"])
                return
            pti = ptslot[(k, c)]
            for bi, (kb, r) in enumerate(ch):
                P.op("pe", lambda e, bi=bi, kb=kb: e.matmul(
                    out=ps[oi][:, hc], lhsT=PTb[pti][:, bi * 128:(bi + 1) * 128],
                    rhs=A5[:, kb, hc], start=(c == 0 and bi == 0), stop=False),
                    reads=[f"PT{pti}", f"dv{kb}"], writes=[f"ps{oi}"])

        def st_tail(k):
            j, h = items[k]
            oi = 6 + (j % 2)
            hc = slice(h * 64, (h + 1) * 64)
            P.op("pe", lambda e: e.matmul(
                out=ps[oi][:, hc], lhsT=negM2T, rhs=A5[:, j, hc], start=False, stop=False),
                reads=["cbf", f"dv{j}"], writes=[f"ps{oi}"])
            P.op("pe", lambda e: e.matmul(
                out=ps[oi][:, hc], lhsT=ident_bf, rhs=A3[:, j, hc], start=False, stop=True),
                reads=["cbf"], writes=[f"ps{oi}"])
            if h == 7:
                P.op("dve", lambda e: e.tensor_tensor(
                    out=mixtok[0][:, :], in0=ps[oi][:, :512], in1=A4[:, j, :], op=ALU.mult),
                    reads=[f"ps{oi}"], writes=["mixtok"])
                ti = 4 + (j % 2)

                def sb_mix(j=j, ti=ti):
                    for c in range(4):
                        P.op("pe", lambda e, c=c: e.transpose(
                            out=psb[ti][:, c * 128:(c + 1) * 128], in_=mixtok[0][:, c * 128:(c + 1) * 128], identity=ident_bf),
                            reads=["mixtok", "cbf"], writes=[f"ps{ti}"])
                    P.op("act", lambda e: e.activation(
                        out=mixT[:, 0:4, j * 128:(j + 1) * 128],
                        in_=psb[ti][:, 0:512].rearrange("p (c t) -> p c t", t=128), func=AF.Copy),
                        reads=[f"ps{ti}"], writes=[])
                sb_pending.append(sb_mix)
            elif sb_pending:
                sb_pending.pop(0)()

        sb_pending = []

        def nch(k):
            return len(sb_chunks(items[k][0])) if 0 <= k < NI else 0

        SK = 2
        for k in range(NI + SK + 1):
            if 0 <= k - 1 < NI:
                st_scan(k - 1)
            na, nt_, nv_ = nch(k), nch(k - SK), nch(k - SK - 1)
            for c in range(na):
                st_z(k, c)
            for c in range(max(nt_, nv_)):
                if c < nt_:
                    st_t(k - SK, c)
                if c < nv_:
                    st_av(k - SK - 1, c)
            if 0 <= k - SK - 1 < NI:
                st_tail(k - SK - 1)
            if k == NI:
                for slot, g in ((2, 6), (3, 7)):
                    src = wext[:, g * 512:(g + 1) * 512].rearrange("(c p) n -> p c n", p=128)
                    P.dma("pool", f"w{slot}", wbuf[slot], src, writes=[f"w{slot}", "keep0", "keep1", "dmat_serial"])
        while sb_pending:
            sb_pending.pop(0)()
        P.barrier()
        load_w(0, 4)
        load_w(1, 5)

        pspool[0] = [0, 1, 2, 3, 4, 5, 6, 7]
        ropec = A5f[:, 0:NTOK]
        ropes = A5f[:, NTOK:2 * NTOK]
        P.dma("sp", "c_rc", ropec, ropec_d, writes=["ropec"])
        P.dma("sp", "c_rs", ropes, ropes_d, writes=["ropes"])
        vaug4 = A3[:, :, :].rearrange("p b (h e) -> p b h e", e=130)
        P.op("pool", lambda e: e.memset(vaug4[:, :, :, 128:129], 1.0), writes=["vones"])
        P.op("pool", lambda e: e.memset(vaug4[:, :, :, 129:130], 0.0), writes=["vzero"])

        QPD = [A1[:, h, 0:SEQ] for h in range(4)] + [mixT[:, 4 + h, :] for h in range(4)]
        for h in range(8):
            zr = slice(64, 128) if h % 2 == 0 else slice(0, 64)
            P.op("pool", lambda e, h=h, zr=zr: e.memset(QPD[h][zr, :], 0.0), writes=[f"qpdz{h}"])

        qhb = [scr[:, i * 512:(i + 1) * 512] for i in range(4)]
        qlb = [scr[:, 2048 + i * 512: 2048 + (i + 1) * 512] for i in range(4)]

        def proj_rope(dst3, slot, padded=False):
            units = [(cc, t0, n) for cc in range(4) for (t0, n) in (TR[:4] if padded else TR)]
            pend = None

            def finish(u):
                cc, t0, n, pa, qi = u
                pb = nextps([0, 1, 2, 3, 4, 5, 6, 7])
                P.op("pe", lambda e: e.matmul(out=ps[pb][:, :n], lhsT=permT_bf, rhs=qhb[qi][:, :n], start=True, stop=False),
                     reads=[f"qs{qi}", "cbf"], writes=[f"ps{pb}"])
                P.op("pe", lambda e: e.matmul(out=ps[pb][:, :n], lhsT=permT_bf, rhs=qlb[qi][:, :n], start=False, stop=True),
                     reads=[f"ql{qi}", "cbf"], writes=[f"ps{pb}"])
                ta = rot("k", 2)
                t1 = tmpf[:, ta * 1024: ta * 1024 + n]
                t2 = tmpf[:, ta * 1024 + 512: ta * 1024 + 512 + n]
                P.op("dve", lambda e: e.tensor_tensor(out=t1, in0=ps[pa][:, :n], in1=ropec[:, t0:t0 + n], op=ALU.mult),
                     reads=[f"ps{pa}", "ropec"], writes=[f"t1{ta}"])
                P.op("dve", lambda e: e.tensor_tensor(out=t2, in0=ps[pb][:, :n], in1=ropes[:, t0:t0 + n], op=ALU.mult),
                     reads=[f"ps{pb}", "ropes"], writes=[f"t2{ta}"])
                if padded:
                    P.op("pool", lambda e: e.tensor_tensor(
                        out=QPD[2 * cc][0:64, t0:t0 + n], in0=t1[0:64, :], in1=t2[0:64, :], op=ALU.add),
                        reads=[f"t1{ta}", f"t2{ta}", f"qpdz{2 * cc}"], writes=[f"qpa{ta}"])
                    P.op("pool", lambda e: e.tensor_tensor(
                        out=QPD[2 * cc + 1][64:128, t0:t0 + n], in0=t1[64:128, :], in1=t2[64:128, :], op=ALU.add),
                        reads=[f"t1{ta}", f"t2{ta}", f"qpdz{2 * cc + 1}"], writes=[f"qpb{ta}"])
                else:
                    P.op("pool", lambda e: e.tensor_tensor(
                        out=dst3[:, cc, t0:t0 + n], in0=t1, in1=t2, op=ALU.add),
                        reads=[f"t1{ta}", f"t2{ta}"], writes=[f"qpa{ta}"])

            for (cc, t0, n) in units:
                pa = nextps([0, 1, 2, 3, 4, 5, 6, 7])
                for dc in range(8):
                    P.op("pe", lambda e, dc=dc, pa=pa, cc=cc, t0=t0, n=n: e.matmul(
                        out=ps[pa][:, :n], lhsT=wbuf[slot][:, dc, cc * 128:(cc + 1) * 128],
                        rhs=uT[:, dc, t0:t0 + n], start=(dc == 0), stop=(dc == 7)),
                        reads=[f"w{slot}"], writes=[f"ps{pa}"])
                qi = rot("p", 4)
                P.op("act", lambda e, pa=pa, qi=qi, n=n: e.activation(out=qhb[qi][:, :n], in_=ps[pa][:, :n], func=AF.Copy),
                     reads=[f"ps{pa}"], writes=[f"qs{qi}"])
                P.op("dve", lambda e, pa=pa, qi=qi, n=n: e.tensor_tensor(
                    out=qlb[qi][:, :n], in0=ps[pa][:, :n], in1=qhb[qi][:, :n], op=ALU.subtract),
                    reads=[f"ps{pa}", f"qs{qi}"], writes=[f"ql{qi}"])
                if pend is not None:
                    finish(pend)
                pend = (cc, t0, n, pa, qi)
            finish(pend)

        def evac_vd(b, rows, pi):
            P.op("act", lambda e: e.activation(
                out=vaug4[:rows, b, :, 0:128], in_=ps[pi][:rows, :512].rearrange("p (h e) -> p h e", e=128), func=AF.Copy),
                reads=[f"ps{pi}"], writes=[])
        if DFP_LEVEL >= 1:
            proj_tok(2, evac_vd, NB)
        if DFP_LEVEL >= 2:
            proj_tok(3, evac_g, 16)
        if DFP_LEVEL >= 3:
            proj_rope(None, 0, padded=True)
        if DFP_LEVEL >= 4:
            proj_rope(A2, 1)
        P.barrier()

        wo = scr[:, 8192:16384].rearrange("p (c n) -> p c n", n=1024)
        P.dma("pool", "w0", wo, wout.rearrange("(c p) n -> p c n", p=128), writes=["wo"])
        gfin = A5f[:, 0:1024]
        P.dma("sp", "c_rc", gfin, gfin_d, writes=["gfin"])
        ETb = [scr[:, i * 512:(i + 1) * 512] for i in range(8)]
        of32 = [tmpf[:, i * 128:(i + 1) * 128] for i in range(2)]
        yf32 = [tmpf[:, 256 + i * 128: 256 + (i + 1) * 128] for i in range(2)]
        sqj2 = tmpf[:, 0:128]
        mixtok2s = [scr[:, 4096:4608], scr[:, 4608:5120], scr[:, 5120:5632]]
        sc = {"i": 0}
        dsteps = []
        for j in range(16):
            for hp in range(2):
                st_l = [[kb] for kb in range(j, NB)]
                for si, kbs in enumerate(st_l):
                    dsteps.append({"j": j, "hp": hp, "kbs": kbs, "last": si == len(st_l) - 1,
                                   "par": ((2 * j + hp) % 2) if DF_ODB else 1})

        def d_zexp(stp):
            j, hp, kbs = stp["j"], stp["hp"], stp["kbs"]
            rows = blk_rows(kbs[0])
            zb = [rot("z", DF_ZP) for _ in kbs]
            eb = [rot("k", 8) for _ in kbs]
            stp["eb"] = eb
            for bi, kb in enumerate(kbs):
                for p2 in range(2):
                    hc0 = 4 * hp + 2 * p2
                    cc = hc0 // 2
                    if hc0 < 4:
                        rhs2 = A1[:, hc0:hc0 + 2, j * 128:(j + 1) * 128]
                    else:
                        rhs2 = mixT[:, hc0:hc0 + 2, j * 128:(j + 1) * 128]
                    P.op("pe", lambda e, zi=zb[bi], p2=p2, cc=cc, kb=kb, rhs2=rhs2: e.matmul(
                        out=ps[zi][:rows, p2 * 256:(p2 + 1) * 256], lhsT=A2[:, cc, kb * 128:kb * 128 + rows],
                        rhs=rhs2, start=True, stop=True),
                        reads=[f"qpd{j}"], writes=[f"ps{zb[bi]}"])
                zi, ei = zb[bi], eb[bi]
                P.op("act", lambda e, zi=zi, ei=ei: e.activation(
                    out=ETb[ei][:rows, :], in_=ps[zi][:rows, :], func=AF.Exp, scale=0.125),
                    reads=[f"ps{zi}"], writes=[f"ET{ei}"])
                if kb == j:
                    rect = ETb[ei][0:64, :].rearrange("p (i t) -> p i t", t=128)[:, :, 64:128]
                    P.op("pool", lambda e, rect=rect: e.memset(rect, 0.0),
                         reads=[], writes=[f"ET{ei}"])

        def d_av(stp):
            j, hp, kbs, eb = stp["j"], stp["hp"], stp["kbs"], stp["eb"]
            rows = blk_rows(kbs[0])
            for bi, kb in enumerate(kbs):
                for i in range(4):
                    head = (4 * hp + i) // 2
                    ei = eb[bi]
                    col = i * 128
                    ob, oc = 4 + 2 * stp["par"] + i // 2, (i % 2) * 256
                    P.op("pe", lambda e, i=i, ei=ei, col=col, kb=kb, head=head, ob=ob, oc=oc: e.matmul(
                        out=ps[ob][:, oc:oc + 129], lhsT=ETb[ei][:rows, col:col + 128],
                        rhs=A3[:rows, kb, head * 130:head * 130 + 129],
                        start=(kb == j and i % 2 == 0), stop=(kb == NB - 1), skip_group_check=True),
                        reads=[f"ET{ei}", "vones"], writes=[f"ps{ob}"])
            if stp["last"]:
                d_epilogue(j, hp, stp["par"])

        def d_epilogue(j, hp, par):
            k = sc["i"] % 4
            sc["i"] += 1
            q = k % 2
            cb = 80 + k * 16
            banks = [4 + 2 * par, 5 + 2 * par]
            tb = 512 + q * 768
            t1b = [tmpf[:, tb + hl * 128: tb + (hl + 1) * 128] for hl in range(2)]
            ofb = [tmpf[:, tb + 256 + hl * 128: tb + 256 + (hl + 1) * 128] for hl in range(2)]
            yfb = [tmpf[:, tb + 512 + hl * 128: tb + 512 + (hl + 1) * 128] for hl in range(2)]
            for hl in range(2):
                pb_ = banks[hl]
                P.op("dve", lambda e, pb_=pb_, hl=hl: e.reciprocal(
                    out=small[:, cb + 2 * hl:cb + 2 * hl + 2], in_=ps[pb_][:, 128:512:256]),
                    reads=[f"ps{pb_}"], writes=[f"rz{k}_{hl}"])
            P.op("dve", lambda e: e.tensor_scalar(
                out=small[:, cb + 4:cb + 6], in0=small[:, cb + 1:cb + 4:2], scalar1=neglam, scalar2=None, op0=ALU.mult),
                reads=[f"rz{k}_0", f"rz{k}_1", "neglam"], writes=[f"c1{k}"])
            act_t1 = j >= EPI_ACT_FROM_TILE
            for hl in range(2):
                pb_ = banks[hl]
                if act_t1:
                    P.op("act", lambda e, pb_=pb_, hl=hl: e.activation(
                        out=t1b[hl], in_=ps[pb_][:, 256:384], func=AF.Copy, scale=small[:, cb + 4 + hl:cb + 5 + hl]),
                        reads=[f"ps{pb_}", f"c1{k}"], writes=[f"t1e{q}_{hl}"])
                else:
                    P.op("dve", lambda e, pb_=pb_, hl=hl: e.tensor_scalar(
                        out=t1b[hl], in0=ps[pb_][:, 256:384], scalar1=small[:, cb + 4 + hl:cb + 5 + hl], scalar2=None, op0=ALU.mult),
                        reads=[f"ps{pb_}", f"c1{k}"], writes=[f"t1e{q}_{hl}"])
            for hl in range(2):
                pb_ = banks[hl]
                P.op("dve", lambda e, pb_=pb_, hl=hl: e.scalar_tensor_tensor(
                    out=ofb[hl], in0=ps[pb_][:, 0:128], scalar=small[:, cb + 2 * hl:cb + 2 * hl + 1], in1=t1b[hl],
                    op0=ALU.mult, op1=ALU.add),
                    reads=[f"ps{pb_}", f"rz{k}_{hl}", f"t1e{q}_{hl}"], writes=[f"of{q}_{hl}"])
            for hl in range(2):
                P.op("dve", lambda e, hl=hl: e.scalar_tensor_tensor(
                    out=sqj2, in0=ofb[hl], scalar=1.0, in1=ofb[hl], op0=ALU.mult, op1=ALU.mult,
                    accum_out=small[:, cb + 6 + hl:cb + 7 + hl]),
                    reads=[f"of{q}_{hl}"], writes=["sqj2", f"ss2{k}_{hl}"])
            P.op("dve", lambda e: e.tensor_scalar(
                out=small[:, cb + 8:cb + 10], in0=small[:, cb + 6:cb + 8], scalar1=1.0 / 128, scalar2=EPS,
                op0=ALU.mult, op1=ALU.add),
                reads=[f"ss2{k}_0", f"ss2{k}_1"], writes=[f"sq2{k}"])
            P.op("pool", lambda e: e.tensor_tensor(
                out=small[:, cb + 10:cb + 12], in0=small[:, cb + 8:cb + 10], in1=neghalf2, op=ALU.pow),
                reads=[f"sq2{k}", "neghalf"], writes=[f"rs2{k}"])
            def part_b():
                for hl in range(2):
                    head = 2 * hp + hl
                    P.op("dve", lambda e, hl=hl: e.scalar_tensor_tensor(
                        out=yfb[hl], in0=ofb[hl], scalar=small[:, cb + 10 + hl:cb + 11 + hl], in1=gsub[:, :],
                        op0=ALU.mult, op1=ALU.mult),
                        reads=[f"of{q}_{hl}", f"rs2{k}", "gsub"], writes=[f"yf{q}_{hl}"])
                    P.op("pool", lambda e, hl=hl, head=head: e.tensor_tensor(
                        out=mixtok2s[j % 3][:, head * 128:(head + 1) * 128], in0=yfb[hl], in1=A4[:, j, head * 128:(head + 1) * 128], op=ALU.mult),
                        reads=[f"yf{q}_{hl}"], writes=[f"mt2_{j % 3}_{head}"])
                if hp == 1:
                    pending_mix.append([j, 13])
            while pend_b:
                pend_b.pop(0)()
            pend_b.append(part_b)

        pend_b = []

        def d_mix(j):
            ti = rot("z", DF_ZP)
            mt = mixtok2s[j % 3]
            for c in range(4):
                P.op("pe", lambda e, ti=ti, c=c: e.transpose(
                    out=psb[ti][:, c * 128:(c + 1) * 128], in_=mt[:, c * 128:(c + 1) * 128], identity=ident_bf),
                    reads=[f"mt2_{j % 3}_{c}", "cbf"], writes=[f"ps{ti}"])
            P.op("dve", lambda e, ti=ti, j=j: e.tensor_copy(
                out=mixT[:, 4:8, j * 128:(j + 1) * 128],
                in_=psb[ti][:, 0:512].rearrange("p (c t) -> p c t", t=128)),
                reads=[f"ps{ti}"], writes=[f"qpd{j}"])

        pending_mix = []
        DSK = DF_DSK
        for idx in range(len(dsteps) + DSK):
            if idx < len(dsteps):
                d_zexp(dsteps[idx])
            for pm in list(pending_mix):
                pm[1] -= 1
                if pm[1] <= 0:
                    pending_mix.remove(pm)
                    d_mix(pm[0])
            if idx >= DSK:
                d_av(dsteps[idx - DSK])
        while pend_b:
            pend_b.pop(0)()
        for pm in pending_mix:
            d_mix(pm[0])
        P.barrier()

        xs2 = [scrf[:, s * 1024:(s + 1) * 1024] for s in range(4)]
        yo = [tmpf[:, 0:1024], tmpf[:, 1024:2048]]
        sqj3 = A4[:, 0:2, :].rearrange("p a n -> p (a n)")

        def p5_front(j):
            s = j % 4
            P.dma("sp", f"xs{s}", xs2[s], xr[j * 128:(j + 1) * 128, :], writes=[f"xs2{s}"])
            pa, pb = 2 * (j % 4), 2 * (j % 4) + 1
            for (pi, half) in ((pa, 0), (pb, 1)):
                for c in range(8):
                    P.op("pe", lambda e, pi=pi, half=half, c=c: e.matmul(
                        out=ps[pi][:, :512], lhsT=mixT[:, c, j * 128:(j + 1) * 128],
                        rhs=wo[:, c, half * 512:(half + 1) * 512], start=(c == 0), stop=(c == 7)),
                        reads=["wo"], writes=[f"ps{pi}"])
            for (pi, half) in ((pa, 0), (pb, 1)):
                P.op("dve", lambda e, pi=pi, half=half: e.tensor_tensor(
                    out=xs2[s][:, half * 512:(half + 1) * 512], in0=ps[pi][:, :512],
                    in1=xs2[s][:, half * 512:(half + 1) * 512], op=ALU.add),
                    reads=[f"ps{pi}", f"xs2{s}"], writes=[f"xs2{s}"])
            cb = 200 + (j % 4) * 4
            P.op("act", lambda e: e.activation(
                out=sqj3, in_=xs2[s], func=AF.Square, accum_out=small[:, cb:cb + 1]),
                reads=[f"xs2{s}"], writes=["sqj3", f"ss3{j % 4}"])

        def p5_back(j):
            s = j % 4
            so = j % 2
            cb = 200 + (j % 4) * 4
            P.op("dve", lambda e: e.tensor_scalar(
                out=small[:, cb + 1:cb + 2], in0=small[:, cb:cb + 1], scalar1=1.0 / D, scalar2=EPS,
                op0=ALU.mult, op1=ALU.add),
                reads=[f"ss3{j % 4}"], writes=[f"sq3{j % 4}"])
            P.op("pool", lambda e: e.tensor_tensor(
                out=small[:, cb + 2:cb + 3], in0=small[:, cb + 1:cb + 2], in1=neghalf, op=ALU.pow),
                reads=[f"sq3{j % 4}", "neghalf"], writes=[f"rs3{j % 4}"])
            P.op("dve", lambda e: e.scalar_tensor_tensor(
                out=yo[so], in0=xs2[s], scalar=small[:, cb + 2:cb + 3], in1=gfin, op0=ALU.mult, op1=ALU.mult),
                reads=[f"xs2{s}", f"rs3{j % 4}", "gfin"], writes=[f"yo{so}"])
            P.dma("pool", f"o{so}", out_d[j * 128:(j + 1) * 128, :], yo[so], reads=[f"yo{so}"], writes=[f"out{j}"])

        for j in range(17):
            if j < 16:
                p5_front(j)
            if j >= 1:
                p5_back(j - 1)
        P.barrier()

        with nc.Block() as block:
            @block.tensor
            def _(e):
                for f in P.streams["pe"]:
                    f(e)

            @block.scalar
            def _(e):
                for f in P.streams["act"]:
                    f(e)

            @block.vector
            def _(e):
                for f in P.streams["dve"]:
                    f(e)

            @block.gpsimd
            def _(e):
                for f in P.streams["pool"]:
                    f(e)

            @block.sync
            def _(e):
                for f in P.streams["sp"]:
                    f(e)
    return nc


def _consts():
    bf = ml_dtypes.bfloat16
    k = np.arange(128)[:, None]
    m = np.arange(128)[None, :]
    ident = (k == m).astype(np.float32)
    negM2T = -(k < m).astype(np.float32)
    Dm = (k == m + 1).astype(np.float32) - (k == m).astype(np.float32)
    dlast = np.zeros((128, 128), np.float32)
    dlast[0, 127] = 1.0
    swp = (np.arange(128) // 64) * 64 + ((np.arange(128) % 64) + 32) % 64
    permT = np.zeros((128, 128), np.float32)
    permT[swp, np.arange(128)] = 1.0
    cbf = np.concatenate([ident, negM2T, Dm, dlast, permT], axis=1).astype(bf)
    M1 = (m <= k).astype(np.float32)
    cf = np.concatenate([permT, M1, np.zeros((128, 1936), np.float32)], axis=1).astype(np.float32)
    inv = (1.0 / (np.float32(10000.0) ** (np.arange(0, 64, 2, dtype=np.float32) / np.float32(64)))).astype(np.float32)
    pos = (NTOK - 1 - np.arange(NTOK)).astype(np.float32)
    ang = (pos[None, :] * inv[:, None]).astype(np.float32)
    cos = np.cos(ang).astype(np.float32)
    sin = np.sin(ang).astype(np.float32)
    p = np.arange(128)
    ropec = cos[p % 32, :]
    sign = np.where((p % 64) < 32, -1.0, 1.0).astype(np.float32)[:, None]
    ropes = (sin[p % 32, :] * sign).astype(np.float32)
    return cbf, cf, np.ascontiguousarray(ropec), np.ascontiguousarray(ropes)


_NC_CACHE = {}


def kernel(x, meta_tokens, norm_gain, w_in, w_out, lambda_q1, lambda_k1, lambda_q2, lambda_k2,
           subln_gain, final_norm_gain):
    x = np.asarray(x, np.float32)
    B = x.shape[0]
    w = np.asarray(w_in, np.float32)[0]
    wext = np.ascontiguousarray(w)
    wout = np.ascontiguousarray(np.asarray(w_out, np.float32)[0])
    rep = lambda v, n: np.ascontiguousarray(np.broadcast_to(np.asarray(v, np.float32).reshape(1, n), (128, n)))
    gbc = rep(norm_gain[0], D)
    gfin = rep(final_norm_gain, D)
    gsub = rep(subln_gain[0], 128)
    lamv = np.ascontiguousarray(np.concatenate(
        [rep(lambda_q1[0], 64), rep(lambda_k1[0], 64), rep(lambda_q2[0], 64), rep(lambda_k2[0], 64)], axis=1))
    cbf, cf, ropec, ropes = _consts()
    metar = np.ascontiguousarray(np.asarray(meta_tokens, np.float32)[::-1])
    if "nc" not in _NC_CACHE:
        _NC_CACHE["nc"] = build_nc(DEBUG)
    nc = _NC_CACHE["nc"]
    in_maps = []
    for b in range(B):
        in_maps.append({
            "xr": np.ascontiguousarray(x[b, ::-1, :]), "metar": metar, "wext": wext, "wout": wout,
            "gbc": gbc, "gfin": gfin, "gsub": gsub, "lamv": lamv, "ropec": ropec, "ropes": ropes,
            "cbf": cbf, "cf": cf,
        })
    res = run_bass_kernel_spmd(nc, in_maps, core_ids=list(range(B)))
    outs = [np.asarray(r["out"], np.float32)[::-1] for r in res.results]
    return np.ascontiguousarray(np.stack(outs, axis=0))
```
